# Optimizing a Trainium2 kernel written in Bass

```python
import math
import jax, jax.numpy as jnp
from jax import lax
import numpy as np

D_MODEL = 1024
BATCH = 8
SEQ = 4096
DEPTH = 4

CTX_LEN = 256
GRID_W = 64
N_MIXERS = 4
N_SUB = 3
N_MOD = 3 * N_SUB
D_FF = 2816
RMS_EPS = 1e-6
NEG_INF = -1e30
ROPE_BASE = 10000.0

S5_GROUP = 16
S5_GROUPS = D_MODEL // S5_GROUP
S5_STATE = 64
S5_DT_MIN = 1e-3
S5_DT_MAX = 1e-1

DIFF_HEAD_DIM = 64
DIFF_HEADS = D_MODEL // (2 * DIFF_HEAD_DIM)
Q_BLOCK = 128

NA_HEADS = 16
NA_HEAD_DIM = D_MODEL // NA_HEADS
WIN_H = 8
WIN_W = 16

HG_EXPAND = 128
HG_HEADS = D_MODEL // HG_EXPAND
HG_CHUNK = 64

kernel_name = "hybrid_s5_diffattn_natten_hgrn2_macaron_dit"

f32 = jnp.float32


def rms_norm(x, gain):
    xf = x.astype(f32)
    xf = xf * lax.rsqrt(jnp.mean(jnp.square(xf), axis=-1, keepdims=True) + RMS_EPS)
    return (xf * gain.astype(f32)).astype(x.dtype)


def modulate(h, shift, scale):
    return h * (1 + scale) + shift


def pre_norm(h, g, mod):
    shift, scale, _ = mod
    return modulate(rms_norm(h, g), shift, scale)


def post_residual(h, y, g, mod, weight):
    return h + weight * mod[2] * rms_norm(y, g)


def swiglu(h, w1, w3, w2):
    return (jax.nn.silu(h @ w1) * (h @ w3)) @ w2


def half_ffn(h, g_in, g_out, mod, w1, w3, w2):
    y = swiglu(pre_norm(h, g_in, mod), w1, w3, w2)
    return post_residual(h, y, g_out, mod, 0.5)


def axial_rope(n_tokens, head_dim):
    n_freq = head_dim // 4
    inv_freq = ROPE_BASE ** (-jnp.arange(n_freq, dtype=f32) / n_freq)
    t = jnp.arange(n_tokens)
    row = (t // GRID_W).astype(f32)
    col = (t % GRID_W).astype(f32)
    ang = jnp.concatenate([row[:, None] * inv_freq, col[:, None] * inv_freq], axis=-1)
    return jnp.cos(ang), jnp.sin(ang)


def apply_rope(x, cos, sin):
    xp = x.astype(f32).reshape(x.shape[:-1] + (-1, 2))
    x1, x2 = xp[..., 0], xp[..., 1]
    out = jnp.stack([x1 * cos - x2 * sin, x1 * sin + x2 * cos], axis=-1)
    return out.reshape(x.shape).astype(x.dtype)


def lti_scan(lam_bar, bu, h0):
    if h0 is not None:
        bu = bu.at[0].add(lam_bar * h0)
    a = jnp.broadcast_to(lam_bar, (bu.shape[0], 1) + lam_bar.shape)

    def combine(left, right):
        a1, b1 = left
        a2, b2 = right
        return a1 * a2, a2 * b1 + b2

    _, h = lax.associative_scan(combine, (a, bu), axis=0)
    return h


def s5_mixer(h_lat, h_ctx, a_re, a_im, log_dt, b_re, b_im, c_re, c_im, d_skip, w_glu, b_glu, ctx_out):
    B, S, D = h_lat.shape
    L = h_ctx.shape[1]
    G, N = S5_GROUPS, S5_GROUP
    u_l = h_lat.astype(f32).reshape(B, S, G, N)
    u_c = h_ctx.astype(f32).reshape(B, L, G, N)
    d_vec = d_skip.astype(f32)
    y_l = d_vec * h_lat.astype(f32)
    y_c = d_vec * h_ctx.astype(f32) if ctx_out else None
    for dirn in range(2):
        flip = (lambda a: jnp.flip(a, axis=0)) if dirn else (lambda a: a)
        lam = lax.complex(a_re[dirn].astype(f32), a_im[dirn].astype(f32))
        dt = jnp.exp(log_dt[dirn].astype(f32))[:, None]
        lam_bar = jnp.exp(lam * dt)
        b_bar = ((lam_bar - 1) / lam)[:, :, None] * lax.complex(b_re[dirn].astype(f32), b_im[dirn].astype(f32))
        c_mat = lax.complex(c_re[dirn].astype(f32), c_im[dirn].astype(f32))
        h_c = lti_scan(lam_bar, flip(jnp.einsum('gpn,btgn->tbgp', b_bar, u_c)), None)
        h_l = lti_scan(lam_bar, flip(jnp.einsum('gpn,btgn->tbgp', b_bar, u_l)), h_c[-1])
        y_l = y_l + jnp.real(jnp.einsum('gnp,tbgp->btgn', c_mat, flip(h_l))).reshape(B, S, D)
        if ctx_out:
            y_c = y_c + jnp.real(jnp.einsum('gnp,tbgp->btgn', c_mat, flip(h_c))).reshape(B, L, D)
    w = w_glu.astype(f32)
    bb = b_glu.astype(f32)

    def glu(y):
        z = jax.nn.gelu(y)
        return (z * jax.nn.sigmoid(z @ w + bb)).astype(h_lat.dtype)

    return glu(y_l), (glu(y_c) if ctx_out else None)


def diff_attention(h_lat, h_ctx, w_qkv, w_o, lam_q1, lam_k1, lam_q2, lam_k2, subln_g, layer_idx, ctx_out):
    B, S, D = h_lat.shape
    H, d = DIFF_HEADS, DIFF_HEAD_DIM
    lam_init = 0.8 - 0.6 * math.exp(-0.3 * layer_idx)
    lam = (jnp.exp(jnp.sum(lam_q1.astype(f32) * lam_k1.astype(f32)))
           - jnp.exp(jnp.sum(lam_q2.astype(f32) * lam_k2.astype(f32))) + lam_init)

    def project(h):
        T = h.shape[1]
        q, k, v = jnp.split(h @ w_qkv, 3, axis=-1)
        q = q.reshape(B, T, H, 2, d).transpose(0, 2, 3, 1, 4)
        k = k.reshape(B, T, H, 2, d).transpose(0, 2, 3, 1, 4)
        v = v.reshape(B, T, H, 2 * d).transpose(0, 2, 1, 3)
        return q, k, v

    def mix(q, k, v):
        s = jnp.einsum('bhmqd,bhmkd->bhmqk', q, k).astype(f32) * (d ** -0.5)
        p = jax.nn.softmax(s, axis=-1)
        a = p[:, :, 0] - lam * p[:, :, 1]
        return jnp.einsum('bhqk,bhkd->bhqd', a.astype(v.dtype), v)

    def readout(o):
        T = o.shape[2]
        o = rms_norm(o, subln_g) * (1 - lam_init)
        return o.transpose(0, 2, 1, 3).reshape(B, T, D) @ w_o

    q_c, k_c, v_c = project(h_ctx)
    q_l, k_l, v_l = project(h_lat)
    cos, sin = axial_rope(S, d)
    q_l = apply_rope(q_l, cos, sin)
    k_l = apply_rope(k_l, cos, sin)
    k_all = jnp.concatenate([k_c, k_l], axis=3)
    v_all = jnp.concatenate([v_c, v_l], axis=2)
    n_blk = S // Q_BLOCK
    q_blocks = jnp.moveaxis(q_l.reshape(B, H, 2, n_blk, Q_BLOCK, d), 3, 0)
    o_l = lax.map(lambda qb: mix(qb, k_all, v_all), q_blocks)
    o_l = jnp.moveaxis(o_l, 0, 2).reshape(B, H, S, 2 * d)
    y_ctx = readout(mix(q_c, k_c, v_c)) if ctx_out else None
    return readout(o_l), y_ctx


def neighbourhood_attention(h_lat, h_ctx, w_qkv, w_o, rpb, ctx_out):
    B, S, D = h_lat.shape
    L = h_ctx.shape[1]
    H, d = NA_HEADS, NA_HEAD_DIM
    rows = S // GRID_W
    kh = min(WIN_H, rows)
    scale = d ** -0.5

    def project(h):
        T = h.shape[1]
        qkv = (h @ w_qkv).reshape(B, T, 3, H, d).transpose(2, 0, 3, 1, 4)
        return qkv[0], qkv[1], qkv[2]

    q_c, k_c, v_c = project(h_ctx)
    q_l, k_l, v_l = project(h_lat)
    q_l, k_l, v_l = [a.reshape(B, H, rows, GRID_W, d) for a in (q_l, k_l, v_l)]

    col = jnp.arange(GRID_W)
    c0 = jnp.clip(col - WIN_W // 2, 0, GRID_W - WIN_W)
    col_mask = (col[None, :] >= c0[:, None]) & (col[None, :] < c0[:, None] + WIN_W)
    dc_idx = jnp.clip(col[None, :] - col[:, None] + WIN_W - 1, 0, 2 * WIN_W - 2)
    rpb_cols = rpb.astype(f32)[:, :, dc_idx]

    def row_block(r):
        r0 = jnp.clip(r - kh // 2, 0, rows - kh)
        q_r = lax.dynamic_index_in_dim(q_l, r, axis=2, keepdims=False)
        k_b = lax.dynamic_slice_in_dim(k_l, r0, kh, axis=2)
        v_b = lax.dynamic_slice_in_dim(v_l, r0, kh, axis=2).reshape(B, H, kh * GRID_W, d)
        dr_idx = r0 + jnp.arange(kh) - r + WIN_H - 1
        bias = rpb_cols[:, dr_idx].transpose(0, 2, 1, 3)
        s_lat = jnp.einsum('bhqd,bhrkd->bhqrk', q_r, k_b).astype(f32) * scale + bias[None]
        s_lat = jnp.where(col_mask[:, None, :], s_lat, NEG_INF).reshape(B, H, GRID_W, kh * GRID_W)
        s_ctx = jnp.einsum('bhqd,bhkd->bhqk', q_r, k_c).astype(f32) * scale
        p = jax.nn.softmax(jnp.concatenate([s_ctx, s_lat], axis=-1), axis=-1)
        return (jnp.einsum('bhqk,bhkd->bhqd', p[..., :L].astype(v_c.dtype), v_c)
                + jnp.einsum('bhqk,bhkd->bhqd', p[..., L:].astype(v_b.dtype), v_b))

    o_l = lax.map(row_block, jnp.arange(rows))
    y_lat = o_l.transpose(1, 0, 3, 2, 4).reshape(B, S, D) @ w_o
    y_ctx = None
    if ctx_out:
        p = jax.nn.softmax(jnp.einsum('bhqd,bhkd->bhqk', q_c, k_c).astype(f32) * scale, axis=-1)
        o_c = jnp.einsum('bhqk,bhkd->bhqd', p.astype(v_c.dtype), v_c)
        y_ctx = o_c.transpose(0, 2, 1, 3).reshape(B, L, D) @ w_o
    return y_lat, y_ctx


def gla_chunks(q, k, v, log_f, s0, with_output):
    B, H, T, K = k.shape
    V = v.shape[-1]
    C = HG_CHUNK
    n = T // C
    chunk = lambda a: a.reshape(B, H, n, C, a.shape[-1])
    k, v, log_f = chunk(k), chunk(v), chunk(log_f)
    g = jnp.cumsum(log_f, axis=3)
    g_last = g[:, :, :, -1]
    ds = jnp.einsum('bhnck,bhncv->bhnkv', k * jnp.exp(g_last[:, :, :, None] - g), v)

    def step(s, inp):
        decay, ds_n = inp
        return decay[..., None] * s + ds_n, s

    s_last, s_start = lax.scan(step, s0, (jnp.moveaxis(jnp.exp(g_last), 2, 0), jnp.moveaxis(ds, 2, 0)))
    if not with_output:
        return None, s_last
    q = chunk(q)
    q_dec = q * jnp.exp(g)
    k_inv = k * jnp.exp(-g)
    earlier_or_same = jnp.tril(jnp.ones((C, C), bool))
    att = jnp.where(earlier_or_same, jnp.einsum('bhnck,bhnsk->bhncs', q_dec, k_inv), 0.0)
    o = (jnp.einsum('bhncs,bhnsv->bhncv', att, v)
         + jnp.einsum('bhnck,bhnkv->bhncv', q_dec, jnp.moveaxis(s_start, 0, 2)))
    return o.reshape(B, H, T, V), s_last


def hgrn2_mixer(h_lat, h_ctx, w_qig, w_f, b_f, lb, gn_g, w_o, ctx_out):
    B, S, D = h_lat.shape
    L = h_ctx.shape[1]
    H = HG_HEADS
    lb = lb.astype(f32)
    heads = lambda a: a.astype(f32).reshape(B, a.shape[1], H, -1).transpose(0, 2, 1, 3)

    def log_forget(h, dirn):
        z = (h @ w_f[dirn] + b_f[dirn]).astype(f32)
        return heads(jnp.log(lb + (1 - lb) * jax.nn.sigmoid(z)))

    q_l, i_l, g_l = jnp.split(h_lat @ w_qig, 3, axis=-1)
    q_l, i_l = heads(q_l), heads(i_l)
    if ctx_out:
        q_c, i_c, g_c = jnp.split(h_ctx @ w_qig, 3, axis=-1)
        q_c = heads(q_c)
    else:
        i_c = h_ctx @ w_qig[:, D:2 * D]
    i_c = heads(i_c)
    s0 = jnp.zeros((B, H, HG_EXPAND, D // H), f32)
    o_l = 0.0
    o_c = 0.0
    for dirn in range(2):
        flip = (lambda a: jnp.flip(a, axis=2)) if dirn else (lambda a: a)
        lf_c = log_forget(h_ctx, dirn)
        lf_l = log_forget(h_lat, dirn)
        oc, s_c = gla_chunks(flip(q_c) if ctx_out else None, flip(-jnp.expm1(lf_c)), flip(i_c),
                             flip(lf_c), s0, ctx_out)
        ol, _ = gla_chunks(flip(q_l), flip(-jnp.expm1(lf_l)), flip(i_l), flip(lf_l), s_c, True)
        o_l = o_l + flip(ol)
        if ctx_out:
            o_c = o_c + flip(oc)

    def readout(o, gate):
        T = o.shape[2]
        o = rms_norm(o, gn_g).transpose(0, 2, 1, 3).reshape(B, T, D)
        return (o * jax.nn.silu(gate.astype(f32))).astype(h_lat.dtype) @ w_o

    return readout(o_l, g_l), (readout(o_c, g_c) if ctx_out else None)


def setup_inputs(seed: int = 0) -> dict:
    key = jax.random.key(seed)
    keys = iter(jax.random.split(key, 64))

    def nrm(shape, scale):
        return scale * jax.random.normal(next(keys), shape, f32)

    def gain(shape):
        return 1.0 + 0.02 * jax.random.normal(next(keys), shape, f32)

    nA, nB, nC, nD = [len(range(m, DEPTH, N_MIXERS)) for m in range(N_MIXERS)]
    D = D_MODEL
    G, N, P = S5_GROUPS, S5_GROUP, S5_STATE
    sd = D ** -0.5
    n_idx = jnp.arange(P, dtype=f32)
    return {
        'x': nrm((BATCH, SEQ, D), 1.0),
        'c': nrm((BATCH, D), 1.0),
        'ctx': nrm((BATCH, CTX_LEN, D), 1.0),
        'c_ctx': nrm((D,), 1.0),
        'w_ada': nrm((DEPTH, D, N_MOD * D), sd),
        'b_ada': nrm((DEPTH, N_MOD * D), 0.02),
        'g_pre': gain((DEPTH, N_SUB, D)),
        'g_post': gain((DEPTH, N_SUB, D)),
        'w_ff1': nrm((DEPTH, 2, D, D_FF), sd),
        'w_ff3': nrm((DEPTH, 2, D, D_FF), sd),
        'w_ff2': nrm((DEPTH, 2, D_FF, D), D_FF ** -0.5),
        's5_a_re': -0.5 * gain((nA, 2, G, P)),
        's5_a_im': math.pi * n_idx + nrm((nA, 2, G, P), 0.01),
        's5_log_dt': jax.random.uniform(next(keys), (nA, 2, G), f32,
                                        minval=math.log(S5_DT_MIN), maxval=math.log(S5_DT_MAX)),
        's5_b_re': nrm((nA, 2, G, P, N), (2 * N) ** -0.5),
        's5_b_im': nrm((nA, 2, G, P, N), (2 * N) ** -0.5),
        's5_c_re': nrm((nA, 2, G, N, P), (2 * P) ** -0.5),
        's5_c_im': nrm((nA, 2, G, N, P), (2 * P) ** -0.5),
        's5_d': nrm((nA, D), 1.0),
        's5_w_glu': nrm((nA, D, D), sd),
        's5_b_glu': nrm((nA, D), 0.02),
        'da_w_qkv': nrm((nB, D, 3 * D), sd),
        'da_w_o': nrm((nB, D, D), sd),
        'da_lam_q1': nrm((nB, DIFF_HEAD_DIM), 0.1),
        'da_lam_k1': nrm((nB, DIFF_HEAD_DIM), 0.1),
        'da_lam_q2': nrm((nB, DIFF_HEAD_DIM), 0.1),
        'da_lam_k2': nrm((nB, DIFF_HEAD_DIM), 0.1),
        'da_subln': gain((nB, 2 * DIFF_HEAD_DIM)),
        'na_w_qkv': nrm((nC, D, 3 * D), sd),
        'na_w_o': nrm((nC, D, D), sd),
        'na_rpb': nrm((nC, NA_HEADS, 2 * WIN_H - 1, 2 * WIN_W - 1), 0.02),
        'hg_w_qig': nrm((nD, D, 3 * D), sd),
        'hg_w_f': nrm((nD, 2, D, D), sd),
        'hg_b_f': nrm((nD, 2, D), 0.1),
        'hg_lb_logits': nrm((DEPTH, D), 0.1),
        'hg_gnorm': gain((nD, D // HG_HEADS)),
        'hg_w_o': nrm((nD, D, D), sd),
    }


def reference(x, c, ctx, c_ctx, w_ada, b_ada, g_pre, g_post, w_ff1, w_ff3, w_ff2,
              s5_a_re, s5_a_im, s5_log_dt, s5_b_re, s5_b_im, s5_c_re, s5_c_im, s5_d, s5_w_glu, s5_b_glu,
              da_w_qkv, da_w_o, da_lam_q1, da_lam_k1, da_lam_q2, da_lam_k2, da_subln,
              na_w_qkv, na_w_o, na_rpb,
              hg_w_qig, hg_w_f, hg_b_f, hg_lb_logits, hg_gnorm, hg_w_o):
    B, S, D = x.shape
    lb_p = jax.nn.softmax(hg_lb_logits.astype(f32), axis=0)
    lower_bounds = jnp.cumsum(lb_p, axis=0) - lb_p[0]
    silu_c = jax.nn.silu(c)
    silu_cc = jax.nn.silu(c_ctx)
    x_lat, x_ctx = x, ctx
    for i in range(DEPTH):
        last = i == DEPTH - 1
        occ = i // N_MIXERS
        kind = i % N_MIXERS
        m_lat = (silu_c @ w_ada[i] + b_ada[i]).reshape(B, N_SUB, 3, 1, D)
        m_ctx = (silu_cc @ w_ada[i] + b_ada[i]).reshape(N_SUB, 3, D)
        lat_mod = [(m_lat[:, j, 0], m_lat[:, j, 1], m_lat[:, j, 2]) for j in range(N_SUB)]
        ctx_mod = [(m_ctx[j, 0], m_ctx[j, 1], m_ctx[j, 2]) for j in range(N_SUB)]

        x_lat = half_ffn(x_lat, g_pre[i, 0], g_post[i, 0], lat_mod[0], w_ff1[i, 0], w_ff3[i, 0], w_ff2[i, 0])
        x_ctx = half_ffn(x_ctx, g_pre[i, 0], g_post[i, 0], ctx_mod[0], w_ff1[i, 0], w_ff3[i, 0], w_ff2[i, 0])

        h_lat = pre_norm(x_lat, g_pre[i, 1], lat_mod[1])
        h_ctx = pre_norm(x_ctx, g_pre[i, 1], ctx_mod[1])
        if kind == 0:
            y_lat, y_ctx = s5_mixer(h_lat, h_ctx, s5_a_re[occ], s5_a_im[occ], s5_log_dt[occ], s5_b_re[occ],
                                    s5_b_im[occ], s5_c_re[occ], s5_c_im[occ], s5_d[occ], s5_w_glu[occ],
                                    s5_b_glu[occ], not last)
        elif kind == 1:
            y_lat, y_ctx = diff_attention(h_lat, h_ctx, da_w_qkv[occ], da_w_o[occ], da_lam_q1[occ], da_lam_k1[occ],
                                          da_lam_q2[occ], da_lam_k2[occ], da_subln[occ], i, not last)
        elif kind == 2:
            y_lat, y_ctx = neighbourhood_attention(h_lat, h_ctx, na_w_qkv[occ], na_w_o[occ], na_rpb[occ], not last)
        else:
            y_lat, y_ctx = hgrn2_mixer(h_lat, h_ctx, hg_w_qig[occ], hg_w_f[occ], hg_b_f[occ], lower_bounds[i],
                                       hg_gnorm[occ], hg_w_o[occ], not last)
        x_lat = post_residual(x_lat, y_lat, g_post[i, 1], lat_mod[1], 1.0)

        x_lat = half_ffn(x_lat, g_pre[i, 2], g_post[i, 2], lat_mod[2], w_ff1[i, 1], w_ff3[i, 1], w_ff2[i, 1])
        if not last:
            x_ctx = post_residual(x_ctx, y_ctx, g_post[i, 1], ctx_mod[1], 1.0)
            x_ctx = half_ffn(x_ctx, g_pre[i, 2], g_post[i, 2], ctx_mod[2], w_ff1[i, 1], w_ff3[i, 1], w_ff2[i, 1])
    return x_lat
```

```python
import math
import numpy as np
from contextlib import ExitStack
import concourse.bass as bass
import concourse.mybir as mybir
from concourse.bass_utils import run_bass_kernel_spmd

F32 = mybir.dt.float32
BF16 = mybir.dt.bfloat16
AF = mybir.ActivationFunctionType
ALU = mybir.AluOpType

D = 1024
S = 4096
LC = 256
T = S + LC
DFF = 2816
NCH = 8
NFF = 22
DEPTH = 4
EPS = 1e-6
EPOCH = 30000
NRING = 32
NPR = 40


class KB:
    def __init__(self):
        self.nc = bass.Bass("TRN2", target_bir_lowering=False)
        self.es = ExitStack()
        nc = self.nc
        self.eng = {"pe": nc.tensor, "dve": nc.vector, "act": nc.scalar, "pool": nc.gpsimd, "sp": nc.sync}
        self.sems = {e: [] for e in self.eng}
        self.cnt = {e: 0 for e in self.eng}
        self.known = {}
        self.last_w = {}
        self.readers = {}
        self.ring = [self.es.enter_context(nc.semaphore("dr%d" % i)) for i in range(NRING)]
        self.ring_val = [0] * NRING
        self.ndma = 0
        self.nins = {e: 0 for e in self.eng}
        self.scopes = []
        self.pring = [self.es.enter_context(nc.semaphore("pr%d" % i)) for i in range(NPR)]
        self.pr_used = 0
        self.uid = 0
        self.dummy = self.sb("dummy", [128, 1], F32)

    def sb(self, name, shape, dtype):
        self.uid += 1
        t = self.nc.sbuf_tensor("%s_%d" % (name, self.uid), list(shape), dtype)
        st = self.scopes[-1] if self.scopes else self.es
        return st.enter_context(t)

    def ps(self, name, shape, dtype=F32):
        st = self.scopes[-1] if self.scopes else self.es
        return st.enter_context(self.nc.psum_tensor(name, list(shape), dtype))

    def push(self):
        self.scopes.append(ExitStack())

    def pop(self):
        self.barrier()
        self.scopes.pop().close()

    def _cursem(self, e):
        if not self.sems[e] or self.cnt[e] >= EPOCH:
            s = self.es.enter_context(self.nc.semaphore("s_%s_%d" % (e, len(self.sems[e]))))
            self.sems[e].append(s)
            self.cnt[e] = 0
        return len(self.sems[e]) - 1

    @staticmethod
    def _key(x):
        if isinstance(x, (tuple, str)):
            return x
        if hasattr(x, "tensor"):
            return x.tensor.name
        return x.name

    def _wait(self, e, dep):
        if dep[0] == "c":
            _, te, ep, c = dep
            if te == e and e == "pe":
                return
            kk = (e, te, ep)
            if self.known.get(kk, 0) >= c:
                return
            self.eng[e].wait_ge(self.sems[te][ep], c)
            self.known[kk] = c
        elif dep[0] == "p":
            _, slot, val = dep
            kk = (e, "pring", slot)
            if self.known.get(kk, 0) >= val:
                return
            self.eng[e].wait_ge(self.pring[slot], val)
            self.known[kk] = val
        else:
            _, slot, val = dep
            kk = (e, "ring", slot)
            if self.known.get(kk, 0) >= val:
                return
            self.eng[e].wait_ge(self.ring[slot], val)
            self.known[kk] = val

    def _deps(self, e, r, w):
        deps = []
        for k in r:
            k = self._key(k)
            if k in self.last_w:
                deps.append(self.last_w[k])
        for k in w:
            k = self._key(k)
            if k in self.last_w:
                deps.append(self.last_w[k])
            deps.extend(self.readers.get(k, ()))
        for d in deps:
            self._wait(e, d)

    def _record(self, me, r, w):
        for k in w:
            k = self._key(k)
            self.last_w[k] = me
            self.readers[k] = []
        for k in r:
            k = self._key(k)
            lst = self.readers.setdefault(k, [])
            lst[:] = [d for d in lst if d[:-1] != me[:-1]]
            lst.append(me)

    def op(self, e, fn, r=(), w=()):
        ep = self._cursem(e)
        self._deps(e, r, w)
        ins = fn(self.eng[e])
        self.cnt[e] += 1
        self.nins[e] += 1
        ins.then_inc(self.sems[e][ep], 1)
        self._record(("c", e, ep, self.cnt[e]), r, w)
        return ins

    def dma(self, out, in_, r=None, w=None, q="sp", **kw):
        r = [in_] if r is None else r
        w = [out] if w is None else w
        if q == "pool":
            assert self.pr_used < NPR, "too many gpsimd DMAs in one phase"
            slot = self.pr_used
            self.pr_used += 1
            self._deps(q, r, w)
            ins = self.eng[q].dma_start(out=out, in_=in_, **kw)
            ins.then_inc(self.pring[slot], 16)
            self._record(("p", slot, 16), r, w)
            return ins
        slot = self.ndma % NRING
        self.ndma += 1
        if self.ring_val[slot] > 0:
            self._wait(q, ("d", slot, self.ring_val[slot]))
        self._deps(q, r, w)
        ins = self.eng[q].dma_start(out=out, in_=in_, **kw)
        self.ring_val[slot] += 16
        ins.then_inc(self.ring[slot], 16)
        self._record(("d", slot, self.ring_val[slot]), r, w)
        return ins

    def barrier(self):
        self._wait_all("pool")
        if self.pr_used:
            for slot in range(self.pr_used):
                self._wait("pool", ("p", slot, 16))
            for slot in range(self.pr_used):
                self.eng["pool"].sem_clear(self.pring[slot])
            self.known = {kk: v for kk, v in self.known.items() if kk[1] != "pring"}
            self.pr_used = 0
            self.op("pool", lambda e: e.memset(self.dummy[:], 0.0), w=[self.dummy])
        for e in self.eng:
            if e != "pool":
                self._wait_all(e)
        self.last_w = {}
        self.readers = {}

    def _wait_all(self, e):
        for slot in range(NRING):
            if self.ring_val[slot]:
                self._wait(e, ("d", slot, self.ring_val[slot]))
        for te in self.eng:
            if te != e and self.sems[te] and self.cnt[te]:
                self._wait(e, ("c", te, len(self.sems[te]) - 1, self.cnt[te]))

    def close(self):
        self.barrier()
        while self.scopes:
            self.scopes.pop().close()
        self.es.close()


def bcast_mid(ap2d, n):
    a = ap2d.ap
    return bass.AP(ap2d.tensor, ap2d.offset, [list(a[0]), [0, n], list(a[1])])


class Prog:
    def __init__(self, stop_after=None, layers=None):
        self.k = KB()
        self.nc = self.k.nc
        self.stop_after = stop_after
        self.layers = list(range(DEPTH)) if layers is None else layers
        self.build()

    def din(self, name, shape, dt=F32):
        return self.nc.dram_tensor(name, list(shape), dt, kind="ExternalInput")

    def wload(self, dst, src2d, kc_n, ncols, q="pool", col0=0, stage=None):
        k = self.k
        if stage is None:
            c = 0
            while c < ncols:
                w = min(2048, ncols - c)
                k0 = 0
                while k0 < kc_n:
                    kn = min(16, kc_n - k0)
                    k.dma(dst[:, k0:k0 + kn, c:c + w],
                          src2d[k0 * 128:(k0 + kn) * 128, col0 + c:col0 + c + w].rearrange("(k p) n -> p k n", p=128), q="sp")
                    k0 += kn
                c += w
            return
        CW = 256
        for k0 in range(0, kc_n, 8):
            kn = min(8, kc_n - k0)
            for c in range(0, ncols, CW):
                w = min(CW, ncols - c)
                st = stage[self._stg % len(stage)]
                self._stg += 1
                k.dma(st[:, 0:kn, 0:w],
                      src2d[k0 * 128:(k0 + kn) * 128, col0 + c:col0 + c + w].rearrange("(k p) n -> p k n", p=128), q="sp")
                k.op("pool", lambda e: e.tensor_copy(out=dst[:, k0:k0 + kn, c:c + w], in_=st[:, 0:kn, 0:w]), r=[st], w=[dst])

    def build(self):
        k, nc = self.k, self.nc
        self.xin = self.din("xin", [D, T])
        self.cT = self.din("cT", [128, NCH, 2])
        self.w_ada = self.din("w_ada", [DEPTH, D, 9 * D])
        self.b_ada = self.din("b_ada", [128, DEPTH, 72])
        self.g_pre = self.din("g_pre", [128, DEPTH, 3, NCH])
        self.g_post = self.din("g_post", [128, DEPTH, 3, NCH])
        self.w_ff1 = self.din("w_ff1", [DEPTH, 2, D, DFF])
        self.w_ff3 = self.din("w_ff3", [DEPTH, 2, D, DFF])
        self.w_ff2 = self.din("w_ff2", [DEPTH, 2, DFF, D])
        self.da_wqkv = self.din("da_w_qkv", [D, 3 * D])
        self.da_wo = self.din("da_w_o", [D, D])
        self.da_lam = self.din("da_lam", [64, 4])
        self.da_subln = self.din("da_subln", [128, 1])
        self.na_wqkv = self.din("na_w_qkv", [D, 3 * D])
        self.na_wo = self.din("na_w_o", [D, D])
        self.na_rpbg = self.din("na_rpbg", [16, 128, 1408])
        self.s5_are = self.din("s5_are", [128, 2, 32])
        self.s5_aim = self.din("s5_aim", [128, 2, 32])
        self.s5_ldt = self.din("s5_ldt", [128, 2, 32])
        self.s5_bre = self.din("s5_bre", [128, 2, 32, 16])
        self.s5_bim = self.din("s5_bim", [128, 2, 32, 16])
        self.s5_cre = self.din("s5_cre", [128, 2, 32, 16])
        self.s5_cim = self.din("s5_cim", [128, 2, 32, 16])
        self.s5_d = self.din("s5_d", [128, NCH])
        self.s5_bglu = self.din("s5_bglu", [128, NCH])
        self.s5_wglu = self.din("s5_w_glu", [D, D])
        self.hg_wqig = self.din("hg_w_qig", [D, 3 * D])
        self.hg_wf = self.din("hg_w_f", [2, D, D])
        self.hg_bf = self.din("hg_b_f", [128, 2, NCH])
        self.hg_lb = self.din("hg_lb", [128, DEPTH, NCH])
        self.hg_gn = self.din("hg_gn", [128, 1])
        self.hg_wo = self.din("hg_w_o", [D, D])
        self.c_triu = self.din("c_triu", [128, 128])
        self.c_tril = self.din("c_tril", [128, 128])
        self.Fs = nc.dram_tensor("Fs", [2, D, T], F32, kind="Internal")
        self.c_ropeC = self.din("c_ropeC", [128, S])
        self.c_ropeS = self.din("c_ropeS", [128, S])
        self.c_pm = self.din("c_pm", [128, 128])
        self.c_ident = self.din("c_ident", [128, 128])
        self.c_maskI = self.din("c_maskI", [128, 1408])
        self.c_maskF = self.din("c_maskF", [128, 1408])
        self.Hs = nc.dram_tensor("Hs", [D, T], BF16, kind="Internal")
        self.Ys = nc.dram_tensor("Ys", [D, T], F32, kind="Internal")
        self.QKs = nc.dram_tensor("QKs", [2 * D, T], BF16, kind="Internal")
        self.Vs = nc.dram_tensor("Vs", [T, D], BF16, kind="Internal")
        self.AO = nc.dram_tensor("AOs", [D, T], BF16, kind="Internal")
        self.out = nc.dram_tensor("out", [D, S], F32, kind="ExternalOutput")
        self.X = nc.dram_tensor("Xres", [D, T], F32, kind="Internal")

        self.ones_bf = k.sb("ones_bf", [128, 128], BF16)
        k.op("dve", lambda e: e.memset(self.ones_bf[:], 1.0), w=[self.ones_bf])
        self.A = k.sb("modA", [128, DEPTH, 3, NCH, 2], F32)
        self.SH = k.sb("modSH", [128, DEPTH, 3, NCH, 2], F32)
        self.GT = k.sb("modGT", [128, DEPTH, 3, NCH, 2], F32)
        self.psb = [k.ps("psb%d" % i, [128, 512], F32) for i in range(7)]
        self.ps_bf = k.ps("psbf", [128, 1024], BF16)
        self._eps = k.sb("eps_c", [128, 1], F32)
        k.op("dve", lambda e: e.memset(self._eps[:], EPS), w=[self._eps])

        self.setup_mod()
        if self.stop_after == "mod":
            k.dma(self.out.ap(), self.xin.ap()[:, LC:T], q="sp")
            k.close()
            return
        first = True
        for i in self.layers:
            last = i == DEPTH - 1
            self.ffn(i, 0, 0, src=(self.xin if first else self.X), dst=self.X, with_ctx=True)
            first = False
            if self.stop_after == "a%d" % i:
                return self.finish_dbg()
            kind = i % 4
            if kind in (0, 1, 2, 3):
                self.mixer_prologue(i)
                if kind == 0:
                    self.s5_core(i)
                elif kind == 3:
                    self.hg_proj(i)
                    self.hg_core(i)
                    self.oproj_phase(self.hg_wo, with_ctx=not last)
                elif kind == 1:
                    self.qkv_phase(self.da_wqkv, rope=True, qscale=None)
                    self.da_attn(i)
                    self.oproj_phase(self.da_wo)
                else:
                    self.qkv_phase(self.na_wqkv, rope=False, qscale=0.125)
                    self.na_attn(i)
                    self.oproj_phase(self.na_wo)
                self.mixer_epilogue(i, with_ctx=not last)
            if self.stop_after == "m%d" % i:
                return self.finish_dbg()
            self.ffn(i, 1, 2, src=self.X, dst=self.X, with_ctx=not last, final=last)
            if self.stop_after == "b%d" % i:
                return self.finish_dbg()
        k.close()

    def finish_dbg(self):
        k = self.k
        k.dma(self.out.ap(), self.X.ap()[:, LC:T], q="sp")
        k.close()

    def setup_mod(self):
        k = self.k
        k.push()
        sc = k.sb("sc", [128, NCH, 2], F32)
        k.dma(sc[:], self.cT.ap())
        sg = k.sb("sg", [128, NCH, 2], F32)
        k.op("act", lambda e: e.activation(out=sg[:], in_=sc[:], func=AF.Sigmoid), r=[sc], w=[sg])
        k.op("dve", lambda e: e.tensor_tensor(out=sc[:], in0=sc[:], in1=sg[:], op=ALU.mult), r=[sc, sg], w=[sc])
        bada = k.sb("bada", [128, DEPTH, 72], F32)
        k.dma(bada[:], self.b_ada.ap())
        gpre = k.sb("gpre", [128, DEPTH, 3, NCH], F32)
        gpost = k.sb("gpost", [128, DEPTH, 3, NCH], F32)
        k.dma(gpre[:], self.g_pre.ap())
        k.dma(gpost[:], self.g_post.ap())
        m = k.sb("m_sb", [128, DEPTH, 72, 2], F32)
        PIECE = 1152
        wbuf = [k.sb("wada%d" % i, [128, NCH, PIECE], F32) for i in range(2)]
        pi = 0
        for i in range(DEPTH):
            ps = self.psb[i % 2]
            for pc in range(8):
                wb = wbuf[pi % 2]
                pi += 1
                self.wload(wb, self.w_ada.ap()[i], NCH, PIECE, col0=pc * PIECE)
                for jj in range(9):
                    jc = pc * 9 + jj
                    for kc in range(NCH):
                        k.op("pe", lambda e: e.matmul(out=ps[:, 2 * jc:2 * jc + 2], lhsT=wb[:, kc, jj * 128:(jj + 1) * 128],
                                                      rhs=sc[:, kc, :], start=(kc == 0), stop=(kc == NCH - 1)),
                             r=[wb, sc], w=[ps])
            k.op("dve", lambda e: e.tensor_tensor(out=m[:, i], in0=ps[:, 0:144].rearrange("p (j v) -> p j v", v=2),
                                                  in1=bada[:, i].unsqueeze(2).to_broadcast([128, 72, 2]), op=ALU.add),
                 r=[ps, bada], w=[m])
        for i in range(DEPTH):
            for j in range(3):
                base = j * 24
                sh = m[:, i, base + 0:base + 8, :]
                scl = m[:, i, base + 8:base + 16, :]
                gt = m[:, i, base + 16:base + 24, :]
                wgt = 0.5 if j != 1 else 1.0
                gp = gpre[:, i, j, :].unsqueeze(2).to_broadcast([128, NCH, 2])
                gq = gpost[:, i, j, :].unsqueeze(2).to_broadcast([128, NCH, 2])
                k.op("dve", lambda e: e.scalar_tensor_tensor(out=self.A[:, i, j], in0=scl, scalar=1.0, in1=gp, op0=ALU.add, op1=ALU.mult),
                     r=[m, gpre], w=[self.A])
                k.op("dve", lambda e: e.scalar_tensor_tensor(out=self.GT[:, i, j], in0=gt, scalar=wgt, in1=gq, op0=ALU.mult, op1=ALU.mult),
                     r=[m, gpost], w=[self.GT])
                k.op("dve", lambda e: e.tensor_copy(out=self.SH[:, i, j], in_=sh), r=[m], w=[self.SH])
        k.pop()

    def rstd_from(self, sq, nchunks, N, rstd, ps, dim):
        k = self.k
        for c in range(nchunks):
            k.op("pe", lambda e: e.matmul(out=ps[:, 0:N], lhsT=self.ones_bf[:], rhs=sq[:, c, 0:N], start=(c == 0), stop=(c == nchunks - 1)),
                 r=[sq, self.ones_bf], w=[ps])
        k.op("act", lambda e: e.activation(out=rstd[:, 0:N], in_=ps[:, 0:N], func=AF.Sqrt, bias=self.eps_ap(), scale=1.0 / dim), r=[ps], w=[rstd])
        k.op("dve", lambda e: e.reciprocal(out=rstd[:, 0:N], in_=rstd[:, 0:N]), r=[rstd], w=[rstd])

    def eps_ap(self):
        return self._eps[:]

    def token_tiles(self, with_ctx, n):
        tiles = []
        if with_ctx:
            for t0 in range(0, LC, n):
                tiles.append((t0, min(n, LC - t0), 1))
        for t0 in range(LC, T, n):
            tiles.append((t0, min(n, T - t0), 0))
        return tiles

    def prenorm(self, xt, N, i, j, v, sq, tmp, h, rstd, ps):
        k = self.k
        k.op("act", lambda e: e.activation(out=sq[:, :, 0:N], in_=xt[:, :, 0:N], func=AF.Square), r=[xt], w=[sq])
        self.rstd_from(sq, NCH, N, rstd, ps, D)
        for c in range(NCH):
            k.op("dve", lambda e: e.scalar_tensor_tensor(out=tmp[:, c, 0:N], in0=xt[:, c, 0:N], scalar=self.A[:, i, j, c, v:v + 1],
                                                         in1=rstd[:, 0:N], op0=ALU.mult, op1=ALU.mult), r=[xt, rstd, self.A], w=[tmp])
            k.op("act", lambda e: e.activation(out=h[:, c, 0:N], in_=tmp[:, c, 0:N], func=AF.Identity, bias=self.SH[:, i, j, c, v:v + 1], scale=1.0),
                 r=[tmp, self.SH], w=[h])

    def postres(self, y, xt, N, i, j, v, sq, rstd, ps):
        k = self.k
        self.rstd_from(sq, NCH, N, rstd, ps, D)
        for c in range(NCH):
            k.op("dve", lambda e: e.scalar_tensor_tensor(out=y[:, c, 0:N], in0=y[:, c, 0:N], scalar=self.GT[:, i, j, c, v:v + 1],
                                                         in1=rstd[:, 0:N], op0=ALU.mult, op1=ALU.mult), r=[y, rstd, self.GT], w=[y])
        k.op("pool", lambda e: e.tensor_tensor(out=xt[:, :, 0:N], in0=xt[:, :, 0:N], in1=y[:, :, 0:N], op=ALU.add), r=[xt, y], w=[xt])


    def mixer_prologue(self, i):
        k = self.k
        NT = 256
        k.push()
        xts = [k.sb("pxt%d" % b, [128, NCH, NT], F32) for b in range(2)]
        sq = k.sb("psq", [128, NCH, NT], BF16)
        tmp = k.sb("ptmp", [128, NCH, NT], F32)
        hs = [k.sb("ph%d" % b, [128, NCH, NT], BF16) for b in range(2)]
        rstd = k.sb("prstd", [128, NT], F32)
        for ti, (t0, N, v) in enumerate(self.token_tiles(True, NT)):
            xt, h = xts[ti % 2], hs[ti % 2]
            k.dma(xt[:, :, 0:N], self.X.ap()[:, t0:t0 + N].rearrange("(c p) n -> p c n", p=128))
            self.prenorm(xt, N, i, 1, v, sq, tmp, h, rstd, self.psb[0])
            k.dma(self.Hs.ap()[:, t0:t0 + N].rearrange("(c p) n -> p c n", p=128), h[:, :, 0:N], w=[("Hs", t0)])
        k.pop()

    def mixer_epilogue(self, i, with_ctx):
        k = self.k
        NT = 256
        k.push()
        xts = [k.sb("ext%d" % b, [128, NCH, NT], F32) for b in range(2)]
        ys = [k.sb("eyt%d" % b, [128, NCH, NT], F32) for b in range(2)]
        sq = k.sb("esq", [128, NCH, NT], BF16)
        rstd = k.sb("erstd", [128, NT], F32)
        for ti, (t0, N, v) in enumerate(self.token_tiles(with_ctx, NT)):
            xt, y = xts[ti % 2], ys[ti % 2]
            k.dma(xt[:, :, 0:N], self.X.ap()[:, t0:t0 + N].rearrange("(c p) n -> p c n", p=128))
            k.dma(y[:, :, 0:N], self.Ys.ap()[:, t0:t0 + N].rearrange("(c p) n -> p c n", p=128))
            k.op("act", lambda e: e.activation(out=sq[:, :, 0:N], in_=y[:, :, 0:N], func=AF.Square), r=[y], w=[sq])
            self.postres(y, xt, N, i, 1, v, sq, rstd, self.psb[0])
            k.dma(self.X.ap()[:, t0:t0 + N].rearrange("(c p) n -> p c n", p=128), xt[:, :, 0:N], w=[("X", t0)])
        k.pop()

    def qkv_phase(self, w_dram, rope, qscale):
        k = self.k
        NT = 512
        k.push()
        w = k.sb("wqkv", [128, NCH, 3 * D], BF16)
        stage = [k.sb("wstg%d" % b, [128, 8, 256], F32) for b in range(2)]
        self._stg = 0
        self.wload(w, w_dram.ap(), NCH, 3 * D, stage=stage)
        if rope:
            Ct = k.sb("ropeC", [128, S], F32)
            St = k.sb("ropeS", [128, S], F32)
            k.dma(Ct[:], self.c_ropeC.ap())
            k.dma(St[:], self.c_ropeS.ap())
            pmf = k.sb("pmf", [128, 128], F32)
            pm = k.sb("pm", [128, 128], BF16)
            k.dma(pmf[:], self.c_pm.ap())
            k.op("dve", lambda e: e.tensor_copy(out=pm[:], in_=pmf[:]), r=[pmf], w=[pm])
        hts = [k.sb("ht%d" % b, [128, NCH, NT], BF16) for b in range(2)]
        qs = [k.sb("qs%d" % b, [128, NT], BF16) for b in range(2)]
        t1 = [k.sb("t1%d" % b, [128, NT], F32) for b in range(2)]
        t2 = [k.sb("t2%d" % b, [128, NT], F32) for b in range(2)]
        qo = [k.sb("qo%d" % b, [128, NT], BF16) for b in range(2)]
        vo = [k.sb("vo%d" % b, [128, 512], BF16) for b in range(2)]
        nv = 0
        for ti, (t0, N, v) in enumerate(self.token_tiles(True, NT)):
            ht = hts[ti % 2]
            k.dma(ht[:, :, 0:N], self.Hs.ap()[:, t0:t0 + N].rearrange("(c p) n -> p c n", p=128))
            for cc in range(16):
                b = cc % 2
                ps = self.psb[1 + b]
                for c in range(NCH):
                    k.op("pe", lambda e: e.matmul(out=ps[:, 0:N], lhsT=w[:, c, cc * 128:(cc + 1) * 128], rhs=ht[:, c, 0:N],
                                                  start=(c == 0), stop=(c == NCH - 1)), r=[w, ht], w=[ps])
                if rope and v == 0:
                    l0 = t0 - LC
                    sw = self.psb[3 + b]
                    k.op("dve", lambda e: e.tensor_copy(out=qs[b][:, 0:N], in_=ps[:, 0:N]), r=[ps], w=[qs[b]])
                    k.op("pe", lambda e: e.matmul(out=sw[:, 0:N], lhsT=pm[:], rhs=qs[b][:, 0:N], start=True, stop=True), r=[pm, qs[b]], w=[sw])
                    k.op("dve", lambda e: e.tensor_tensor(out=t1[b][:, 0:N], in0=ps[:, 0:N], in1=Ct[:, l0:l0 + N], op=ALU.mult), r=[ps, Ct], w=[t1[b]])
                    k.op("dve", lambda e: e.tensor_tensor(out=t2[b][:, 0:N], in0=sw[:, 0:N], in1=St[:, l0:l0 + N], op=ALU.mult), r=[sw, St], w=[t2[b]])
                    k.op("pool", lambda e: e.tensor_tensor(out=qo[b][:, 0:N], in0=t1[b][:, 0:N], in1=t2[b][:, 0:N], op=ALU.add), r=[t1[b], t2[b]], w=[qo[b]])
                elif qscale is not None and cc < 8:
                    k.op("dve", lambda e: e.tensor_scalar(out=qo[b][:, 0:N], in0=ps[:, 0:N], scalar1=float(qscale), scalar2=None, op0=ALU.mult), r=[ps], w=[qo[b]])
                else:
                    k.op("dve", lambda e: e.tensor_copy(out=qo[b][:, 0:N], in_=ps[:, 0:N]), r=[ps], w=[qo[b]])
                k.dma(self.QKs.ap()[cc * 128:(cc + 1) * 128, t0:t0 + N], qo[b][:, 0:N], w=[("QKs", cc, t0)])
            for sub in range(N // 128):
                for vb in range(2):
                    ps = self.psb[5 + vb]
                    for c in range(NCH):
                        k.op("pe", lambda e: e.matmul(out=ps[:, :], lhsT=ht[:, c, sub * 128:(sub + 1) * 128], rhs=w[:, c, 2 * D + vb * 512:2 * D + (vb + 1) * 512],
                                                      start=(c == 0), stop=(c == NCH - 1)), r=[w, ht], w=[ps])
                    vv = vo[nv % 2]
                    nv += 1
                    k.op("act", lambda e: e.activation(out=vv[:], in_=ps[:, :], func=AF.Identity), r=[ps], w=[vv])
                    k.dma(self.Vs.ap()[t0 + sub * 128:t0 + (sub + 1) * 128, vb * 512:(vb + 1) * 512], vv[:], w=[("Vs", t0, sub, vb)])
        k.pop()

    def oproj_phase(self, wo_dram, with_ctx=True):
        k = self.k
        NT = 512
        k.push()
        wo = k.sb("wo", [128, NCH, D], BF16)
        stage = [k.sb("wstg%d" % b, [128, 8, 256], F32) for b in range(2)]
        self._stg = 0
        self.wload(wo, wo_dram.ap(), NCH, D, stage=stage)
        aos = [k.sb("ao%d" % b, [128, NCH, NT], BF16) for b in range(2)]
        yo = [k.sb("yo%d" % b, [128, NT], F32) for b in range(2)]
        for ti, (t0, N, v) in enumerate(self.token_tiles(with_ctx, NT)):
            ao = aos[ti % 2]
            k.dma(ao[:, :, 0:N], self.AO.ap()[:, t0:t0 + N].rearrange("(c p) n -> p c n", p=128))
            for c2 in range(NCH):
                ps = self.psb[1 + c2 % 2]
                for c in range(NCH):
                    k.op("pe", lambda e: e.matmul(out=ps[:, 0:N], lhsT=wo[:, c, c2 * 128:(c2 + 1) * 128], rhs=ao[:, c, 0:N],
                                                  start=(c == 0), stop=(c == NCH - 1)), r=[wo, ao], w=[ps])
                y = yo[c2 % 2]
                k.op("dve", lambda e: e.tensor_copy(out=y[:, 0:N], in_=ps[:, 0:N]), r=[ps], w=[y])
                k.dma(self.Ys.ap()[c2 * 128:(c2 + 1) * 128, t0:t0 + N], y[:, 0:N], w=[("Ys", c2, t0)])
        k.pop()

    def da_attn(self, i):
        k = self.k
        lam_init = 0.8 - 0.6 * math.exp(-0.3 * i)
        k.push()
        lv = k.sb("lamv", [64, 4], F32)
        k.dma(lv[:], self.da_lam.ap())
        pr = k.sb("lampr", [64, 2], F32)
        k.op("dve", lambda e: e.tensor_tensor(out=pr[:, 0:1], in0=lv[:, 0:1], in1=lv[:, 1:2], op=ALU.mult), r=[lv], w=[pr])
        k.op("dve", lambda e: e.tensor_tensor(out=pr[:, 1:2], in0=lv[:, 2:3], in1=lv[:, 3:4], op=ALU.mult), r=[lv, pr], w=[pr])
        onesf = k.sb("onesf", [64, 128], F32)
        k.op("dve", lambda e: e.memset(onesf[:], 1.0), w=[onesf])
        psl = self.psb[6]
        k.op("pe", lambda e: e.matmul(out=psl[:, 0:2], lhsT=onesf[:], rhs=pr[:], start=True, stop=True), r=[onesf, pr], w=[psl])
        ex = k.sb("lamex", [128, 2], F32)
        k.op("act", lambda e: e.activation(out=ex[:], in_=psl[:, 0:2], func=AF.Exp), r=[psl], w=[ex])
        neglam = k.sb("neglam", [128, 1], F32)
        k.op("dve", lambda e: e.tensor_tensor(out=neglam[:], in0=ex[:, 1:2], in1=ex[:, 0:1], op=ALU.subtract), r=[ex], w=[neglam])
        k.op("dve", lambda e: e.tensor_scalar(out=neglam[:], in0=neglam[:], scalar1=-lam_init, scalar2=None, op0=ALU.add), r=[neglam], w=[neglam])
        gsub = k.sb("gsub", [128, 1], F32)
        k.dma(gsub[:], self.da_subln.ap())
        k.op("dve", lambda e: e.tensor_scalar(out=gsub[:], in0=gsub[:], scalar1=1.0 - lam_init, scalar2=None, op0=ALU.mult), r=[gsub], w=[gsub])

        qTs = [k.sb("qT%d" % b, [128, T], BF16) for b in range(2)]
        kTs = [k.sb("kT%d" % b, [128, T], BF16) for b in range(2)]
        vhs = [k.sb("vh%d" % b, [128, 34, 128], BF16) for b in range(2)]
        pb = [k.sb("pexp%d" % b, [128, 512], BF16) for b in range(4)]
        accA = [k.sb("accA%d" % b, [128, 512], F32) for b in range(2)]
        accB = [k.sb("accB%d" % b, [128, 512], F32) for b in range(2)]
        ones128f = k.sb("ones128f", [128, 128], F32)
        k.op("dve", lambda e: e.memset(ones128f[:], 1.0), w=[ones128f])
        rb = k.sb("rb", [128, 512], F32)
        o1 = k.sb("o1", [128, 512], F32)
        o2 = k.sb("o2", [128, 512], F32)
        sq = k.sb("dsq", [128, 1, 512], BF16)
        rstd = k.sb("drstd", [128, 512], F32)
        ons = [k.sb("on%d" % b, [128, 512], BF16) for b in range(2)]
        blocks = [(0, LC, [0, 1])] + [(LC + 512 * b, 512, list(range(34))) for b in range(8)]
        nb = 0
        for hd in range(8):
            qT, kT, vh = qTs[hd % 2], kTs[hd % 2], vhs[hd % 2]
            k.dma(qT[:], self.QKs.ap()[hd * 128:(hd + 1) * 128, :])
            k.dma(kT[:], self.QKs.ap()[D + hd * 128:D + (hd + 1) * 128, :])
            k.dma(vh[:], self.Vs.ap()[:, hd * 128:(hd + 1) * 128].rearrange("(kt p) d -> p kt d", p=128))
            for (q0, N, kts) in blocks:
                un = 0
                for ki, kt in enumerate(kts):
                    for m in range(2):
                        num = self.psb[2 * m]
                        pss = self.psb[4 + un % 3]
                        P = pb[un % 4]
                        un += 1
                        k.op("pe", lambda e: e.matmul(out=pss[:, 0:N], lhsT=kT[m * 64:(m + 1) * 64, kt * 128:(kt + 1) * 128],
                                                      rhs=qT[m * 64:(m + 1) * 64, q0:q0 + N], start=True, stop=True), r=[kT, qT], w=[pss])
                        k.op("act", lambda e: e.activation(out=P[:, 0:N], in_=pss[:, 0:N], func=AF.Exp, scale=0.125), r=[pss], w=[P])
                        k.op("pe", lambda e: e.matmul(out=num[:, 0:N], lhsT=vh[:, kt, :], rhs=P[:, 0:N], start=(ki == 0), stop=(ki == len(kts) - 1)), r=[vh, P], w=[num])
                        eng_, acc_ = ("dve", accA[m]) if ki % 2 == 0 else ("pool", accB[m])
                        if ki < 2:
                            k.op(eng_, lambda e: e.tensor_copy(out=acc_[:, 0:N], in_=P[:, 0:N]), r=[P], w=[acc_])
                        else:
                            k.op(eng_, lambda e: e.tensor_tensor(out=acc_[:, 0:N], in0=acc_[:, 0:N], in1=P[:, 0:N], op=ALU.add), r=[P, acc_], w=[acc_])
                for m in range(2):
                    den = self.psb[2 * m + 1]
                    k.op("dve", lambda e: e.tensor_tensor(out=accA[m][:, 0:N], in0=accA[m][:, 0:N], in1=accB[m][:, 0:N], op=ALU.add), r=[accA[m], accB[m]], w=[accA[m]])
                    k.op("pe", lambda e: e.matmul(out=den[:, 0:N], lhsT=ones128f[:], rhs=accA[m][:, 0:N], start=True, stop=True), r=[ones128f, accA[m]], w=[den])
                n0, d0, n1, d1 = self.psb[0], self.psb[1], self.psb[2], self.psb[3]
                k.op("dve", lambda e: e.reciprocal(out=rb[:, 0:N], in_=d0[:, 0:N]), r=[d0], w=[rb])
                k.op("dve", lambda e: e.tensor_tensor(out=o1[:, 0:N], in0=n0[:, 0:N], in1=rb[:, 0:N], op=ALU.mult), r=[n0, rb], w=[o1])
                k.op("dve", lambda e: e.reciprocal(out=rb[:, 0:N], in_=d1[:, 0:N]), r=[d1], w=[rb])
                k.op("dve", lambda e: e.tensor_tensor(out=o2[:, 0:N], in0=n1[:, 0:N], in1=rb[:, 0:N], op=ALU.mult), r=[n1, rb], w=[o2])
                k.op("dve", lambda e: e.scalar_tensor_tensor(out=o1[:, 0:N], in0=o2[:, 0:N], scalar=neglam[:, 0:1], in1=o1[:, 0:N], op0=ALU.mult, op1=ALU.add), r=[o1, o2, neglam], w=[o1])
                k.op("act", lambda e: e.activation(out=sq[:, 0, 0:N], in_=o1[:, 0:N], func=AF.Square), r=[o1], w=[sq])
                self.rstd_from(sq, 1, N, rstd, self.psb[1], 128)
                on = ons[nb % 2]
                nb += 1
                k.op("dve", lambda e: e.scalar_tensor_tensor(out=on[:, 0:N], in0=o1[:, 0:N], scalar=gsub[:, 0:1], in1=rstd[:, 0:N], op0=ALU.mult, op1=ALU.mult), r=[o1, gsub, rstd], w=[on])
                k.dma(self.AO.ap()[hd * 128:(hd + 1) * 128, q0:q0 + N], on[:, 0:N], w=[("AO", hd, q0)])
        k.pop()

    def na_attn(self, i):
        k = self.k
        k.push()
        idf = k.sb("idf", [128, 128], F32)
        ident = k.sb("ident", [128, 128], BF16)
        k.dma(idf[:], self.c_ident.ap())
        k.op("dve", lambda e: e.tensor_copy(out=ident[:], in_=idf[:]), r=[idf], w=[ident])
        mI = k.sb("mI", [128, 1408], F32)
        mF = k.sb("mF", [128, 1408], F32)
        k.dma(mI[:], self.c_maskI.ap())
        k.dma(mF[:], self.c_maskF.ap())
        qTs = [k.sb("nqT%d" % b, [64, T], BF16) for b in range(2)]
        kTs = [k.sb("nkT%d" % b, [64, T], BF16) for b in range(2)]
        vhs = [k.sb("nvh%d" % b, [128, 34, 64], BF16) for b in range(2)]
        Gs = [k.sb("nG%d" % b, [128, 1408], F32) for b in range(2)]
        TIs = [k.sb("nTI%d" % b, [128, 1408], BF16) for b in range(2)]
        TFs = [k.sb("nTF%d" % b, [128, 1408], BF16) for b in range(2)]
        pb = [k.sb("npexp%d" % b, [128, 512], BF16) for b in range(4)]
        rb = k.sb("nrb", [64, 512], F32)
        sbias = [k.sb("nsb%d" % b, [128, 512], F32) for b in range(3)]
        nacc = k.sb("nacc", [128, 512], F32)
        ones128f = k.sb("nones128f", [128, 128], F32)
        k.op("dve", lambda e: e.memset(ones128f[:], 1.0), w=[ones128f])
        ons = [k.sb("non%d" % b, [64, 512], BF16) for b in range(2)]
        def lat_keys(qr0, nr, krs, tab):
            return [(0, None, 0, 0), (1, None, 0, 0)] + [(2 + kr // 2, tab, 10 - (kr - qr0), nr) for kr in krs]
        blocks = [(0, LC, [(0, None, 0, 0), (1, None, 0, 0)])]
        blocks.append((LC, 4 * 64, lat_keys(0, 4, [0, 2, 4, 6], "F")))
        for qr0 in range(4, 60, 8):
            blocks.append((LC + qr0 * 64, 8 * 64, lat_keys(qr0, 8, list(range(qr0 - 4, qr0 + 12, 2)), "I")))
        blocks.append((LC + 60 * 64, 64, lat_keys(60, 1, [56, 58, 60, 62], "I")))
        blocks.append((LC + 61 * 64, 3 * 64, lat_keys(61, 3, [56, 58, 60, 62], "F")))
        nb = 0
        for hd in range(16):
            b2 = hd % 2
            qT, kT, vh, G, TI, TF = qTs[b2], kTs[b2], vhs[b2], Gs[b2], TIs[b2], TFs[b2]
            k.dma(qT[:], self.QKs.ap()[hd * 64:(hd + 1) * 64, :])
            k.dma(kT[:], self.QKs.ap()[D + hd * 64:D + (hd + 1) * 64, :])
            k.dma(vh[:], self.Vs.ap()[:, hd * 64:(hd + 1) * 64].rearrange("(kt p) d -> p kt d", p=128))
            k.dma(G[:], self.na_rpbg.ap()[hd])
            k.op("dve", lambda e: e.tensor_tensor(out=TI[:], in0=G[:], in1=mI[:], op=ALU.add), r=[G, mI], w=[TI])
            k.op("pool", lambda e: e.tensor_tensor(out=TF[:], in0=G[:], in1=mF[:], op=ALU.add), r=[G, mF], w=[TF])
            for (q0, N, keys) in blocks:
                num, den = self.psb[0], self.psb[1]
                for ki, (kt, tab, jj0, nr) in enumerate(keys):
                    pss = self.psb[2 + ki % 4]
                    P = pb[ki % 4]
                    k.op("pe", lambda e: e.matmul(out=pss[:, 0:N], lhsT=kT[:, kt * 128:(kt + 1) * 128], rhs=qT[:, q0:q0 + N],
                                                  start=True, stop=True), r=[kT, qT], w=[pss])
                    if tab is not None:
                        tb = TI if tab == "I" else TF
                        sbb = sbias[ki % 3]
                        k.op("dve", lambda e: e.tensor_tensor(out=sbb[:, 0:N], in0=pss[:, 0:N], in1=tb[:, jj0 * 64:(jj0 + nr) * 64], op=ALU.add), r=[pss, tb], w=[sbb])
                        k.op("act", lambda e: e.activation(out=P[:, 0:N], in_=sbb[:, 0:N], func=AF.Exp), r=[sbb], w=[P])
                    else:
                        k.op("act", lambda e: e.activation(out=P[:, 0:N], in_=pss[:, 0:N], func=AF.Exp), r=[pss], w=[P])
                    k.op("pe", lambda e: e.matmul(out=num[0:64, 0:N], lhsT=vh[:, kt, :], rhs=P[:, 0:N], start=(ki == 0), stop=(ki == len(keys) - 1)), r=[vh, P], w=[num])
                    if ki < 1:
                        k.op("pool", lambda e: e.tensor_copy(out=nacc[:, 0:N], in_=P[:, 0:N]), r=[P], w=[nacc])
                    else:
                        k.op("pool", lambda e: e.tensor_tensor(out=nacc[:, 0:N], in0=nacc[:, 0:N], in1=P[:, 0:N], op=ALU.add), r=[P, nacc], w=[nacc])
                k.op("pe", lambda e: e.matmul(out=den[0:64, 0:N], lhsT=ones128f[:, 0:64], rhs=nacc[:, 0:N], start=True, stop=True), r=[ones128f, nacc], w=[den])
                on = ons[nb % 2]
                nb += 1
                k.op("dve", lambda e: e.reciprocal(out=rb[:, 0:N], in_=den[0:64, 0:N]), r=[den], w=[rb])
                k.op("dve", lambda e: e.tensor_tensor(out=on[:, 0:N], in0=num[0:64, 0:N], in1=rb[:, 0:N], op=ALU.mult), r=[num, rb], w=[on])
                k.dma(self.AO.ap()[hd * 64:(hd + 1) * 64, q0:q0 + N], on[:, 0:N], w=[("AO", hd, q0)])
        k.pop()


    def s5_core(self, i):
        k = self.k
        L = 256
        NTL = T // L
        k.push()
        TT = lambda e, o, a, b, op: e.tensor_tensor(out=o, in0=a, in1=b, op=op)
        wg = k.sb("wglu", [128, NCH, D], BF16)
        idf = k.sb("idf", [128, 128], F32)
        k.dma(idf[:], self.c_ident.ap())
        prm = {}
        for nm, src, shp in (("are", self.s5_are, [128, 2, 32]), ("aim", self.s5_aim, [128, 2, 32]), ("ldt", self.s5_ldt, [128, 2, 32]),
                                                          ("d", self.s5_d, [128, NCH]), ("bglu", self.s5_bglu, [128, NCH])):
            t_ = k.sb("s5" + nm, shp, F32)
            k.dma(t_[:], src.ap())
            prm[nm] = t_
        for nm in ("bre", "bim", "cre", "cim"):
            prm[nm] = k.sb("s5" + nm, [128, 32, 16], F32)
        hpi = k.sb("hpi", [128, 1], F32)
        k.op("dve", lambda e: e.memset(hpi[:], math.pi / 2), w=[hpi])
        sm = {nm: k.sb("s5" + nm, [128, 32], F32) for nm in ("dt", "rho", "th", "c", "s", "cc", "ss", "nr", "ni", "inv", "gr", "gi", "t0", "t1")}
        bbr = k.sb("bbr", [128, 32, 16], F32)
        bbi = k.sb("bbi", [128, 32, 16], F32)
        bt = k.sb("bt", [128, 32, 16], F32)
        ZA = k.sb("ZA", [128, 32, 128], F32)
        ZB = k.sb("ZB", [128, 32, 128], F32)
        stage = [Z_[:, 0:16, :].rearrange("p (a b) n -> p a (b n)", b=2) for Z_ in (ZA, ZB)]
        self._stg = 0
        self.wload(wg, self.s5_wglu.ap(), NCH, D, stage=stage)
        WBr = k.sb("WBr", [128, 32, 128], BF16)
        WBi = k.sb("WBi", [128, 32, 128], BF16)
        ZCr = k.sb("ZCr", [128, 32, 128], BF16)
        ZCi = k.sb("ZCi", [128, 32, 128], BF16)
        k.op("pool", lambda e: e.memset(ZCr[:], 0.0), w=[ZCr])
        k.op("pool", lambda e: e.memset(ZCi[:], 0.0), w=[ZCi])
        Tc = k.sb("Tc", [128, 32, L], F32)
        Ts = k.sb("Ts", [128, 32, L], F32)
        car = k.sb("car", [128, 32], F32)
        cai = k.sb("cai", [128, 32], F32)
        hts = [k.sb("s5ht%d" % b, [128, NCH, L], BF16) for b in range(1)] * 2
        wk = {nm: [k.sb("s5w%s%d" % (nm, b), [128, L], F32) for b in range(2)] for nm in ("a", "b", "gr", "gi", "rr", "ri", "hr", "hi")}
        hb = {nm: [k.sb("s5h%s%d" % (nm, b), [128, L], BF16) for b in range(2)] for nm in ("r", "i")}
        wcs = [k.sb("s5wc%d" % b, [128, L], F32) for b in range(2)]
        wds = [k.sb("s5wd%d" % b, [128, L], F32) for b in range(2)]
        ytot = k.sb("ytot", [128, NCH, L], F32)
        zb = k.sb("zb", [128, NCH, L], BF16)
        sg = [k.sb("s5sg%d" % b, [128, L], F32) for b in range(1)] * 2
        yo = [k.sb("s5yo%d" % b, [128, L], F32) for b in range(1)] * 2
        ps_x = [(self.psb[1], self.psb[2]), (self.psb[3], self.psb[4])]
        ps_tr = self.psb[0]

        def bc16(ap2):
            return ap2.unsqueeze(2).to_broadcast([128, 32, 16])

        def rev(t_, n):
            return bass.AP(t_, n - 1, [[t_[:].ap[0][0], 128], [-1, n]])

        nw = 0
        for dr in range(2):
            A = sm
            for nm, src in (("bre", self.s5_bre), ("bim", self.s5_bim), ("cre", self.s5_cre), ("cim", self.s5_cim)):
                k.dma(prm[nm][:], src.ap()[:, dr])
            k.op("act", lambda e: e.activation(out=A["dt"][:], in_=prm["ldt"][:, dr], func=AF.Exp), r=[prm["ldt"]], w=[A["dt"]])
            k.op("dve", lambda e: TT(e, A["t0"][:], prm["are"][:, dr], A["dt"][:], ALU.mult), r=[prm["are"], A["dt"]], w=[A["t0"]])
            k.op("act", lambda e: e.activation(out=A["rho"][:], in_=A["t0"][:], func=AF.Exp), r=[A["t0"]], w=[A["rho"]])
            k.op("dve", lambda e: TT(e, A["th"][:], prm["aim"][:, dr], A["dt"][:], ALU.mult), r=[prm["aim"], A["dt"]], w=[A["th"]])
            k.op("act", lambda e: e.activation(out=A["s"][:], in_=A["th"][:], func=AF.Sin, scale=1.0 / 16), r=[A["th"]], w=[A["s"]])
            k.op("act", lambda e: e.activation(out=A["c"][:], in_=A["th"][:], func=AF.Sin, scale=1.0 / 16, bias=hpi[:]), r=[A["th"], hpi], w=[A["c"]])
            for _ in range(4):
                k.op("dve", lambda e: TT(e, A["cc"][:], A["c"][:], A["c"][:], ALU.mult), r=[A["c"]], w=[A["cc"]])
                k.op("dve", lambda e: TT(e, A["ss"][:], A["s"][:], A["s"][:], ALU.mult), r=[A["s"]], w=[A["ss"]])
                k.op("dve", lambda e: e.scalar_tensor_tensor(out=A["s"][:], in0=A["c"][:], scalar=2.0, in1=A["s"][:], op0=ALU.mult, op1=ALU.mult), r=[A["c"], A["s"]], w=[A["s"]])
                k.op("dve", lambda e: TT(e, A["c"][:], A["cc"][:], A["ss"][:], ALU.subtract), r=[A["cc"], A["ss"]], w=[A["c"]])
            k.op("dve", lambda e: TT(e, A["nr"][:], A["rho"][:], A["c"][:], ALU.mult), r=[A["rho"], A["c"]], w=[A["nr"]])
            k.op("dve", lambda e: e.tensor_scalar(out=A["nr"][:], in0=A["nr"][:], scalar1=-1.0, scalar2=None, op0=ALU.add), r=[A["nr"]], w=[A["nr"]])
            k.op("dve", lambda e: TT(e, A["ni"][:], A["rho"][:], A["s"][:], ALU.mult), r=[A["rho"], A["s"]], w=[A["ni"]])
            k.op("dve", lambda e: TT(e, A["t0"][:], prm["are"][:, dr], prm["are"][:, dr], ALU.mult), r=[prm["are"]], w=[A["t0"]])
            k.op("dve", lambda e: TT(e, A["t1"][:], prm["aim"][:, dr], prm["aim"][:, dr], ALU.mult), r=[prm["aim"]], w=[A["t1"]])
            k.op("dve", lambda e: TT(e, A["inv"][:], A["t0"][:], A["t1"][:], ALU.add), r=[A["t0"], A["t1"]], w=[A["inv"]])
            k.op("dve", lambda e: e.reciprocal(out=A["inv"][:], in_=A["inv"][:]), r=[A["inv"]], w=[A["inv"]])
            k.op("dve", lambda e: TT(e, A["t0"][:], A["nr"][:], prm["are"][:, dr], ALU.mult), r=[A["nr"], prm["are"]], w=[A["t0"]])
            k.op("dve", lambda e: TT(e, A["t1"][:], A["ni"][:], prm["aim"][:, dr], ALU.mult), r=[A["ni"], prm["aim"]], w=[A["t1"]])
            k.op("dve", lambda e: TT(e, A["gr"][:], A["t0"][:], A["t1"][:], ALU.add), r=[A["t0"], A["t1"]], w=[A["gr"]])
            k.op("dve", lambda e: TT(e, A["gr"][:], A["gr"][:], A["inv"][:], ALU.mult), r=[A["gr"], A["inv"]], w=[A["gr"]])
            k.op("dve", lambda e: TT(e, A["t0"][:], A["ni"][:], prm["are"][:, dr], ALU.mult), r=[A["ni"], prm["are"]], w=[A["t0"]])
            k.op("dve", lambda e: TT(e, A["t1"][:], A["nr"][:], prm["aim"][:, dr], ALU.mult), r=[A["nr"], prm["aim"]], w=[A["t1"]])
            k.op("dve", lambda e: TT(e, A["gi"][:], A["t0"][:], A["t1"][:], ALU.subtract), r=[A["t0"], A["t1"]], w=[A["gi"]])
            k.op("dve", lambda e: TT(e, A["gi"][:], A["gi"][:], A["inv"][:], ALU.mult), r=[A["gi"], A["inv"]], w=[A["gi"]])
            k.op("dve", lambda e: TT(e, bbr[:], prm["bre"][:], bc16(A["gr"][:]), ALU.mult), r=[prm["bre"], A["gr"]], w=[bbr])
            k.op("dve", lambda e: TT(e, bt[:], prm["bim"][:], bc16(A["gi"][:]), ALU.mult), r=[prm["bim"], A["gi"]], w=[bt])
            k.op("dve", lambda e: TT(e, bbr[:], bbr[:], bt[:], ALU.subtract), r=[bbr, bt], w=[bbr])
            k.op("dve", lambda e: TT(e, bbi[:], prm["bim"][:], bc16(A["gr"][:]), ALU.mult), r=[prm["bim"], A["gr"]], w=[bbi])
            k.op("dve", lambda e: TT(e, bt[:], prm["bre"][:], bc16(A["gi"][:]), ALU.mult), r=[prm["bre"], A["gi"]], w=[bt])
            k.op("dve", lambda e: TT(e, bbi[:], bbi[:], bt[:], ALU.add), r=[bbi, bt], w=[bbi])
            k.op("pool", lambda e: e.memset(ZA[:], 0.0), w=[ZA])
            k.op("pool", lambda e: e.memset(ZB[:], 0.0), w=[ZB])
            for gl in range(2):
                for r4 in range(4):
                    c0 = (2 * r4 + gl) * 16
                    pr_ = slice(gl * 64, (gl + 1) * 64)
                    k.op("dve", lambda e: e.tensor_copy(out=ZA[pr_, r4::4, c0:c0 + 16], in_=bbr[pr_, r4::4, :]), r=[bbr], w=[ZA])
                    k.op("dve", lambda e: e.tensor_copy(out=ZB[pr_, r4::4, c0:c0 + 16], in_=bbi[pr_, r4::4, :]), r=[bbi], w=[ZB])
                    k.op("dve", lambda e: e.tensor_copy(out=ZCr[pr_, r4::4, c0:c0 + 16], in_=prm["cre"][pr_, r4::4, :]), r=[prm["cre"]], w=[ZCr])
                    k.op("dve", lambda e: e.tensor_scalar(out=ZCi[pr_, r4::4, c0:c0 + 16], in0=prm["cim"][pr_, r4::4, :], scalar1=-1.0, scalar2=None, op0=ALU.mult),
                         r=[prm["cim"]], w=[ZCi])
            for st in range(32):
                for Z, W in ((ZA, WBr), (ZB, WBi)):
                    k.op("pe", lambda e: e.transpose(out=ps_tr[:, 0:128], in_=Z[:, st, :], identity=idf[:]), r=[Z, idf], w=[ps_tr])
                    k.op("act", lambda e: e.activation(out=W[:, st, :], in_=ps_tr[:, 0:128], func=AF.Identity), r=[ps_tr], w=[W])
            k.op("dve", lambda e: e.tensor_copy(out=Tc[:, :, 0], in_=A["c"][:]), r=[A["c"]], w=[Tc])
            k.op("dve", lambda e: e.tensor_copy(out=Ts[:, :, 0], in_=A["s"][:]), r=[A["s"]], w=[Ts])
            m = 1
            while m < L:
                pc = Tc[:, :, m - 1:m].to_broadcast([128, 32, m])
                pS = Ts[:, :, m - 1:m].to_broadcast([128, 32, m])
                k.op("dve", lambda e: TT(e, ZA[:, :, 0:m], Tc[:, :, 0:m], pc, ALU.mult), r=[Tc, WBr, WBi], w=[ZA])
                k.op("dve", lambda e: TT(e, ZB[:, :, 0:m], Ts[:, :, 0:m], pS, ALU.mult), r=[Ts, Tc], w=[ZB])
                k.op("dve", lambda e: TT(e, Tc[:, :, m:2 * m], ZA[:, :, 0:m], ZB[:, :, 0:m], ALU.subtract), r=[ZA, ZB], w=[Tc])
                k.op("dve", lambda e: TT(e, ZA[:, :, 0:m], Tc[:, :, 0:m], pS, ALU.mult), r=[Tc, Ts], w=[ZA])
                k.op("dve", lambda e: TT(e, ZB[:, :, 0:m], Ts[:, :, 0:m], pc, ALU.mult), r=[Ts, Tc], w=[ZB])
                k.op("dve", lambda e: TT(e, Ts[:, :, m:2 * m], ZA[:, :, 0:m], ZB[:, :, 0:m], ALU.add), r=[ZA, ZB], w=[Ts])
                m *= 2
            if dr == 1:
                H2 = L // 2
                for Tt in (Tc, Ts):
                    up = bass.AP(Tt, L - 1, [[32 * L, 128], [L, 32], [-1, H2]])
                    lo = bass.AP(Tt, H2 - 1, [[32 * L, 128], [L, 32], [-1, H2]])
                    k.op("dve", lambda e: e.tensor_copy(out=ZA[:, :, 0:H2], in_=up), r=[Tt], w=[ZA])
                    k.op("dve", lambda e: e.tensor_copy(out=Tt[:, :, H2:L], in_=lo), r=[Tt, ZA], w=[Tt])
                    k.op("dve", lambda e: e.tensor_copy(out=Tt[:, :, 0:H2], in_=ZA[:, :, 0:H2]), r=[ZA], w=[Tt])
            k.op("dve", lambda e: e.memset(car[:], 0.0), w=[car])
            k.op("dve", lambda e: e.memset(cai[:], 0.0), w=[cai])
            order = list(range(NTL)) if dr == 0 else [0] + list(range(NTL - 1, 0, -1))
            for oi, tl in enumerate(order):
                t0 = tl * L
                ht = hts[oi % 2]
                k.dma(ht[:], self.Hs.ap()[:, t0:t0 + L].rearrange("(c p) n -> p c n", p=128))
                if dr == 1:
                    k.dma(ytot[:], self.Ys.ap()[:, t0:t0 + L].rearrange("(c p) n -> p c n", p=128), w=[(ytot.name, c_) for c_ in range(NCH)])
                for c in range(NCH):
                    py = self.psb[5 + c % 2]
                    for s4 in range(4):
                        st = 4 * c + s4
                        b = nw % 2
                        nw += 1
                        pxr, pxi = ps_x[b]
                        k.op("pe", lambda e: e.matmul(out=pxr[:, 0:L], lhsT=WBr[:, st, :], rhs=ht[:, c, :], start=True, stop=True), r=[WBr, ht], w=[pxr])
                        k.op("pe", lambda e: e.matmul(out=pxi[:, 0:L], lhsT=WBi[:, st, :], rhs=ht[:, c, :], start=True, stop=True), r=[WBi, ht], w=[pxi])
                        wa, wb_, gr, gi, rr, ri, hr, hi = (wk[n_][b] for n_ in ("a", "b", "gr", "gi", "rr", "ri", "hr", "hi"))
                        wc, wd = wcs[b], wds[b]
                        tc, ts = Tc[:, st, :], Ts[:, st, :]
                        if dr == 0:
                            sc_out = lambda t_: t_[:]
                            last = L - 1
                        else:
                            sc_out = lambda t_: bass.AP(t_, L - 1, [[L, 128], [-1, L]])
                            last = 0
                        k.op("dve", lambda e: TT(e, wa[:], pxr[:, 0:L], tc, ALU.mult), r=[pxr, Tc], w=[wa])
                        k.op("dve", lambda e: TT(e, wb_[:], pxi[:, 0:L], ts, ALU.mult), r=[pxi, Ts], w=[wb_])
                        k.op("dve", lambda e: TT(e, gr[:], wa[:], wb_[:], ALU.add), r=[wa, wb_], w=[gr])
                        k.op("dve", lambda e: TT(e, wa[:], pxi[:, 0:L], tc, ALU.mult), r=[pxi, Tc], w=[wa])
                        k.op("dve", lambda e: TT(e, wb_[:], pxr[:, 0:L], ts, ALU.mult), r=[pxr, Ts], w=[wb_])
                        k.op("dve", lambda e: TT(e, gi[:], wa[:], wb_[:], ALU.subtract), r=[wa, wb_], w=[gi])
                        rho_b = A["rho"][:, st:st + 1].to_broadcast([128, L])
                        k.op("dve", lambda e: e.tensor_tensor_scan(out=sc_out(rr), data0=rho_b, data1=sc_out(gr), initial=car[:, st:st + 1], op0=ALU.mult, op1=ALU.add),
                             r=[gr, A["rho"], car], w=[rr])
                        k.op("dve", lambda e: e.tensor_tensor_scan(out=sc_out(ri), data0=rho_b, data1=sc_out(gi), initial=cai[:, st:st + 1], op0=ALU.mult, op1=ALU.add),
                             r=[gi, A["rho"], cai], w=[ri])
                        k.op("pool", lambda e: TT(e, wc[:], rr[:], tc, ALU.mult), r=[rr, Tc], w=[wc])
                        k.op("pool", lambda e: TT(e, wd[:], ri[:], ts, ALU.mult), r=[ri, Ts], w=[wd])
                        k.op("pool", lambda e: TT(e, hr[:], wc[:], wd[:], ALU.subtract), r=[wc, wd], w=[hr])
                        k.op("pool", lambda e: TT(e, wc[:], rr[:], ts, ALU.mult), r=[rr, Ts], w=[wc])
                        k.op("pool", lambda e: TT(e, wd[:], ri[:], tc, ALU.mult), r=[ri, Tc], w=[wd])
                        k.op("pool", lambda e: TT(e, hi[:], wc[:], wd[:], ALU.add), r=[wc, wd], w=[hi])
                        k.op("pool", lambda e: e.tensor_copy(out=car[:, st:st + 1], in_=hr[:, last:last + 1]), r=[hr], w=[car])
                        k.op("pool", lambda e: e.tensor_copy(out=cai[:, st:st + 1], in_=hi[:, last:last + 1]), r=[hi], w=[cai])
                        hbr, hbi = hb["r"][b], hb["i"][b]
                        k.op("act", lambda e: e.activation(out=hbr[:], in_=hr[:], func=AF.Identity), r=[hr], w=[hbr])
                        k.op("act", lambda e: e.activation(out=hbi[:], in_=hi[:], func=AF.Identity), r=[hi], w=[hbi])
                        k.op("pe", lambda e: e.matmul(out=py[:, 0:L], lhsT=ZCr[:, st, :], rhs=hbr[:], start=(s4 == 0), stop=False), r=[ZCr, hbr], w=[py])
                        k.op("pe", lambda e: e.matmul(out=py[:, 0:L], lhsT=ZCi[:, st, :], rhs=hbi[:], start=False, stop=(s4 == 3)), r=[ZCi, hbi], w=[py])
                    if dr == 0:
                        y = yo[c % 2]
                        k.op("dve", lambda e: e.tensor_copy(out=y[:], in_=py[:, 0:L]), r=[py], w=[y])
                        k.dma(self.Ys.ap()[c * 128:(c + 1) * 128, t0:t0 + L], y[:], w=[("Ysp", c, t0)])
                    else:
                        k.op("dve", lambda e: TT(e, ytot[:, c, :], py[:, 0:L], ytot[:, c, :], ALU.add), r=[py, (ytot.name, c)], w=[(ytot.name, c)])
                        k.op("dve", lambda e: e.scalar_tensor_tensor(out=ytot[:, c, :], in0=ht[:, c, :], scalar=prm["d"][:, c:c + 1], in1=ytot[:, c, :], op0=ALU.mult, op1=ALU.add),
                             r=[ht, prm["d"], (ytot.name, c)], w=[(ytot.name, c)])
                        k.op("act", lambda e: e.activation(out=ytot[:, c, :], in_=ytot[:, c, :], func=AF.Gelu), r=[(ytot.name, c)], w=[(ytot.name, c)])
                        k.op("act", lambda e: e.activation(out=zb[:, c, :], in_=ytot[:, c, :], func=AF.Identity), r=[(ytot.name, c)], w=[(zb.name, c)])
                if dr == 1:
                    for c2 in range(NCH):
                        pu = self.psb[5 + c2 % 2]
                        for c in range(NCH):
                            k.op("pe", lambda e: e.matmul(out=pu[:, 0:L], lhsT=wg[:, c, c2 * 128:(c2 + 1) * 128], rhs=zb[:, c, :], start=(c == 0), stop=(c == NCH - 1)),
                                 r=[wg, (zb.name, c)], w=[pu])
                        sg_ = sg[c2 % 2]
                        y = yo[c2 % 2]
                        k.op("act", lambda e: e.activation(out=sg_[:], in_=pu[:, 0:L], func=AF.Sigmoid, bias=prm["bglu"][:, c2:c2 + 1], scale=1.0), r=[pu, prm["bglu"]], w=[sg_])
                        k.op("dve", lambda e: TT(e, y[:], ytot[:, c2, :], sg_[:], ALU.mult), r=[(ytot.name, c2), sg_], w=[y])
                        k.dma(self.Ys.ap()[c2 * 128:(c2 + 1) * 128, t0:t0 + L], y[:], w=[("Ysf", c2, t0)])
        k.pop()

    def hg_proj(self, i):
        k = self.k
        NT = 512
        k.push()
        w = k.sb("wqig", [128, NCH, 3 * D], BF16)
        wf = k.sb("wf", [128, NCH, 2 * D], BF16)
        stage = [k.sb("wstg%d" % b, [128, 8, 256], F32) for b in range(2)]
        self._stg = 0
        self.wload(w, self.hg_wqig.ap(), NCH, 3 * D, stage=stage)
        for dr in range(2):
            for c in range(0, D, 256):
                st = stage[self._stg % 2]
                self._stg += 1
                k.dma(st[:, :, :], self.hg_wf.ap()[dr][:, c:c + 256].rearrange("(k p) n -> p k n", p=128))
                k.op("pool", lambda e: e.tensor_copy(out=wf[:, :, dr * D + c:dr * D + c + 256], in_=st[:, :, :]), r=[st], w=[wf])
        bfm = k.sb("bfm", [128, 2, NCH], F32)
        k.dma(bfm[:], self.hg_bf.ap())
        lg = k.sb("lblg", [128, DEPTH, NCH], F32)
        k.dma(lg[:], self.hg_lb.ap())
        k.op("act", lambda e: e.activation(out=lg[:], in_=lg[:], func=AF.Exp), r=[lg], w=[lg])
        ssum = k.sb("lbsum", [128, NCH], F32)
        lb = k.sb("lb", [128, NCH], F32)
        oml = k.sb("oml", [128, NCH], F32)
        k.op("dve", lambda e: e.tensor_tensor(out=ssum[:], in0=lg[:, 0], in1=lg[:, 1], op=ALU.add), r=[lg], w=[ssum])
        for l in (2, 3):
            k.op("dve", lambda e: e.tensor_tensor(out=ssum[:], in0=ssum[:], in1=lg[:, l], op=ALU.add), r=[lg, ssum], w=[ssum])
        k.op("dve", lambda e: e.reciprocal(out=ssum[:], in_=ssum[:]), r=[ssum], w=[ssum])
        k.op("dve", lambda e: e.memset(lb[:], 0.0), w=[lb])
        for l in range(1, i + 1):
            k.op("dve", lambda e: e.tensor_tensor(out=lb[:], in0=lb[:], in1=lg[:, l], op=ALU.add), r=[lg, lb], w=[lb])
        k.op("dve", lambda e: e.tensor_tensor(out=lb[:], in0=lb[:], in1=ssum[:], op=ALU.mult), r=[lb, ssum], w=[lb])
        k.op("dve", lambda e: e.tensor_scalar(out=oml[:], in0=lb[:], scalar1=-1.0, scalar2=1.0, op0=ALU.mult, op1=ALU.add), r=[lb], w=[oml])
        self.hg_lbt = None
        hts = [k.sb("ht%d" % b, [128, NCH, NT], BF16) for b in range(2)]
        qo = [k.sb("qo%d" % b, [128, NT], BF16) for b in range(2)]
        sg = [k.sb("sg%d" % b, [128, NT], F32) for b in range(2)]
        fo = [k.sb("fo%d" % b, [128, NT], F32) for b in range(2)]
        vo = [k.sb("vo%d" % b, [128, 512], BF16) for b in range(2)]
        nv = 0
        for ti, (t0, N, v) in enumerate(self.token_tiles(True, NT)):
            ht = hts[ti % 2]
            k.dma(ht[:, :, 0:N], self.Hs.ap()[:, t0:t0 + N].rearrange("(c p) n -> p c n", p=128))
            for cc in range(16):
                b = cc % 2
                ps = self.psb[1 + b]
                col0 = cc * 128 if cc < 8 else 2 * D + (cc - 8) * 128
                for c in range(NCH):
                    k.op("pe", lambda e: e.matmul(out=ps[:, 0:N], lhsT=w[:, c, col0:col0 + 128], rhs=ht[:, c, 0:N],
                                                  start=(c == 0), stop=(c == NCH - 1)), r=[w, ht], w=[ps])
                fn = AF.Identity if cc < 8 else AF.Silu
                k.op("act", lambda e: e.activation(out=qo[b][:, 0:N], in_=ps[:, 0:N], func=fn), r=[ps], w=[qo[b]])
                k.dma(self.QKs.ap()[cc * 128:(cc + 1) * 128, t0:t0 + N], qo[b][:, 0:N], w=[("QKs", cc, t0)])
            for dr in range(2):
                for cc in range(8):
                    b = cc % 2
                    ps = self.psb[3 + b]
                    for c in range(NCH):
                        k.op("pe", lambda e: e.matmul(out=ps[:, 0:N], lhsT=wf[:, c, dr * D + cc * 128:dr * D + (cc + 1) * 128], rhs=ht[:, c, 0:N],
                                                      start=(c == 0), stop=(c == NCH - 1)), r=[wf, ht], w=[ps])
                    k.op("act", lambda e: e.activation(out=sg[b][:, 0:N], in_=ps[:, 0:N], func=AF.Sigmoid, bias=bfm[:, dr, cc:cc + 1], scale=1.0), r=[ps, bfm], w=[sg[b]])
                    k.op("dve", lambda e: e.tensor_scalar(out=fo[b][:, 0:N], in0=sg[b][:, 0:N], scalar1=oml[:, cc:cc + 1], scalar2=lb[:, cc:cc + 1],
                                                          op0=ALU.mult, op1=ALU.add), r=[sg[b], oml, lb], w=[fo[b]])
                    k.dma(self.Fs.ap()[dr, cc * 128:(cc + 1) * 128, t0:t0 + N], fo[b][:, 0:N], w=[("Fs", dr, cc, t0)])
            for sub in range(N // 128):
                for vb in range(2):
                    ps = self.psb[5 + vb]
                    for c in range(NCH):
                        k.op("pe", lambda e: e.matmul(out=ps[:, :], lhsT=ht[:, c, sub * 128:(sub + 1) * 128], rhs=w[:, c, D + vb * 512:D + (vb + 1) * 512],
                                                      start=(c == 0), stop=(c == NCH - 1)), r=[w, ht], w=[ps])
                    vv = vo[nv % 2]
                    nv += 1
                    k.op("act", lambda e: e.activation(out=vv[:], in_=ps[:, :], func=AF.Identity), r=[ps], w=[vv])
                    k.dma(self.Vs.ap()[t0 + sub * 128:t0 + (sub + 1) * 128, vb * 512:(vb + 1) * 512], vv[:], w=[("Vs", t0, sub, vb)])
        k.pop()

    def hg_core(self, i):
        k = self.k
        CH = 128
        NKT = T // CH
        k.push()
        idf = k.sb("idf", [128, 128], F32)
        ident = k.sb("ident", [128, 128], BF16)
        k.dma(idf[:], self.c_ident.ap())
        k.op("dve", lambda e: e.tensor_copy(out=ident[:], in_=idf[:]), r=[idf], w=[ident])
        masks = []
        for nm, src in (("triu", self.c_triu), ("tril", self.c_tril)):
            mk = k.sb(nm, [128, 128], F32)
            k.dma(mk[:], src.ap())
            masks.append(mk)
        gn = k.sb("gn", [128, 1], F32)
        k.dma(gn[:], self.hg_gn.ap())
        zeros = k.sb("zeros", [128, CH], F32)
        k.op("dve", lambda e: e.memset(zeros[:], 0.0), w=[zeros])
        qT = k.sb("hqT", [128, T], BF16)
        gT = k.sb("hgT", [128, T], BF16)
        vh = k.sb("hvh", [128, NKT, 128], BF16)
        f = k.sb("hf", [128, T], F32)
        P = k.sb("hP", [128, T], F32)
        kinv = k.sb("hkinv", [128, T], F32)
        qdec = k.sb("hqdec", [128, T], BF16)
        kinvb = k.sb("hkinvb", [128, T], BF16)
        Oacc = k.sb("hO", [128, S], F32)
        Sst = k.sb("hS", [128, 128], F32)
        Sbf = [k.sb("hSbf%d" % b, [128, 128], BF16) for b in range(2)]
        kdec = [k.sb("hkdec%d" % b, [128, CH], BF16) for b in range(2)]
        kdt = [k.sb("hkdt%d" % b, [128, CH], BF16) for b in range(2)]
        attm = [k.sb("hattm%d" % b, [128, CH], BF16) for b in range(2)]
        sq = k.sb("hsq", [128, 1, 512], BF16)
        rstd = k.sb("hrstd", [128, 512], F32)
        ons = [k.sb("hon%d" % b, [128, 512], BF16) for b in range(2)]
        ps_att, ps_o, ps_ds = self.psb[1], self.psb[2], self.psb[3]
        ps_tr = self.ps_bf[:, 0:128]
        nb = 0
        for hd in range(8):
            k.dma(qT[:], self.QKs.ap()[hd * 128:(hd + 1) * 128, :])
            k.dma(gT[:], self.QKs.ap()[D + hd * 128:D + (hd + 1) * 128, :])
            k.dma(vh[:], self.Vs.ap()[:, hd * 128:(hd + 1) * 128].rearrange("(kt p) d -> p kt d", p=128))
            for dr in range(2):
                k.dma(f[:], self.Fs.ap()[dr, hd * 128:(hd + 1) * 128, :])
                order = list(range(NKT)) if dr == 0 else [1, 0] + list(range(NKT - 1, 1, -1))
                for kt in range(NKT):
                    c0 = kt * CH
                    if dr == 0:
                        fa, pa = f[:, c0:c0 + CH], P[:, c0:c0 + CH]
                    else:
                        fa = bass.AP(f, c0 + CH - 1, [[T, 128], [-1, CH]])
                        pa = bass.AP(P, c0 + CH - 1, [[T, 128], [-1, CH]])
                    k.op("dve", lambda e: e.tensor_tensor_scan(out=pa, data0=fa, data1=zeros[:], initial=1.0, op0=ALU.mult, op1=ALU.add),
                         r=[f, zeros], w=[P])
                k.op("dve", lambda e: e.reciprocal(out=kinv[:], in_=P[:]), r=[P], w=[kinv])
                k.op("pool", lambda e: e.tensor_scalar(out=f[:], in0=f[:], scalar1=-1.0, scalar2=1.0, op0=ALU.mult, op1=ALU.add), r=[f], w=[f])
                k.op("dve", lambda e: e.tensor_tensor(out=kinv[:], in0=kinv[:], in1=f[:], op=ALU.mult), r=[kinv, f], w=[kinv])
                k.op("pool", lambda e: e.tensor_tensor(out=qdec[:], in0=qT[:], in1=P[:], op=ALU.mult), r=[qT, P], w=[qdec])
                k.op("act", lambda e: e.activation(out=kinvb[:], in_=kinv[:], func=AF.Identity), r=[kinv], w=[kinvb])
                k.op("dve", lambda e: e.memset(Sst[:], 0.0), w=[Sst])
                k.op("pool", lambda e: e.memset(Sbf[0][:], 0.0), w=[Sbf[0]])
                si = 0
                mask = masks[dr]
                for kt in order:
                    c0 = kt * CH
                    plast = P[:, c0 + CH - 1:c0 + CH] if dr == 0 else P[:, c0:c0 + 1]
                    Scur = Sbf[si % 2]
                    if kt >= 2:
                        l0 = c0 - LC
                        am = attm[nb % 2]
                        k.op("pe", lambda e: e.matmul(out=ps_att[:, 0:CH], lhsT=kinvb[:, c0:c0 + CH], rhs=qdec[:, c0:c0 + CH], start=True, stop=True),
                             r=[kinvb, qdec], w=[ps_att])
                        k.op("dve", lambda e: e.tensor_tensor(out=am[:], in0=ps_att[:, 0:CH], in1=mask[:], op=ALU.mult), r=[ps_att, mask], w=[am])
                        k.op("pe", lambda e: e.matmul(out=ps_o[:, 0:CH], lhsT=vh[:, kt, :], rhs=am[:], start=True, stop=False), r=[vh, am], w=[ps_o])
                        k.op("pe", lambda e: e.matmul(out=ps_o[:, 0:CH], lhsT=Scur[:], rhs=qdec[:, c0:c0 + CH], start=False, stop=True), r=[Scur, qdec], w=[ps_o])
                        if dr == 0:
                            k.op("dve", lambda e: e.tensor_copy(out=Oacc[:, l0:l0 + CH], in_=ps_o[:, 0:CH]), r=[ps_o], w=[Oacc])
                        else:
                            k.op("dve", lambda e: e.tensor_tensor(out=Oacc[:, l0:l0 + CH], in0=ps_o[:, 0:CH], in1=Oacc[:, l0:l0 + CH], op=ALU.add), r=[ps_o, Oacc], w=[Oacc])
                    kd, kt_ = kdec[nb % 2], kdt[nb % 2]
                    nb += 1
                    k.op("dve", lambda e: e.tensor_scalar(out=kd[:], in0=kinv[:, c0:c0 + CH], scalar1=plast, scalar2=None, op0=ALU.mult), r=[kinv, P], w=[kd])
                    k.op("pe", lambda e: e.transpose(out=ps_tr, in_=kd[:], identity=ident[:]), r=[kd, ident], w=["pstr"])
                    k.op("act", lambda e: e.activation(out=kt_[:], in_=ps_tr, func=AF.Identity), r=["pstr"], w=[kt_])
                    k.op("pe", lambda e: e.matmul(out=ps_ds[:, 0:128], lhsT=kt_[:], rhs=vh[:, kt, :], start=True, stop=True), r=[kt_, vh], w=[ps_ds])
                    k.op("dve", lambda e: e.tensor_scalar(out=Sst[:], in0=Sst[:], scalar1=plast, scalar2=None, op0=ALU.mult), r=[Sst, P], w=[Sst])
                    k.op("dve", lambda e: e.tensor_tensor(out=Sst[:], in0=ps_ds[:, 0:128], in1=Sst[:], op=ALU.add), r=[ps_ds, Sst], w=[Sst])
                    si += 1
                    k.op("act", lambda e: e.activation(out=Sbf[si % 2][:], in_=Sst[:], func=AF.Identity), r=[Sst], w=[Sbf[si % 2]])
            for qb in range(8):
                l0 = qb * 512
                k.op("act", lambda e: e.activation(out=sq[:, 0, :], in_=Oacc[:, l0:l0 + 512], func=AF.Square), r=[Oacc], w=[sq])
                self.rstd_from(sq, 1, 512, rstd, self.psb[6], 128)
                on = ons[qb % 2]
                k.op("dve", lambda e: e.scalar_tensor_tensor(out=rstd[:], in0=Oacc[:, l0:l0 + 512], scalar=gn[:, 0:1], in1=rstd[:], op0=ALU.mult, op1=ALU.mult), r=[Oacc, gn, rstd], w=[rstd])
                k.op("dve", lambda e: e.tensor_tensor(out=on[:], in0=rstd[:], in1=gT[:, LC + l0:LC + l0 + 512], op=ALU.mult), r=[rstd, gT], w=[on])
                k.dma(self.AO.ap()[hd * 128:(hd + 1) * 128, LC + l0:LC + l0 + 512], on[:], w=[("AO", hd, l0)])
        k.pop()

    def ffn(self, i, which, j, src, dst, with_ctx, final=False):
        k = self.k
        NT = 256
        k.push()
        w1 = k.sb("w1", [128, NCH, DFF], BF16)
        w3 = k.sb("w3", [128, NCH, DFF], BF16)
        w2 = k.sb("w2", [128, NFF, D], BF16)
        stage = [k.sb("wstg%d" % b, [128, 8, 256], F32) for b in range(2)]
        self._stg = 0
        self.wload(w1, self.w_ff1.ap()[i, which], NCH, DFF, stage=stage)
        self.wload(w3, self.w_ff3.ap()[i, which], NCH, DFF, stage=stage)
        self.wload(w2, self.w_ff2.ap()[i, which], NFF, D, stage=stage)
        xts = [k.sb("xt%d" % b, [128, NCH, NT], F32) for b in range(2)]
        sq = k.sb("sq", [128, NCH, NT], BF16)
        h = k.sb("h", [128, NCH, NT], BF16)
        g = k.sb("g", [128, NFF, NT], BF16)
        y = k.sb("y", [128, NCH, NT], F32)
        sl = [k.sb("sl%d" % b, [128, NT], BF16) for b in range(2)]
        rstd = k.sb("rstd", [128, NT], F32)
        psn = self.psb[0]
        tiles = self.token_tiles(with_ctx, NT)
        import os
        dbg = int(os.environ.get("FF_DBG", "0"))
        if dbg:
            tiles = tiles[:dbg]
        for ti, (t0, N, v) in enumerate(tiles):
            xt = xts[ti % 2]
            k.dma(xt[:, :, 0:N], src.ap()[:, t0:t0 + N].rearrange("(c p) n -> p c n", p=128), q="sp")
            step = int(os.environ.get("FF_STEP", "9"))
            if step >= 1:
                self.prenorm(xt, N, i, j, v, sq, y, h, rstd, psn)
            for f in range(NFF if step >= 2 else 0):
                pa = self.psb[1 + (f % 2)]
                pb = self.psb[3 + (f % 2)]
                for c in range(NCH):
                    k.op("pe", lambda e: e.matmul(out=pa[:, 0:N], lhsT=w1[:, c, f * 128:(f + 1) * 128], rhs=h[:, c, 0:N],
                                                  start=(c == 0), stop=(c == NCH - 1)), r=[w1, h], w=[pa])
                for c in range(NCH):
                    k.op("pe", lambda e: e.matmul(out=pb[:, 0:N], lhsT=w3[:, c, f * 128:(f + 1) * 128], rhs=h[:, c, 0:N],
                                                  start=(c == 0), stop=(c == NCH - 1)), r=[w3, h], w=[pb])
                s = sl[f % 2]
                sub = int(os.environ.get("FF_SUB", "9"))
                if sub >= 2:
                    k.op("act", lambda e: e.activation(out=s[:, 0:N], in_=pa[:, 0:N], func=AF.Silu), r=[pa], w=[s])
                if sub >= 3:
                    k.op("dve", lambda e: e.tensor_tensor(out=g[:, f, 0:N], in0=pb[:, 0:N], in1=s[:, 0:N], op=ALU.mult), r=[s, pb], w=[(g.name, f)])
            for c in range(NCH if step >= 3 else 0):
                py = self.psb[5 + (c % 2)]
                for f in range(NFF):
                    k.op("pe", lambda e: e.matmul(out=py[:, 0:N], lhsT=w2[:, f, c * 128:(c + 1) * 128], rhs=g[:, f, 0:N],
                                                  start=(f == 0), stop=(f == NFF - 1)), r=[w2, (g.name, f)], w=[py])
                k.op("dve", lambda e: e.tensor_copy(out=y[:, c, 0:N], in_=py[:, 0:N]), r=[py], w=[y])
                k.op("act", lambda e: e.activation(out=sq[:, c, 0:N], in_=y[:, c, 0:N], func=AF.Square), r=[y], w=[sq])
            if step >= 4:
                self.postres(y, xt, N, i, j, v, sq, rstd, psn)
            if final:
                k.dma(self.out.ap()[:, t0 - LC:t0 - LC + N].rearrange("(c p) n -> p c n", p=128), xt[:, :, 0:N], q="sp")
            else:
                k.dma(dst.ap()[:, t0:t0 + N].rearrange("(c p) n -> p c n", p=128), xt[:, :, 0:N], q="sp",
                      w=[(dst.name, t0)])
        k.pop()


def fm(v):
    v = np.asarray(v, np.float32)
    lead = v.shape[:-1]
    a = v.reshape(lead + (NCH, 128))
    a = np.moveaxis(a, -1, 0)
    return np.ascontiguousarray(a)


_CONST = {}


def consts():
    if _CONST:
        return _CONST
    nfreq = 16
    inv = 10000.0 ** (-np.arange(nfreq, dtype=np.float32) / nfreq)
    t = np.arange(S)
    row = (t // 64).astype(np.float32)
    col = (t % 64).astype(np.float32)
    ang = np.concatenate([row[:, None] * inv, col[:, None] * inv], axis=-1).astype(np.float32)
    p = np.arange(128)
    pair = (p % 64) // 2
    C = np.cos(ang)[:, pair].T
    Sn = np.sin(ang)[:, pair].T
    sign = np.where(p % 2 == 0, -1.0, 1.0)[:, None]
    pm = np.zeros((128, 128), np.float32)
    pm[p ^ 1, p] = 1.0
    a = (p // 64)[:, None, None]
    kc = (p % 64)[:, None, None]
    jj = np.arange(22)[None, :, None]
    qc = np.arange(64)[None, None, :]
    dr = (17 - jj) + a
    c0 = np.clip(qc - 8, 0, 48)
    colok = (kc >= c0) & (kc < c0 + 16)
    NEG = -30000.0
    mI = np.where(colok & (dr >= 3) & (dr <= 10), 0.0, NEG).astype(np.float32).reshape(128, 1408)
    mF = np.where(colok & (dr >= 0) & (dr <= 14), 0.0, NEG).astype(np.float32).reshape(128, 1408)
    dri = np.broadcast_to(np.clip(dr, 0, 14), (128, 22, 64))
    dci = np.broadcast_to(np.clip(kc - qc + 15, 0, 30), (128, 22, 64))
    drv = np.broadcast_to((dr >= 0) & (dr <= 14), (128, 22, 64))
    _CONST.update(triu=np.triu(np.ones((128, 128), np.float32)), tril=np.tril(np.ones((128, 128), np.float32)))
    _CONST.update(C=np.ascontiguousarray(C, np.float32), S=np.ascontiguousarray(Sn * sign, np.float32), pm=pm,
                  ident=np.eye(128, dtype=np.float32), mI=mI, mF=mF, dri=dri, dci=dci, drv=drv)
    return _CONST


def to_sm(a):
    a = np.asarray(a, np.float32)
    rest = a.shape[3:]
    a = a.reshape((2, 32, 2, 64) + rest)
    a = np.moveaxis(a, (2, 3), (0, 1))
    return np.ascontiguousarray(a.reshape((128, 2, 32) + rest))


def make_inputs(inp, b, xin=None):
    cs = consts()
    if xin is None:
        xin = np.ascontiguousarray(np.concatenate([inp["ctx"][b], inp["x"][b]], axis=0).T)
    cT = np.stack([fm(inp["c"][b]), fm(inp["c_ctx"])], axis=-1)
    b_ada = fm(inp["b_ada"].reshape(DEPTH, 9, D)).reshape(128, DEPTH, 72)
    rpb = inp["na_rpb"][0]
    rpbg = rpb[:, cs["dri"], cs["dci"]]
    rpbg = np.where(cs["drv"][None], rpbg, np.float32(0.0)).reshape(16, 128, 1408)
    lam = np.stack([inp["da_lam_q1"][0], inp["da_lam_k1"][0], inp["da_lam_q2"][0], inp["da_lam_k2"][0]], axis=-1)
    return {
        "xin": xin, "cT": np.ascontiguousarray(cT),
        "w_ada": inp["w_ada"], "b_ada": np.ascontiguousarray(b_ada),
        "g_pre": fm(inp["g_pre"]), "g_post": fm(inp["g_post"]),
        "w_ff1": inp["w_ff1"], "w_ff3": inp["w_ff3"], "w_ff2": inp["w_ff2"],
        "da_w_qkv": inp["da_w_qkv"][0], "da_w_o": inp["da_w_o"][0],
        "da_lam": np.ascontiguousarray(lam, np.float32), "da_subln": np.ascontiguousarray(inp["da_subln"][0].reshape(128, 1)),
        "na_w_qkv": inp["na_w_qkv"][0], "na_w_o": inp["na_w_o"][0],
        "na_rpbg": np.ascontiguousarray(rpbg, np.float32),
        "s5_are": to_sm(inp["s5_a_re"][0]), "s5_aim": to_sm(inp["s5_a_im"][0]),
        "s5_ldt": to_sm(np.broadcast_to(inp["s5_log_dt"][0][:, :, None], (2, 64, 64))),
        "s5_bre": to_sm(inp["s5_b_re"][0]), "s5_bim": to_sm(inp["s5_b_im"][0]),
        "s5_cre": to_sm(np.swapaxes(inp["s5_c_re"][0], 2, 3)), "s5_cim": to_sm(np.swapaxes(inp["s5_c_im"][0], 2, 3)),
        "s5_d": fm(inp["s5_d"][0]), "s5_bglu": fm(inp["s5_b_glu"][0]), "s5_w_glu": inp["s5_w_glu"][0],
        "hg_w_qig": inp["hg_w_qig"][0], "hg_w_f": inp["hg_w_f"][0], "hg_b_f": fm(inp["hg_b_f"][0]),
        "hg_lb": fm(inp["hg_lb_logits"]), "hg_gn": np.ascontiguousarray(inp["hg_gnorm"][0].reshape(128, 1)),
        "hg_w_o": inp["hg_w_o"][0], "c_triu": cs["triu"], "c_tril": cs["tril"],
        "c_ropeC": cs["C"], "c_ropeS": cs["S"], "c_pm": cs["pm"], "c_ident": cs["ident"],
        "c_maskI": cs["mI"], "c_maskF": cs["mF"],
    }


def kernel(**inputs):
    inp = {k_: np.asarray(v) for k_, v in inputs.items()}
    prog = Prog()
    in_maps = [make_inputs(inp, b) for b in range(8)]
    res = run_bass_kernel_spmd(prog.nc, in_maps, core_ids=list(range(8)))
    out = np.stack([np.ascontiguousarray(res.results[b]["out"].T) for b in range(8)], axis=0)
    return out.astype(np.float32)
```

```python
import math
import numpy as np
from contextlib import ExitStack
import concourse.bass as bass
import concourse.mybir as mybir
from concourse.bass_utils import run_bass_kernel_spmd

F32 = mybir.dt.float32
BF16 = mybir.dt.bfloat16
AF = mybir.ActivationFunctionType
ALU = mybir.AluOpType

D = 1024
S = 4096
LC = 256
T = S + LC
DFF = 2816
NCH = 8
NFF = 22
DEPTH = 4
EPS = 1e-6
EPOCH = 30000
NRING = 32
NPR = 40


class KB:
    def __init__(self):
        self.nc = bass.Bass("TRN2", target_bir_lowering=False)
        self.es = ExitStack()
        nc = self.nc
        self.eng = {"pe": nc.tensor, "dve": nc.vector, "act": nc.scalar, "pool": nc.gpsimd, "sp": nc.sync}
        self.sems = {e: [] for e in self.eng}
        self.cnt = {e: 0 for e in self.eng}
        self.known = {}
        self.last_w = {}
        self.readers = {}
        self.ring = [self.es.enter_context(nc.semaphore("dr%d" % i)) for i in range(NRING)]
        self.ring_val = [0] * NRING
        self.ndma = 0
        self.nins = {e: 0 for e in self.eng}
        self.scopes = []
        self.pring = [self.es.enter_context(nc.semaphore("pr%d" % i)) for i in range(NPR)]
        self.pr_used = 0
        self.uid = 0
        self.dummy = self.sb("dummy", [128, 1], F32)

    def sb(self, name, shape, dtype):
        self.uid += 1
        t = self.nc.sbuf_tensor("%s_%d" % (name, self.uid), list(shape), dtype)
        st = self.scopes[-1] if self.scopes else self.es
        return st.enter_context(t)

    def ps(self, name, shape, dtype=F32):
        st = self.scopes[-1] if self.scopes else self.es
        return st.enter_context(self.nc.psum_tensor(name, list(shape), dtype))

    def push(self):
        self.scopes.append(ExitStack())

    def pop(self):
        self.barrier()
        self.scopes.pop().close()

    def _cursem(self, e):
        if not self.sems[e] or self.cnt[e] >= EPOCH:
            s = self.es.enter_context(self.nc.semaphore("s_%s_%d" % (e, len(self.sems[e]))))
            self.sems[e].append(s)
            self.cnt[e] = 0
        return len(self.sems[e]) - 1

    @staticmethod
    def _key(x):
        if isinstance(x, (tuple, str)):
            return x
        if hasattr(x, "tensor"):
            return x.tensor.name
        return x.name

    def _wait(self, e, dep):
        if dep[0] == "c":
            _, te, ep, c = dep
            if te == e and e == "pe":
                return
            kk = (e, te, ep)
            if self.known.get(kk, 0) >= c:
                return
            self.eng[e].wait_ge(self.sems[te][ep], c)
            self.known[kk] = c
        elif dep[0] == "p":
            _, slot, val = dep
            kk = (e, "pring", slot)
            if self.known.get(kk, 0) >= val:
                return
            self.eng[e].wait_ge(self.pring[slot], val)
            self.known[kk] = val
        else:
            _, slot, val = dep
            kk = (e, "ring", slot)
            if self.known.get(kk, 0) >= val:
                return
            self.eng[e].wait_ge(self.ring[slot], val)
            self.known[kk] = val

    def _deps(self, e, r, w):
        deps = []
        for k in r:
            k = self._key(k)
            if k in self.last_w:
                deps.append(self.last_w[k])
        for k in w:
            k = self._key(k)
            if k in self.last_w:
                deps.append(self.last_w[k])
            deps.extend(self.readers.get(k, ()))
        for d in deps:
            self._wait(e, d)

    def _record(self, me, r, w):
        for k in w:
            k = self._key(k)
            self.last_w[k] = me
            self.readers[k] = []
        for k in r:
            k = self._key(k)
            lst = self.readers.setdefault(k, [])
            lst[:] = [d for d in lst if d[:-1] != me[:-1]]
            lst.append(me)

    def op(self, e, fn, r=(), w=()):
        ep = self._cursem(e)
        self._deps(e, r, w)
        ins = fn(self.eng[e])
        self.cnt[e] += 1
        self.nins[e] += 1
        ins.then_inc(self.sems[e][ep], 1)
        self._record(("c", e, ep, self.cnt[e]), r, w)
        return ins

    def dma(self, out, in_, r=None, w=None, q="sp", **kw):
        r = [in_] if r is None else r
        w = [out] if w is None else w
        if q == "pool":
            assert self.pr_used < NPR, "too many gpsimd DMAs in one phase"
            slot = self.pr_used
            self.pr_used += 1
            self._deps(q, r, w)
            ins = self.eng[q].dma_start(out=out, in_=in_, **kw)
            ins.then_inc(self.pring[slot], 16)
            self._record(("p", slot, 16), r, w)
            return ins
        slot = self.ndma % NRING
        self.ndma += 1
        if self.ring_val[slot] > 0:
            self._wait(q, ("d", slot, self.ring_val[slot]))
        self._deps(q, r, w)
        ins = self.eng[q].dma_start(out=out, in_=in_, **kw)
        self.ring_val[slot] += 16
        ins.then_inc(self.ring[slot], 16)
        self._record(("d", slot, self.ring_val[slot]), r, w)
        return ins

    def barrier(self):
        self._wait_all("pool")
        if self.pr_used:
            for slot in range(self.pr_used):
                self._wait("pool", ("p", slot, 16))
            for slot in range(self.pr_used):
                self.eng["pool"].sem_clear(self.pring[slot])
            self.known = {kk: v for kk, v in self.known.items() if kk[1] != "pring"}
            self.pr_used = 0
            self.op("pool", lambda e: e.memset(self.dummy[:], 0.0), w=[self.dummy])
        for e in self.eng:
            if e != "pool":
                self._wait_all(e)
        self.last_w = {}
        self.readers = {}

    def _wait_all(self, e):
        for slot in range(NRING):
            if self.ring_val[slot]:
                self._wait(e, ("d", slot, self.ring_val[slot]))
        for te in self.eng:
            if te != e and self.sems[te] and self.cnt[te]:
                self._wait(e, ("c", te, len(self.sems[te]) - 1, self.cnt[te]))

    def close(self):
        self.barrier()
        while self.scopes:
            self.scopes.pop().close()
        self.es.close()


def bcast_mid(ap2d, n):
    a = ap2d.ap
    return bass.AP(ap2d.tensor, ap2d.offset, [list(a[0]), [0, n], list(a[1])])


class Prog:
    def __init__(self, stop_after=None, layers=None):
        self.k = KB()
        self.nc = self.k.nc
        self.stop_after = stop_after
        self.layers = list(range(DEPTH)) if layers is None else layers
        self.build()

    def din(self, name, shape, dt=F32):
        return self.nc.dram_tensor(name, list(shape), dt, kind="ExternalInput")

    def wload(self, dst, src2d, kc_n, ncols, q="pool", col0=0, stage=None):
        k = self.k
        if stage is None:
            c = 0
            while c < ncols:
                w = min(2048, ncols - c)
                k0 = 0
                while k0 < kc_n:
                    kn = min(16, kc_n - k0)
                    k.dma(dst[:, k0:k0 + kn, c:c + w],
                          src2d[k0 * 128:(k0 + kn) * 128, col0 + c:col0 + c + w].rearrange("(k p) n -> p k n", p=128), q="sp")
                    k0 += kn
                c += w
            return
        CW = 256
        for k0 in range(0, kc_n, 8):
            kn = min(8, kc_n - k0)
            for c in range(0, ncols, CW):
                w = min(CW, ncols - c)
                st = stage[self._stg % len(stage)]
                self._stg += 1
                k.dma(st[:, 0:kn, 0:w],
                      src2d[k0 * 128:(k0 + kn) * 128, col0 + c:col0 + c + w].rearrange("(k p) n -> p k n", p=128), q="sp")
                k.op("pool", lambda e: e.tensor_copy(out=dst[:, k0:k0 + kn, c:c + w], in_=st[:, 0:kn, 0:w]), r=[st], w=[dst])

    def build(self):
        k, nc = self.k, self.nc
        self.xin = self.din("xin", [D, T])
        self.cT = self.din("cT", [128, NCH, 2])
        self.w_ada = self.din("w_ada", [DEPTH, D, 9 * D])
        self.b_ada = self.din("b_ada", [128, DEPTH, 72])
        self.g_pre = self.din("g_pre", [128, DEPTH, 3, NCH])
        self.g_post = self.din("g_post", [128, DEPTH, 3, NCH])
        self.w_ff1 = self.din("w_ff1", [DEPTH, 2, D, DFF])
        self.w_ff3 = self.din("w_ff3", [DEPTH, 2, D, DFF])
        self.w_ff2 = self.din("w_ff2", [DEPTH, 2, DFF, D])
        self.da_wqkv = self.din("da_w_qkv", [D, 3 * D])
        self.da_wo = self.din("da_w_o", [D, D])
        self.da_lam = self.din("da_lam", [64, 4])
        self.da_subln = self.din("da_subln", [128, 1])
        self.na_wqkv = self.din("na_w_qkv", [D, 3 * D])
        self.na_wo = self.din("na_w_o", [D, D])
        self.na_rpbg = self.din("na_rpbg", [16, 128, 1408])
        self.s5_are = self.din("s5_are", [128, 2, 32])
        self.s5_aim = self.din("s5_aim", [128, 2, 32])
        self.s5_ldt = self.din("s5_ldt", [128, 2, 32])
        self.s5_bre = self.din("s5_bre", [128, 2, 32, 16])
        self.s5_bim = self.din("s5_bim", [128, 2, 32, 16])
        self.s5_cre = self.din("s5_cre", [128, 2, 32, 16])
        self.s5_cim = self.din("s5_cim", [128, 2, 32, 16])
        self.s5_d = self.din("s5_d", [128, NCH])
        self.s5_bglu = self.din("s5_bglu", [128, NCH])
        self.s5_wglu = self.din("s5_w_glu", [D, D])
        self.hg_wqig = self.din("hg_w_qig", [D, 3 * D])
        self.hg_wf = self.din("hg_w_f", [2, D, D])
        self.hg_bf = self.din("hg_b_f", [128, 2, NCH])
        self.hg_lb = self.din("hg_lb", [128, DEPTH, NCH])
        self.hg_gn = self.din("hg_gn", [128, 1])
        self.hg_wo = self.din("hg_w_o", [D, D])
        self.c_triu = self.din("c_triu", [128, 128])
        self.c_tril = self.din("c_tril", [128, 128])
        self.Fs = nc.dram_tensor("Fs", [2, D, T], F32, kind="Internal")
        self.c_ropeC = self.din("c_ropeC", [128, S])
        self.c_ropeS = self.din("c_ropeS", [128, S])
        self.c_pm = self.din("c_pm", [128, 128])
        self.c_ident = self.din("c_ident", [128, 128])
        self.c_maskI = self.din("c_maskI", [128, 1408])
        self.c_maskF = self.din("c_maskF", [128, 1408])
        self.Hs = nc.dram_tensor("Hs", [D, T], BF16, kind="Internal")
        self.Ys = nc.dram_tensor("Ys", [D, T], F32, kind="Internal")
        self.QKs = nc.dram_tensor("QKs", [2 * D, T], BF16, kind="Internal")
        self.Vs = nc.dram_tensor("Vs", [T, D], BF16, kind="Internal")
        self.AO = nc.dram_tensor("AOs", [D, T], BF16, kind="Internal")
        self.out = nc.dram_tensor("out", [D, S], F32, kind="ExternalOutput")
        self.X = nc.dram_tensor("Xres", [D, T], F32, kind="Internal")

        self.ones_bf = k.sb("ones_bf", [128, 128], BF16)
        k.op("dve", lambda e: e.memset(self.ones_bf[:], 1.0), w=[self.ones_bf])
        self.A = k.sb("modA", [128, DEPTH, 3, NCH, 2], F32)
        self.SH = k.sb("modSH", [128, DEPTH, 3, NCH, 2], F32)
        self.GT = k.sb("modGT", [128, DEPTH, 3, NCH, 2], F32)
        self.psb = [k.ps("psb%d" % i, [128, 512], F32) for i in range(7)]
        self.ps_bf = k.ps("psbf", [128, 1024], BF16)
        self._eps = k.sb("eps_c", [128, 1], F32)
        k.op("dve", lambda e: e.memset(self._eps[:], EPS), w=[self._eps])

        self.setup_mod()
        if self.stop_after == "mod":
            k.dma(self.out.ap(), self.xin.ap()[:, LC:T], q="sp")
            k.close()
            return
        first = True
        for i in self.layers:
            last = i == DEPTH - 1
            self.ffn(i, 0, 0, src=(self.xin if first else self.X), dst=self.X, with_ctx=True)
            first = False
            if self.stop_after == "a%d" % i:
                return self.finish_dbg()
            kind = i % 4
            if kind in (0, 1, 2, 3):
                self.mixer_prologue(i)
                if kind == 0:
                    self.s5_core(i)
                elif kind == 3:
                    self.hg_proj(i)
                    self.hg_core(i)
                    self.oproj_phase(self.hg_wo, with_ctx=not last)
                elif kind == 1:
                    self.qkv_phase(self.da_wqkv, rope=True, qscale=None)
                    self.da_attn(i)
                    self.oproj_phase(self.da_wo)
                else:
                    self.qkv_phase(self.na_wqkv, rope=False, qscale=0.125)
                    self.na_attn(i)
                    self.oproj_phase(self.na_wo)
                self.mixer_epilogue(i, with_ctx=not last)
            if self.stop_after == "m%d" % i:
                return self.finish_dbg()
            self.ffn(i, 1, 2, src=self.X, dst=self.X, with_ctx=not last, final=last)
            if self.stop_after == "b%d" % i:
                return self.finish_dbg()
        k.close()

    def finish_dbg(self):
        k = self.k
        k.dma(self.out.ap(), self.X.ap()[:, LC:T], q="sp")
        k.close()

    def setup_mod(self):
        k = self.k
        k.push()
        sc = k.sb("sc", [128, NCH, 2], F32)
        k.dma(sc[:], self.cT.ap())
        sg = k.sb("sg", [128, NCH, 2], F32)
        k.op("act", lambda e: e.activation(out=sg[:], in_=sc[:], func=AF.Sigmoid), r=[sc], w=[sg])
        k.op("dve", lambda e: e.tensor_tensor(out=sc[:], in0=sc[:], in1=sg[:], op=ALU.mult), r=[sc, sg], w=[sc])
        bada = k.sb("bada", [128, DEPTH, 72], F32)
        k.dma(bada[:], self.b_ada.ap())
        gpre = k.sb("gpre", [128, DEPTH, 3, NCH], F32)
        gpost = k.sb("gpost", [128, DEPTH, 3, NCH], F32)
        k.dma(gpre[:], self.g_pre.ap())
        k.dma(gpost[:], self.g_post.ap())
        m = k.sb("m_sb", [128, DEPTH, 72, 2], F32)
        PIECE = 1152
        wbuf = [k.sb("wada%d" % i, [128, NCH, PIECE], F32) for i in range(2)]
        pi = 0
        for i in range(DEPTH):
            ps = self.psb[i % 2]
            for pc in range(8):
                wb = wbuf[pi % 2]
                pi += 1
                self.wload(wb, self.w_ada.ap()[i], NCH, PIECE, col0=pc * PIECE)
                for jj in range(9):
                    jc = pc * 9 + jj
                    for kc in range(NCH):
                        k.op("pe", lambda e: e.matmul(out=ps[:, 2 * jc:2 * jc + 2], lhsT=wb[:, kc, jj * 128:(jj + 1) * 128],
                                                      rhs=sc[:, kc, :], start=(kc == 0), stop=(kc == NCH - 1)),
                             r=[wb, sc], w=[ps])
            k.op("dve", lambda e: e.tensor_tensor(out=m[:, i], in0=ps[:, 0:144].rearrange("p (j v) -> p j v", v=2),
                                                  in1=bada[:, i].unsqueeze(2).to_broadcast([128, 72, 2]), op=ALU.add),
                 r=[ps, bada], w=[m])
        for i in range(DEPTH):
            for j in range(3):
                base = j * 24
                sh = m[:, i, base + 0:base + 8, :]
                scl = m[:, i, base + 8:base + 16, :]
                gt = m[:, i, base + 16:base + 24, :]
                wgt = 0.5 if j != 1 else 1.0
                gp = gpre[:, i, j, :].unsqueeze(2).to_broadcast([128, NCH, 2])
                gq = gpost[:, i, j, :].unsqueeze(2).to_broadcast([128, NCH, 2])
                k.op("dve", lambda e: e.scalar_tensor_tensor(out=self.A[:, i, j], in0=scl, scalar=1.0, in1=gp, op0=ALU.add, op1=ALU.mult),
                     r=[m, gpre], w=[self.A])
                k.op("dve", lambda e: e.scalar_tensor_tensor(out=self.GT[:, i, j], in0=gt, scalar=wgt, in1=gq, op0=ALU.mult, op1=ALU.mult),
                     r=[m, gpost], w=[self.GT])
                k.op("dve", lambda e: e.tensor_copy(out=self.SH[:, i, j], in_=sh), r=[m], w=[self.SH])
        k.pop()

    def rstd_from(self, sq, nchunks, N, rstd, ps, dim):
        k = self.k
        for c in range(nchunks):
            k.op("pe", lambda e: e.matmul(out=ps[:, 0:N], lhsT=self.ones_bf[:], rhs=sq[:, c, 0:N], start=(c == 0), stop=(c == nchunks - 1)),
                 r=[sq, self.ones_bf], w=[ps])
        k.op("act", lambda e: e.activation(out=rstd[:, 0:N], in_=ps[:, 0:N], func=AF.Sqrt, bias=self.eps_ap(), scale=1.0 / dim), r=[ps], w=[rstd])
        k.op("dve", lambda e: e.reciprocal(out=rstd[:, 0:N], in_=rstd[:, 0:N]), r=[rstd], w=[rstd])

    def eps_ap(self):
        return self._eps[:]

    def token_tiles(self, with_ctx, n):
        tiles = []
        if with_ctx:
            for t0 in range(0, LC, n):
                tiles.append((t0, min(n, LC - t0), 1))
        for t0 in range(LC, T, n):
            tiles.append((t0, min(n, T - t0), 0))
        return tiles

    def prenorm(self, xt, N, i, j, v, sq, tmp, h, rstd, ps):
        k = self.k
        k.op("act", lambda e: e.activation(out=sq[:, :, 0:N], in_=xt[:, :, 0:N], func=AF.Square), r=[xt], w=[sq])
        self.rstd_from(sq, NCH, N, rstd, ps, D)
        for c in range(NCH):
            k.op("dve", lambda e: e.scalar_tensor_tensor(out=tmp[:, c, 0:N], in0=xt[:, c, 0:N], scalar=self.A[:, i, j, c, v:v + 1],
                                                         in1=rstd[:, 0:N], op0=ALU.mult, op1=ALU.mult), r=[xt, rstd, self.A], w=[tmp])
            k.op("act", lambda e: e.activation(out=h[:, c, 0:N], in_=tmp[:, c, 0:N], func=AF.Identity, bias=self.SH[:, i, j, c, v:v + 1], scale=1.0),
                 r=[tmp, self.SH], w=[h])

    def postres(self, y, xt, N, i, j, v, sq, rstd, ps):
        k = self.k
        self.rstd_from(sq, NCH, N, rstd, ps, D)
        for c in range(NCH):
            k.op("dve", lambda e: e.scalar_tensor_tensor(out=y[:, c, 0:N], in0=y[:, c, 0:N], scalar=self.GT[:, i, j, c, v:v + 1],
                                                         in1=rstd[:, 0:N], op0=ALU.mult, op1=ALU.mult), r=[y, rstd, self.GT], w=[y])
        k.op("pool", lambda e: e.tensor_tensor(out=xt[:, :, 0:N], in0=xt[:, :, 0:N], in1=y[:, :, 0:N], op=ALU.add), r=[xt, y], w=[xt])


    def mixer_prologue(self, i):
        k = self.k
        NT = 256
        k.push()
        xts = [k.sb("pxt%d" % b, [128, NCH, NT], F32) for b in range(2)]
        sq = k.sb("psq", [128, NCH, NT], BF16)
        tmp = k.sb("ptmp", [128, NCH, NT], F32)
        hs = [k.sb("ph%d" % b, [128, NCH, NT], BF16) for b in range(2)]
        rstd = k.sb("prstd", [128, NT], F32)
        for ti, (t0, N, v) in enumerate(self.token_tiles(True, NT)):
            xt, h = xts[ti % 2], hs[ti % 2]
            k.dma(xt[:, :, 0:N], self.X.ap()[:, t0:t0 + N].rearrange("(c p) n -> p c n", p=128))
            self.prenorm(xt, N, i, 1, v, sq, tmp, h, rstd, self.psb[0])
            k.dma(self.Hs.ap()[:, t0:t0 + N].rearrange("(c p) n -> p c n", p=128), h[:, :, 0:N], w=[("Hs", t0)])
        k.pop()

    def mixer_epilogue(self, i, with_ctx):
        k = self.k
        NT = 256
        k.push()
        xts = [k.sb("ext%d" % b, [128, NCH, NT], F32) for b in range(2)]
        ys = [k.sb("eyt%d" % b, [128, NCH, NT], F32) for b in range(2)]
        sq = k.sb("esq", [128, NCH, NT], BF16)
        rstd = k.sb("erstd", [128, NT], F32)
        for ti, (t0, N, v) in enumerate(self.token_tiles(with_ctx, NT)):
            xt, y = xts[ti % 2], ys[ti % 2]
            k.dma(xt[:, :, 0:N], self.X.ap()[:, t0:t0 + N].rearrange("(c p) n -> p c n", p=128))
            k.dma(y[:, :, 0:N], self.Ys.ap()[:, t0:t0 + N].rearrange("(c p) n -> p c n", p=128))
            k.op("act", lambda e: e.activation(out=sq[:, :, 0:N], in_=y[:, :, 0:N], func=AF.Square), r=[y], w=[sq])
            self.postres(y, xt, N, i, 1, v, sq, rstd, self.psb[0])
            k.dma(self.X.ap()[:, t0:t0 + N].rearrange("(c p) n -> p c n", p=128), xt[:, :, 0:N], w=[("X", t0)])
        k.pop()

    def qkv_phase(self, w_dram, rope, qscale):
        k = self.k
        NT = 512
        k.push()
        w = k.sb("wqkv", [128, NCH, 3 * D], BF16)
        stage = [k.sb("wstg%d" % b, [128, 8, 256], F32) for b in range(2)]
        self._stg = 0
        self.wload(w, w_dram.ap(), NCH, 3 * D, stage=stage)
        if rope:
            Ct = k.sb("ropeC", [128, S], F32)
            St = k.sb("ropeS", [128, S], F32)
            k.dma(Ct[:], self.c_ropeC.ap())
            k.dma(St[:], self.c_ropeS.ap())
            pmf = k.sb("pmf", [128, 128], F32)
            pm = k.sb("pm", [128, 128], BF16)
            k.dma(pmf[:], self.c_pm.ap())
            k.op("dve", lambda e: e.tensor_copy(out=pm[:], in_=pmf[:]), r=[pmf], w=[pm])
        hts = [k.sb("ht%d" % b, [128, NCH, NT], BF16) for b in range(2)]
        qs = [k.sb("qs%d" % b, [128, NT], BF16) for b in range(2)]
        t1 = [k.sb("t1%d" % b, [128, NT], F32) for b in range(2)]
        t2 = [k.sb("t2%d" % b, [128, NT], F32) for b in range(2)]
        qo = [k.sb("qo%d" % b, [128, NT], BF16) for b in range(2)]
        vo = [k.sb("vo%d" % b, [128, 512], BF16) for b in range(2)]
        nv = 0
        for ti, (t0, N, v) in enumerate(self.token_tiles(True, NT)):
            ht = hts[ti % 2]
            k.dma(ht[:, :, 0:N], self.Hs.ap()[:, t0:t0 + N].rearrange("(c p) n -> p c n", p=128))
            for cc in range(16):
                b = cc % 2
                ps = self.psb[1 + b]
                for c in range(NCH):
                    k.op("pe", lambda e: e.matmul(out=ps[:, 0:N], lhsT=w[:, c, cc * 128:(cc + 1) * 128], rhs=ht[:, c, 0:N],
                                                  start=(c == 0), stop=(c == NCH - 1)), r=[w, ht], w=[ps])
                if rope and v == 0:
                    l0 = t0 - LC
                    sw = self.psb[3 + b]
                    k.op("dve", lambda e: e.tensor_copy(out=qs[b][:, 0:N], in_=ps[:, 0:N]), r=[ps], w=[qs[b]])
                    k.op("pe", lambda e: e.matmul(out=sw[:, 0:N], lhsT=pm[:], rhs=qs[b][:, 0:N], start=True, stop=True), r=[pm, qs[b]], w=[sw])
                    k.op("dve", lambda e: e.tensor_tensor(out=t1[b][:, 0:N], in0=ps[:, 0:N], in1=Ct[:, l0:l0 + N], op=ALU.mult), r=[ps, Ct], w=[t1[b]])
                    k.op("dve", lambda e: e.tensor_tensor(out=t2[b][:, 0:N], in0=sw[:, 0:N], in1=St[:, l0:l0 + N], op=ALU.mult), r=[sw, St], w=[t2[b]])
                    k.op("pool", lambda e: e.tensor_tensor(out=qo[b][:, 0:N], in0=t1[b][:, 0:N], in1=t2[b][:, 0:N], op=ALU.add), r=[t1[b], t2[b]], w=[qo[b]])
                elif qscale is not None and cc < 8:
                    k.op("dve", lambda e: e.tensor_scalar(out=qo[b][:, 0:N], in0=ps[:, 0:N], scalar1=float(qscale), scalar2=None, op0=ALU.mult), r=[ps], w=[qo[b]])
                else:
                    k.op("dve", lambda e: e.tensor_copy(out=qo[b][:, 0:N], in_=ps[:, 0:N]), r=[ps], w=[qo[b]])
                k.dma(self.QKs.ap()[cc * 128:(cc + 1) * 128, t0:t0 + N], qo[b][:, 0:N], w=[("QKs", cc, t0)])
            for sub in range(N // 128):
                for vb in range(2):
                    ps = self.psb[5 + vb]
                    for c in range(NCH):
                        k.op("pe", lambda e: e.matmul(out=ps[:, :], lhsT=ht[:, c, sub * 128:(sub + 1) * 128], rhs=w[:, c, 2 * D + vb * 512:2 * D + (vb + 1) * 512],
                                                      start=(c == 0), stop=(c == NCH - 1)), r=[w, ht], w=[ps])
                    vv = vo[nv % 2]
                    nv += 1
                    k.op("act", lambda e: e.activation(out=vv[:], in_=ps[:, :], func=AF.Identity), r=[ps], w=[vv])
                    k.dma(self.Vs.ap()[t0 + sub * 128:t0 + (sub + 1) * 128, vb * 512:(vb + 1) * 512], vv[:], w=[("Vs", t0, sub, vb)])
        k.pop()

    def oproj_phase(self, wo_dram, with_ctx=True):
        k = self.k
        NT = 512
        k.push()
        wo = k.sb("wo", [128, NCH, D], BF16)
        stage = [k.sb("wstg%d" % b, [128, 8, 256], F32) for b in range(2)]
        self._stg = 0
        self.wload(wo, wo_dram.ap(), NCH, D, stage=stage)
        aos = [k.sb("ao%d" % b, [128, NCH, NT], BF16) for b in range(2)]
        yo = [k.sb("yo%d" % b, [128, NT], F32) for b in range(2)]
        for ti, (t0, N, v) in enumerate(self.token_tiles(with_ctx, NT)):
            ao = aos[ti % 2]
            k.dma(ao[:, :, 0:N], self.AO.ap()[:, t0:t0 + N].rearrange("(c p) n -> p c n", p=128))
            for c2 in range(NCH):
                ps = self.psb[1 + c2 % 2]
                for c in range(NCH):
                    k.op("pe", lambda e: e.matmul(out=ps[:, 0:N], lhsT=wo[:, c, c2 * 128:(c2 + 1) * 128], rhs=ao[:, c, 0:N],
                                                  start=(c == 0), stop=(c == NCH - 1)), r=[wo, ao], w=[ps])
                y = yo[c2 % 2]
                k.op("dve", lambda e: e.tensor_copy(out=y[:, 0:N], in_=ps[:, 0:N]), r=[ps], w=[y])
                k.dma(self.Ys.ap()[c2 * 128:(c2 + 1) * 128, t0:t0 + N], y[:, 0:N], w=[("Ys", c2, t0)])
        k.pop()

    def da_attn(self, i):
        k = self.k
        lam_init = 0.8 - 0.6 * math.exp(-0.3 * i)
        k.push()
        lv = k.sb("lamv", [64, 4], F32)
        k.dma(lv[:], self.da_lam.ap())
        pr = k.sb("lampr", [64, 2], F32)
        k.op("dve", lambda e: e.tensor_tensor(out=pr[:, 0:1], in0=lv[:, 0:1], in1=lv[:, 1:2], op=ALU.mult), r=[lv], w=[pr])
        k.op("dve", lambda e: e.tensor_tensor(out=pr[:, 1:2], in0=lv[:, 2:3], in1=lv[:, 3:4], op=ALU.mult), r=[lv, pr], w=[pr])
        onesf = k.sb("onesf", [64, 128], F32)
        k.op("dve", lambda e: e.memset(onesf[:], 1.0), w=[onesf])
        psl = self.psb[6]
        k.op("pe", lambda e: e.matmul(out=psl[:, 0:2], lhsT=onesf[:], rhs=pr[:], start=True, stop=True), r=[onesf, pr], w=[psl])
        ex = k.sb("lamex", [128, 2], F32)
        k.op("act", lambda e: e.activation(out=ex[:], in_=psl[:, 0:2], func=AF.Exp), r=[psl], w=[ex])
        neglam = k.sb("neglam", [128, 1], F32)
        k.op("dve", lambda e: e.tensor_tensor(out=neglam[:], in0=ex[:, 1:2], in1=ex[:, 0:1], op=ALU.subtract), r=[ex], w=[neglam])
        k.op("dve", lambda e: e.tensor_scalar(out=neglam[:], in0=neglam[:], scalar1=-lam_init, scalar2=None, op0=ALU.add), r=[neglam], w=[neglam])
        gsub = k.sb("gsub", [128, 1], F32)
        k.dma(gsub[:], self.da_subln.ap())
        k.op("dve", lambda e: e.tensor_scalar(out=gsub[:], in0=gsub[:], scalar1=1.0 - lam_init, scalar2=None, op0=ALU.mult), r=[gsub], w=[gsub])

        qTs = [k.sb("qT%d" % b, [128, T], BF16) for b in range(2)]
        kTs = [k.sb("kT%d" % b, [128, T], BF16) for b in range(2)]
        vhs = [k.sb("vh%d" % b, [128, 34, 128], BF16) for b in range(2)]
        pb = [k.sb("pexp%d" % b, [128, 512], BF16) for b in range(4)]
        accA = [k.sb("accA%d" % b, [128, 512], F32) for b in range(2)]
        accB = [k.sb("accB%d" % b, [128, 512], F32) for b in range(2)]
        ones128f = k.sb("ones128f", [128, 128], F32)
        k.op("dve", lambda e: e.memset(ones128f[:], 1.0), w=[ones128f])
        rb = k.sb("rb", [128, 512], F32)
        o1 = k.sb("o1", [128, 512], F32)
        o2 = k.sb("o2", [128, 512], F32)
        sq = k.sb("dsq", [128, 1, 512], BF16)
        rstd = k.sb("drstd", [128, 512], F32)
        ons = [k.sb("on%d" % b, [128, 512], BF16) for b in range(2)]
        blocks = [(0, LC, [0, 1])] + [(LC + 512 * b, 512, list(range(34))) for b in range(8)]
        nb = 0
        for hd in range(8):
            qT, kT, vh = qTs[hd % 2], kTs[hd % 2], vhs[hd % 2]
            k.dma(qT[:], self.QKs.ap()[hd * 128:(hd + 1) * 128, :])
            k.dma(kT[:], self.QKs.ap()[D + hd * 128:D + (hd + 1) * 128, :])
            k.dma(vh[:], self.Vs.ap()[:, hd * 128:(hd + 1) * 128].rearrange("(kt p) d -> p kt d", p=128))
            for (q0, N, kts) in blocks:
                units = [(ki, kt, m) for ki, kt in enumerate(kts) for m in range(2)]

                def emit_qk(u):
                    ki, kt, m = units[u]
                    pss = self.psb[4 + u % 3]
                    k.op("pe", lambda e: e.matmul(out=pss[:, 0:N], lhsT=kT[m * 64:(m + 1) * 64, kt * 128:(kt + 1) * 128],
                                                  rhs=qT[m * 64:(m + 1) * 64, q0:q0 + N], start=True, stop=True), r=[kT, qT], w=[pss])
                emit_qk(0)
                emit_qk(1)
                for u, (ki, kt, m) in enumerate(units):
                    num = self.psb[2 * m]
                    pss = self.psb[4 + u % 3]
                    P = pb[u % 4]
                    k.op("act", lambda e: e.activation(out=P[:, 0:N], in_=pss[:, 0:N], func=AF.Exp, scale=0.125), r=[pss], w=[P])
                    if u + 2 < len(units):
                        emit_qk(u + 2)
                    k.op("pe", lambda e: e.matmul(out=num[:, 0:N], lhsT=vh[:, kt, :], rhs=P[:, 0:N], start=(ki == 0), stop=(ki == len(kts) - 1)), r=[vh, P], w=[num])
                    eng_, acc_ = ("dve", accA[m]) if ki % 2 == 0 else ("pool", accB[m])
                    if ki < 2:
                        k.op(eng_, lambda e: e.tensor_copy(out=acc_[:, 0:N], in_=P[:, 0:N]), r=[P], w=[acc_])
                    else:
                        k.op(eng_, lambda e: e.tensor_tensor(out=acc_[:, 0:N], in0=acc_[:, 0:N], in1=P[:, 0:N], op=ALU.add), r=[P, acc_], w=[acc_])
                for m in range(2):
                    den = self.psb[2 * m + 1]
                    k.op("dve", lambda e: e.tensor_tensor(out=accA[m][:, 0:N], in0=accA[m][:, 0:N], in1=accB[m][:, 0:N], op=ALU.add), r=[accA[m], accB[m]], w=[accA[m]])
                    k.op("pe", lambda e: e.matmul(out=den[:, 0:N], lhsT=ones128f[:], rhs=accA[m][:, 0:N], start=True, stop=True), r=[ones128f, accA[m]], w=[den])
                n0, d0, n1, d1 = self.psb[0], self.psb[1], self.psb[2], self.psb[3]
                k.op("dve", lambda e: e.reciprocal(out=rb[:, 0:N], in_=d0[:, 0:N]), r=[d0], w=[rb])
                k.op("dve", lambda e: e.tensor_tensor(out=o1[:, 0:N], in0=n0[:, 0:N], in1=rb[:, 0:N], op=ALU.mult), r=[n0, rb], w=[o1])
                k.op("dve", lambda e: e.reciprocal(out=rb[:, 0:N], in_=d1[:, 0:N]), r=[d1], w=[rb])
                k.op("dve", lambda e: e.tensor_tensor(out=o2[:, 0:N], in0=n1[:, 0:N], in1=rb[:, 0:N], op=ALU.mult), r=[n1, rb], w=[o2])
                k.op("dve", lambda e: e.scalar_tensor_tensor(out=o1[:, 0:N], in0=o2[:, 0:N], scalar=neglam[:, 0:1], in1=o1[:, 0:N], op0=ALU.mult, op1=ALU.add), r=[o1, o2, neglam], w=[o1])
                k.op("act", lambda e: e.activation(out=sq[:, 0, 0:N], in_=o1[:, 0:N], func=AF.Square), r=[o1], w=[sq])
                self.rstd_from(sq, 1, N, rstd, self.psb[1], 128)
                on = ons[nb % 2]
                nb += 1
                k.op("dve", lambda e: e.scalar_tensor_tensor(out=on[:, 0:N], in0=o1[:, 0:N], scalar=gsub[:, 0:1], in1=rstd[:, 0:N], op0=ALU.mult, op1=ALU.mult), r=[o1, gsub, rstd], w=[on])
                k.dma(self.AO.ap()[hd * 128:(hd + 1) * 128, q0:q0 + N], on[:, 0:N], w=[("AO", hd, q0)])
        k.pop()

    def na_attn(self, i):
        k = self.k
        k.push()
        idf = k.sb("idf", [128, 128], F32)
        ident = k.sb("ident", [128, 128], BF16)
        k.dma(idf[:], self.c_ident.ap())
        k.op("dve", lambda e: e.tensor_copy(out=ident[:], in_=idf[:]), r=[idf], w=[ident])
        mI = k.sb("mI", [128, 1408], F32)
        mF = k.sb("mF", [128, 1408], F32)
        k.dma(mI[:], self.c_maskI.ap())
        k.dma(mF[:], self.c_maskF.ap())
        qTs = [k.sb("nqT%d" % b, [64, T], BF16) for b in range(2)]
        kTs = [k.sb("nkT%d" % b, [64, T], BF16) for b in range(2)]
        vhs = [k.sb("nvh%d" % b, [128, 34, 64], BF16) for b in range(2)]
        Gs = [k.sb("nG%d" % b, [128, 1408], F32) for b in range(2)]
        TIs = [k.sb("nTI%d" % b, [128, 1408], BF16) for b in range(2)]
        TFs = [k.sb("nTF%d" % b, [128, 1408], BF16) for b in range(2)]
        pb = [k.sb("npexp%d" % b, [128, 512], BF16) for b in range(4)]
        rb = k.sb("nrb", [64, 512], F32)
        sbias = [k.sb("nsb%d" % b, [128, 512], F32) for b in range(3)]
        nacc = k.sb("nacc", [128, 512], F32)
        ones128f = k.sb("nones128f", [128, 128], F32)
        k.op("dve", lambda e: e.memset(ones128f[:], 1.0), w=[ones128f])
        ons = [k.sb("non%d" % b, [64, 512], BF16) for b in range(2)]
        def lat_keys(qr0, nr, krs, tab):
            return [(0, None, 0, 0), (1, None, 0, 0)] + [(2 + kr // 2, tab, 10 - (kr - qr0), nr) for kr in krs]
        blocks = [(0, LC, [(0, None, 0, 0), (1, None, 0, 0)])]
        blocks.append((LC, 4 * 64, lat_keys(0, 4, [0, 2, 4, 6], "F")))
        for qr0 in range(4, 60, 8):
            blocks.append((LC + qr0 * 64, 8 * 64, lat_keys(qr0, 8, list(range(qr0 - 4, qr0 + 12, 2)), "I")))
        blocks.append((LC + 60 * 64, 64, lat_keys(60, 1, [56, 58, 60, 62], "I")))
        blocks.append((LC + 61 * 64, 3 * 64, lat_keys(61, 3, [56, 58, 60, 62], "F")))
        nb = 0
        for hd in range(16):
            b2 = hd % 2
            qT, kT, vh, G, TI, TF = qTs[b2], kTs[b2], vhs[b2], Gs[b2], TIs[b2], TFs[b2]
            k.dma(qT[:], self.QKs.ap()[hd * 64:(hd + 1) * 64, :])
            k.dma(kT[:], self.QKs.ap()[D + hd * 64:D + (hd + 1) * 64, :])
            k.dma(vh[:], self.Vs.ap()[:, hd * 64:(hd + 1) * 64].rearrange("(kt p) d -> p kt d", p=128))
            k.dma(G[:], self.na_rpbg.ap()[hd])
            k.op("dve", lambda e: e.tensor_tensor(out=TI[:], in0=G[:], in1=mI[:], op=ALU.add), r=[G, mI], w=[TI])
            k.op("pool", lambda e: e.tensor_tensor(out=TF[:], in0=G[:], in1=mF[:], op=ALU.add), r=[G, mF], w=[TF])
            for (q0, N, keys) in blocks:
                num, den = self.psb[0], self.psb[1]
                def emit_qk(u):
                    kt_ = keys[u][0]
                    pq = self.psb[2 + u % 4]
                    k.op("pe", lambda e: e.matmul(out=pq[:, 0:N], lhsT=kT[:, kt_ * 128:(kt_ + 1) * 128], rhs=qT[:, q0:q0 + N],
                                                  start=True, stop=True), r=[kT, qT], w=[pq])
                emit_qk(0)
                emit_qk(1)
                for ki, (kt, tab, jj0, nr) in enumerate(keys):
                    pss = self.psb[2 + ki % 4]
                    P = pb[ki % 4]
                    if ki + 2 < len(keys):
                        emit_qk(ki + 2)
                    if tab is not None:
                        tb = TI if tab == "I" else TF
                        sbb = sbias[ki % 3]
                        k.op("dve", lambda e: e.tensor_tensor(out=sbb[:, 0:N], in0=pss[:, 0:N], in1=tb[:, jj0 * 64:(jj0 + nr) * 64], op=ALU.add), r=[pss, tb], w=[sbb])
                        k.op("act", lambda e: e.activation(out=P[:, 0:N], in_=sbb[:, 0:N], func=AF.Exp), r=[sbb], w=[P])
                    else:
                        k.op("act", lambda e: e.activation(out=P[:, 0:N], in_=pss[:, 0:N], func=AF.Exp), r=[pss], w=[P])
                    k.op("pe", lambda e: e.matmul(out=num[0:64, 0:N], lhsT=vh[:, kt, :], rhs=P[:, 0:N], start=(ki == 0), stop=(ki == len(keys) - 1)), r=[vh, P], w=[num])
                    if ki < 1:
                        k.op("pool", lambda e: e.tensor_copy(out=nacc[:, 0:N], in_=P[:, 0:N]), r=[P], w=[nacc])
                    else:
                        k.op("pool", lambda e: e.tensor_tensor(out=nacc[:, 0:N], in0=nacc[:, 0:N], in1=P[:, 0:N], op=ALU.add), r=[P, nacc], w=[nacc])
                k.op("pe", lambda e: e.matmul(out=den[0:64, 0:N], lhsT=ones128f[:, 0:64], rhs=nacc[:, 0:N], start=True, stop=True), r=[ones128f, nacc], w=[den])
                on = ons[nb % 2]
                nb += 1
                k.op("dve", lambda e: e.reciprocal(out=rb[:, 0:N], in_=den[0:64, 0:N]), r=[den], w=[rb])
                k.op("dve", lambda e: e.tensor_tensor(out=on[:, 0:N], in0=num[0:64, 0:N], in1=rb[:, 0:N], op=ALU.mult), r=[num, rb], w=[on])
                k.dma(self.AO.ap()[hd * 64:(hd + 1) * 64, q0:q0 + N], on[:, 0:N], w=[("AO", hd, q0)])
        k.pop()


    def s5_core(self, i):
        k = self.k
        L = 256
        NTL = T // L
        k.push()
        TT = lambda e, o, a, b, op: e.tensor_tensor(out=o, in0=a, in1=b, op=op)
        wg = k.sb("wglu", [128, NCH, D], BF16)
        idf = k.sb("idf", [128, 128], F32)
        k.dma(idf[:], self.c_ident.ap())
        prm = {}
        for nm, src, shp in (("are", self.s5_are, [128, 2, 32]), ("aim", self.s5_aim, [128, 2, 32]), ("ldt", self.s5_ldt, [128, 2, 32]),
                                                          ("d", self.s5_d, [128, NCH]), ("bglu", self.s5_bglu, [128, NCH])):
            t_ = k.sb("s5" + nm, shp, F32)
            k.dma(t_[:], src.ap())
            prm[nm] = t_
        for nm in ("bre", "bim", "cre", "cim"):
            prm[nm] = k.sb("s5" + nm, [128, 32, 16], F32)
        hpi = k.sb("hpi", [128, 1], F32)
        k.op("dve", lambda e: e.memset(hpi[:], math.pi / 2), w=[hpi])
        sm = {nm: k.sb("s5" + nm, [128, 32], F32) for nm in ("dt", "rho", "th", "c", "s", "cc", "ss", "nr", "ni", "inv", "gr", "gi", "t0", "t1")}
        bbr = k.sb("bbr", [128, 32, 16], F32)
        bbi = k.sb("bbi", [128, 32, 16], F32)
        bt = k.sb("bt", [128, 32, 16], F32)
        ZA = k.sb("ZA", [128, 32, 128], F32)
        ZB = k.sb("ZB", [128, 32, 128], F32)
        stage = [Z_[:, 0:16, :].rearrange("p (a b) n -> p a (b n)", b=2) for Z_ in (ZA, ZB)]
        self._stg = 0
        self.wload(wg, self.s5_wglu.ap(), NCH, D, stage=stage)
        WBr = k.sb("WBr", [128, 32, 128], BF16)
        WBi = k.sb("WBi", [128, 32, 128], BF16)
        ZCr = k.sb("ZCr", [128, 32, 128], BF16)
        ZCi = k.sb("ZCi", [128, 32, 128], BF16)
        k.op("pool", lambda e: e.memset(ZCr[:], 0.0), w=[ZCr])
        k.op("pool", lambda e: e.memset(ZCi[:], 0.0), w=[ZCi])
        Tc = k.sb("Tc", [128, 32, L], F32)
        Ts = k.sb("Ts", [128, 32, L], F32)
        car = k.sb("car", [128, 32], F32)
        cai = k.sb("cai", [128, 32], F32)
        hts = [k.sb("s5ht%d" % b, [128, NCH, L], BF16) for b in range(1)] * 2
        wk = {nm: [k.sb("s5w%s%d" % (nm, b), [128, L], F32) for b in range(2)] for nm in ("a", "b", "gr", "gi", "rr", "ri", "hr", "hi")}
        hb = {nm: [k.sb("s5h%s%d" % (nm, b), [128, L], BF16) for b in range(2)] for nm in ("r", "i")}
        wcs = [k.sb("s5wc%d" % b, [128, L], F32) for b in range(2)]
        wds = [k.sb("s5wd%d" % b, [128, L], F32) for b in range(2)]
        ytot = k.sb("ytot", [128, NCH, L], F32)
        zb = k.sb("zb", [128, NCH, L], BF16)
        sg = [k.sb("s5sg%d" % b, [128, L], F32) for b in range(1)] * 2
        yo = [k.sb("s5yo%d" % b, [128, L], F32) for b in range(1)] * 2
        ps_x = [(self.psb[1], self.psb[2]), (self.psb[3], self.psb[4])]
        ps_tr = self.psb[0]

        def bc16(ap2):
            return ap2.unsqueeze(2).to_broadcast([128, 32, 16])

        def rev(t_, n):
            return bass.AP(t_, n - 1, [[t_[:].ap[0][0], 128], [-1, n]])

        nw = 0
        for dr in range(2):
            A = sm
            for nm, src in (("bre", self.s5_bre), ("bim", self.s5_bim), ("cre", self.s5_cre), ("cim", self.s5_cim)):
                k.dma(prm[nm][:], src.ap()[:, dr])
            k.op("act", lambda e: e.activation(out=A["dt"][:], in_=prm["ldt"][:, dr], func=AF.Exp), r=[prm["ldt"]], w=[A["dt"]])
            k.op("dve", lambda e: TT(e, A["t0"][:], prm["are"][:, dr], A["dt"][:], ALU.mult), r=[prm["are"], A["dt"]], w=[A["t0"]])
            k.op("act", lambda e: e.activation(out=A["rho"][:], in_=A["t0"][:], func=AF.Exp), r=[A["t0"]], w=[A["rho"]])
            k.op("dve", lambda e: TT(e, A["th"][:], prm["aim"][:, dr], A["dt"][:], ALU.mult), r=[prm["aim"], A["dt"]], w=[A["th"]])
            k.op("act", lambda e: e.activation(out=A["s"][:], in_=A["th"][:], func=AF.Sin, scale=1.0 / 16), r=[A["th"]], w=[A["s"]])
            k.op("act", lambda e: e.activation(out=A["c"][:], in_=A["th"][:], func=AF.Sin, scale=1.0 / 16, bias=hpi[:]), r=[A["th"], hpi], w=[A["c"]])
            for _ in range(4):
                k.op("dve", lambda e: TT(e, A["cc"][:], A["c"][:], A["c"][:], ALU.mult), r=[A["c"]], w=[A["cc"]])
                k.op("dve", lambda e: TT(e, A["ss"][:], A["s"][:], A["s"][:], ALU.mult), r=[A["s"]], w=[A["ss"]])
                k.op("dve", lambda e: e.scalar_tensor_tensor(out=A["s"][:], in0=A["c"][:], scalar=2.0, in1=A["s"][:], op0=ALU.mult, op1=ALU.mult), r=[A["c"], A["s"]], w=[A["s"]])
                k.op("dve", lambda e: TT(e, A["c"][:], A["cc"][:], A["ss"][:], ALU.subtract), r=[A["cc"], A["ss"]], w=[A["c"]])
            k.op("dve", lambda e: TT(e, A["nr"][:], A["rho"][:], A["c"][:], ALU.mult), r=[A["rho"], A["c"]], w=[A["nr"]])
            k.op("dve", lambda e: e.tensor_scalar(out=A["nr"][:], in0=A["nr"][:], scalar1=-1.0, scalar2=None, op0=ALU.add), r=[A["nr"]], w=[A["nr"]])
            k.op("dve", lambda e: TT(e, A["ni"][:], A["rho"][:], A["s"][:], ALU.mult), r=[A["rho"], A["s"]], w=[A["ni"]])
            k.op("dve", lambda e: TT(e, A["t0"][:], prm["are"][:, dr], prm["are"][:, dr], ALU.mult), r=[prm["are"]], w=[A["t0"]])
            k.op("dve", lambda e: TT(e, A["t1"][:], prm["aim"][:, dr], prm["aim"][:, dr], ALU.mult), r=[prm["aim"]], w=[A["t1"]])
            k.op("dve", lambda e: TT(e, A["inv"][:], A["t0"][:], A["t1"][:], ALU.add), r=[A["t0"], A["t1"]], w=[A["inv"]])
            k.op("dve", lambda e: e.reciprocal(out=A["inv"][:], in_=A["inv"][:]), r=[A["inv"]], w=[A["inv"]])
            k.op("dve", lambda e: TT(e, A["t0"][:], A["nr"][:], prm["are"][:, dr], ALU.mult), r=[A["nr"], prm["are"]], w=[A["t0"]])
            k.op("dve", lambda e: TT(e, A["t1"][:], A["ni"][:], prm["aim"][:, dr], ALU.mult), r=[A["ni"], prm["aim"]], w=[A["t1"]])
            k.op("dve", lambda e: TT(e, A["gr"][:], A["t0"][:], A["t1"][:], ALU.add), r=[A["t0"], A["t1"]], w=[A["gr"]])
            k.op("dve", lambda e: TT(e, A["gr"][:], A["gr"][:], A["inv"][:], ALU.mult), r=[A["gr"], A["inv"]], w=[A["gr"]])
            k.op("dve", lambda e: TT(e, A["t0"][:], A["ni"][:], prm["are"][:, dr], ALU.mult), r=[A["ni"], prm["are"]], w=[A["t0"]])
            k.op("dve", lambda e: TT(e, A["t1"][:], A["nr"][:], prm["aim"][:, dr], ALU.mult), r=[A["nr"], prm["aim"]], w=[A["t1"]])
            k.op("dve", lambda e: TT(e, A["gi"][:], A["t0"][:], A["t1"][:], ALU.subtract), r=[A["t0"], A["t1"]], w=[A["gi"]])
            k.op("dve", lambda e: TT(e, A["gi"][:], A["gi"][:], A["inv"][:], ALU.mult), r=[A["gi"], A["inv"]], w=[A["gi"]])
            k.op("dve", lambda e: TT(e, bbr[:], prm["bre"][:], bc16(A["gr"][:]), ALU.mult), r=[prm["bre"], A["gr"]], w=[bbr])
            k.op("dve", lambda e: TT(e, bt[:], prm["bim"][:], bc16(A["gi"][:]), ALU.mult), r=[prm["bim"], A["gi"]], w=[bt])
            k.op("dve", lambda e: TT(e, bbr[:], bbr[:], bt[:], ALU.subtract), r=[bbr, bt], w=[bbr])
            k.op("dve", lambda e: TT(e, bbi[:], prm["bim"][:], bc16(A["gr"][:]), ALU.mult), r=[prm["bim"], A["gr"]], w=[bbi])
            k.op("dve", lambda e: TT(e, bt[:], prm["bre"][:], bc16(A["gi"][:]), ALU.mult), r=[prm["bre"], A["gi"]], w=[bt])
            k.op("dve", lambda e: TT(e, bbi[:], bbi[:], bt[:], ALU.add), r=[bbi, bt], w=[bbi])
            k.op("pool", lambda e: e.memset(ZA[:], 0.0), w=[ZA])
            k.op("pool", lambda e: e.memset(ZB[:], 0.0), w=[ZB])
            for gl in range(2):
                for r4 in range(4):
                    c0 = (2 * r4 + gl) * 16
                    pr_ = slice(gl * 64, (gl + 1) * 64)
                    k.op("dve", lambda e: e.tensor_copy(out=ZA[pr_, r4::4, c0:c0 + 16], in_=bbr[pr_, r4::4, :]), r=[bbr], w=[ZA])
                    k.op("dve", lambda e: e.tensor_copy(out=ZB[pr_, r4::4, c0:c0 + 16], in_=bbi[pr_, r4::4, :]), r=[bbi], w=[ZB])
                    k.op("dve", lambda e: e.tensor_copy(out=ZCr[pr_, r4::4, c0:c0 + 16], in_=prm["cre"][pr_, r4::4, :]), r=[prm["cre"]], w=[ZCr])
                    k.op("dve", lambda e: e.tensor_scalar(out=ZCi[pr_, r4::4, c0:c0 + 16], in0=prm["cim"][pr_, r4::4, :], scalar1=-1.0, scalar2=None, op0=ALU.mult),
                         r=[prm["cim"]], w=[ZCi])
            for st in range(32):
                for Z, W in ((ZA, WBr), (ZB, WBi)):
                    k.op("pe", lambda e: e.transpose(out=ps_tr[:, 0:128], in_=Z[:, st, :], identity=idf[:]), r=[Z, idf], w=[ps_tr])
                    k.op("act", lambda e: e.activation(out=W[:, st, :], in_=ps_tr[:, 0:128], func=AF.Identity), r=[ps_tr], w=[W])
            k.op("dve", lambda e: e.tensor_copy(out=Tc[:, :, 0], in_=A["c"][:]), r=[A["c"]], w=[Tc])
            k.op("dve", lambda e: e.tensor_copy(out=Ts[:, :, 0], in_=A["s"][:]), r=[A["s"]], w=[Ts])
            m = 1
            while m < L:
                pc = Tc[:, :, m - 1:m].to_broadcast([128, 32, m])
                pS = Ts[:, :, m - 1:m].to_broadcast([128, 32, m])
                k.op("dve", lambda e: TT(e, ZA[:, :, 0:m], Tc[:, :, 0:m], pc, ALU.mult), r=[Tc, WBr, WBi], w=[ZA])
                k.op("dve", lambda e: TT(e, ZB[:, :, 0:m], Ts[:, :, 0:m], pS, ALU.mult), r=[Ts, Tc], w=[ZB])
                k.op("dve", lambda e: TT(e, Tc[:, :, m:2 * m], ZA[:, :, 0:m], ZB[:, :, 0:m], ALU.subtract), r=[ZA, ZB], w=[Tc])
                k.op("dve", lambda e: TT(e, ZA[:, :, 0:m], Tc[:, :, 0:m], pS, ALU.mult), r=[Tc, Ts], w=[ZA])
                k.op("dve", lambda e: TT(e, ZB[:, :, 0:m], Ts[:, :, 0:m], pc, ALU.mult), r=[Ts, Tc], w=[ZB])
                k.op("dve", lambda e: TT(e, Ts[:, :, m:2 * m], ZA[:, :, 0:m], ZB[:, :, 0:m], ALU.add), r=[ZA, ZB], w=[Ts])
                m *= 2
            if dr == 1:
                H2 = L // 2
                for Tt in (Tc, Ts):
                    up = bass.AP(Tt, L - 1, [[32 * L, 128], [L, 32], [-1, H2]])
                    lo = bass.AP(Tt, H2 - 1, [[32 * L, 128], [L, 32], [-1, H2]])
                    k.op("dve", lambda e: e.tensor_copy(out=ZA[:, :, 0:H2], in_=up), r=[Tt], w=[ZA])
                    k.op("dve", lambda e: e.tensor_copy(out=Tt[:, :, H2:L], in_=lo), r=[Tt, ZA], w=[Tt])
                    k.op("dve", lambda e: e.tensor_copy(out=Tt[:, :, 0:H2], in_=ZA[:, :, 0:H2]), r=[ZA], w=[Tt])
            k.op("dve", lambda e: e.memset(car[:], 0.0), w=[car])
            k.op("dve", lambda e: e.memset(cai[:], 0.0), w=[cai])
            order = list(range(NTL)) if dr == 0 else [0] + list(range(NTL - 1, 0, -1))
            for oi, tl in enumerate(order):
                t0 = tl * L
                ht = hts[oi % 2]
                k.dma(ht[:], self.Hs.ap()[:, t0:t0 + L].rearrange("(c p) n -> p c n", p=128))
                if dr == 1:
                    k.dma(ytot[:], self.Ys.ap()[:, t0:t0 + L].rearrange("(c p) n -> p c n", p=128), w=[(ytot.name, c_) for c_ in range(NCH)])
                def emit_x(st_, b_):
                    xr_, xi_ = ps_x[b_]
                    c_ = st_ // 4
                    k.op("pe", lambda e: e.matmul(out=xr_[:, 0:L], lhsT=WBr[:, st_, :], rhs=ht[:, c_, :], start=True, stop=True), r=[WBr, ht], w=[xr_])
                    k.op("pe", lambda e: e.matmul(out=xi_[:, 0:L], lhsT=WBi[:, st_, :], rhs=ht[:, c_, :], start=True, stop=True), r=[WBi, ht], w=[xi_])
                emit_x(0, nw % 2)
                for c in range(NCH):
                    py = self.psb[5 + c % 2]
                    for s4 in range(4):
                        st = 4 * c + s4
                        b = nw % 2
                        nw += 1
                        pxr, pxi = ps_x[b]
                        wa, wb_, gr, gi, rr, ri, hr, hi = (wk[n_][b] for n_ in ("a", "b", "gr", "gi", "rr", "ri", "hr", "hi"))
                        wc, wd = wcs[b], wds[b]
                        tc, ts = Tc[:, st, :], Ts[:, st, :]
                        if dr == 0:
                            sc_out = lambda t_: t_[:]
                            last = L - 1
                        else:
                            sc_out = lambda t_: bass.AP(t_, L - 1, [[L, 128], [-1, L]])
                            last = 0
                        k.op("dve", lambda e: TT(e, wa[:], pxr[:, 0:L], tc, ALU.mult), r=[pxr, Tc], w=[wa])
                        k.op("dve", lambda e: TT(e, wb_[:], pxi[:, 0:L], ts, ALU.mult), r=[pxi, Ts], w=[wb_])
                        k.op("dve", lambda e: TT(e, gr[:], wa[:], wb_[:], ALU.add), r=[wa, wb_], w=[gr])
                        k.op("dve", lambda e: TT(e, wa[:], pxi[:, 0:L], tc, ALU.mult), r=[pxi, Tc], w=[wa])
                        k.op("dve", lambda e: TT(e, wb_[:], pxr[:, 0:L], ts, ALU.mult), r=[pxr, Ts], w=[wb_])
                        k.op("dve", lambda e: TT(e, gi[:], wa[:], wb_[:], ALU.subtract), r=[wa, wb_], w=[gi])
                        rho_b = A["rho"][:, st:st + 1].to_broadcast([128, L])
                        k.op("dve", lambda e: e.tensor_tensor_scan(out=sc_out(rr), data0=rho_b, data1=sc_out(gr), initial=car[:, st:st + 1], op0=ALU.mult, op1=ALU.add),
                             r=[gr, A["rho"], car], w=[rr])
                        k.op("dve", lambda e: e.tensor_tensor_scan(out=sc_out(ri), data0=rho_b, data1=sc_out(gi), initial=cai[:, st:st + 1], op0=ALU.mult, op1=ALU.add),
                             r=[gi, A["rho"], cai], w=[ri])
                        k.op("pool", lambda e: TT(e, wc[:], rr[:], tc, ALU.mult), r=[rr, Tc], w=[wc])
                        k.op("pool", lambda e: TT(e, wd[:], ri[:], ts, ALU.mult), r=[ri, Ts], w=[wd])
                        k.op("pool", lambda e: TT(e, hr[:], wc[:], wd[:], ALU.subtract), r=[wc, wd], w=[hr])
                        k.op("pool", lambda e: TT(e, wc[:], rr[:], ts, ALU.mult), r=[rr, Ts], w=[wc])
                        k.op("pool", lambda e: TT(e, wd[:], ri[:], tc, ALU.mult), r=[ri, Tc], w=[wd])
                        k.op("pool", lambda e: TT(e, hi[:], wc[:], wd[:], ALU.add), r=[wc, wd], w=[hi])
                        k.op("pool", lambda e: e.tensor_copy(out=car[:, st:st + 1], in_=hr[:, last:last + 1]), r=[hr], w=[car])
                        k.op("pool", lambda e: e.tensor_copy(out=cai[:, st:st + 1], in_=hi[:, last:last + 1]), r=[hi], w=[cai])
                        hbr, hbi = hb["r"][b], hb["i"][b]
                        k.op("act", lambda e: e.activation(out=hbr[:], in_=hr[:], func=AF.Identity), r=[hr], w=[hbr])
                        k.op("act", lambda e: e.activation(out=hbi[:], in_=hi[:], func=AF.Identity), r=[hi], w=[hbi])
                        if st + 1 < 32:
                            emit_x(st + 1, nw % 2)
                        k.op("pe", lambda e: e.matmul(out=py[:, 0:L], lhsT=ZCr[:, st, :], rhs=hbr[:], start=(s4 == 0), stop=False), r=[ZCr, hbr], w=[py])
                        k.op("pe", lambda e: e.matmul(out=py[:, 0:L], lhsT=ZCi[:, st, :], rhs=hbi[:], start=False, stop=(s4 == 3)), r=[ZCi, hbi], w=[py])
                    if dr == 0:
                        y = yo[c % 2]
                        k.op("dve", lambda e: e.tensor_copy(out=y[:], in_=py[:, 0:L]), r=[py], w=[y])
                        k.dma(self.Ys.ap()[c * 128:(c + 1) * 128, t0:t0 + L], y[:], w=[("Ysp", c, t0)])
                    else:
                        k.op("dve", lambda e: TT(e, ytot[:, c, :], py[:, 0:L], ytot[:, c, :], ALU.add), r=[py, (ytot.name, c)], w=[(ytot.name, c)])
                        k.op("dve", lambda e: e.scalar_tensor_tensor(out=ytot[:, c, :], in0=ht[:, c, :], scalar=prm["d"][:, c:c + 1], in1=ytot[:, c, :], op0=ALU.mult, op1=ALU.add),
                             r=[ht, prm["d"], (ytot.name, c)], w=[(ytot.name, c)])
                        k.op("act", lambda e: e.activation(out=ytot[:, c, :], in_=ytot[:, c, :], func=AF.Gelu), r=[(ytot.name, c)], w=[(ytot.name, c)])
                        k.op("act", lambda e: e.activation(out=zb[:, c, :], in_=ytot[:, c, :], func=AF.Identity), r=[(ytot.name, c)], w=[(zb.name, c)])
                if dr == 1:
                    for c2 in range(NCH):
                        pu = self.psb[5 + c2 % 2]
                        for c in range(NCH):
                            k.op("pe", lambda e: e.matmul(out=pu[:, 0:L], lhsT=wg[:, c, c2 * 128:(c2 + 1) * 128], rhs=zb[:, c, :], start=(c == 0), stop=(c == NCH - 1)),
                                 r=[wg, (zb.name, c)], w=[pu])
                        sg_ = sg[c2 % 2]
                        y = yo[c2 % 2]
                        k.op("act", lambda e: e.activation(out=sg_[:], in_=pu[:, 0:L], func=AF.Sigmoid, bias=prm["bglu"][:, c2:c2 + 1], scale=1.0), r=[pu, prm["bglu"]], w=[sg_])
                        k.op("dve", lambda e: TT(e, y[:], ytot[:, c2, :], sg_[:], ALU.mult), r=[(ytot.name, c2), sg_], w=[y])
                        k.dma(self.Ys.ap()[c2 * 128:(c2 + 1) * 128, t0:t0 + L], y[:], w=[("Ysf", c2, t0)])
        k.pop()

    def hg_proj(self, i):
        k = self.k
        NT = 512
        k.push()
        w = k.sb("wqig", [128, NCH, 3 * D], BF16)
        wf = k.sb("wf", [128, NCH, 2 * D], BF16)
        stage = [k.sb("wstg%d" % b, [128, 8, 256], F32) for b in range(2)]
        self._stg = 0
        self.wload(w, self.hg_wqig.ap(), NCH, 3 * D, stage=stage)
        for dr in range(2):
            for c in range(0, D, 256):
                st = stage[self._stg % 2]
                self._stg += 1
                k.dma(st[:, :, :], self.hg_wf.ap()[dr][:, c:c + 256].rearrange("(k p) n -> p k n", p=128))
                k.op("pool", lambda e: e.tensor_copy(out=wf[:, :, dr * D + c:dr * D + c + 256], in_=st[:, :, :]), r=[st], w=[wf])
        bfm = k.sb("bfm", [128, 2, NCH], F32)
        k.dma(bfm[:], self.hg_bf.ap())
        lg = k.sb("lblg", [128, DEPTH, NCH], F32)
        k.dma(lg[:], self.hg_lb.ap())
        k.op("act", lambda e: e.activation(out=lg[:], in_=lg[:], func=AF.Exp), r=[lg], w=[lg])
        ssum = k.sb("lbsum", [128, NCH], F32)
        lb = k.sb("lb", [128, NCH], F32)
        oml = k.sb("oml", [128, NCH], F32)
        k.op("dve", lambda e: e.tensor_tensor(out=ssum[:], in0=lg[:, 0], in1=lg[:, 1], op=ALU.add), r=[lg], w=[ssum])
        for l in (2, 3):
            k.op("dve", lambda e: e.tensor_tensor(out=ssum[:], in0=ssum[:], in1=lg[:, l], op=ALU.add), r=[lg, ssum], w=[ssum])
        k.op("dve", lambda e: e.reciprocal(out=ssum[:], in_=ssum[:]), r=[ssum], w=[ssum])
        k.op("dve", lambda e: e.memset(lb[:], 0.0), w=[lb])
        for l in range(1, i + 1):
            k.op("dve", lambda e: e.tensor_tensor(out=lb[:], in0=lb[:], in1=lg[:, l], op=ALU.add), r=[lg, lb], w=[lb])
        k.op("dve", lambda e: e.tensor_tensor(out=lb[:], in0=lb[:], in1=ssum[:], op=ALU.mult), r=[lb, ssum], w=[lb])
        k.op("dve", lambda e: e.tensor_scalar(out=oml[:], in0=lb[:], scalar1=-1.0, scalar2=1.0, op0=ALU.mult, op1=ALU.add), r=[lb], w=[oml])
        self.hg_lbt = None
        hts = [k.sb("ht%d" % b, [128, NCH, NT], BF16) for b in range(2)]
        qo = [k.sb("qo%d" % b, [128, NT], BF16) for b in range(2)]
        sg = [k.sb("sg%d" % b, [128, NT], F32) for b in range(2)]
        fo = [k.sb("fo%d" % b, [128, NT], F32) for b in range(2)]
        vo = [k.sb("vo%d" % b, [128, 512], BF16) for b in range(2)]
        nv = 0
        for ti, (t0, N, v) in enumerate(self.token_tiles(True, NT)):
            ht = hts[ti % 2]
            k.dma(ht[:, :, 0:N], self.Hs.ap()[:, t0:t0 + N].rearrange("(c p) n -> p c n", p=128))
            for cc in range(16):
                b = cc % 2
                ps = self.psb[1 + b]
                col0 = cc * 128 if cc < 8 else 2 * D + (cc - 8) * 128
                for c in range(NCH):
                    k.op("pe", lambda e: e.matmul(out=ps[:, 0:N], lhsT=w[:, c, col0:col0 + 128], rhs=ht[:, c, 0:N],
                                                  start=(c == 0), stop=(c == NCH - 1)), r=[w, ht], w=[ps])
                fn = AF.Identity if cc < 8 else AF.Silu
                k.op("act", lambda e: e.activation(out=qo[b][:, 0:N], in_=ps[:, 0:N], func=fn), r=[ps], w=[qo[b]])
                k.dma(self.QKs.ap()[cc * 128:(cc + 1) * 128, t0:t0 + N], qo[b][:, 0:N], w=[("QKs", cc, t0)])
            for dr in range(2):
                for cc in range(8):
                    b = cc % 2
                    ps = self.psb[3 + b]
                    for c in range(NCH):
                        k.op("pe", lambda e: e.matmul(out=ps[:, 0:N], lhsT=wf[:, c, dr * D + cc * 128:dr * D + (cc + 1) * 128], rhs=ht[:, c, 0:N],
                                                      start=(c == 0), stop=(c == NCH - 1)), r=[wf, ht], w=[ps])
                    k.op("act", lambda e: e.activation(out=sg[b][:, 0:N], in_=ps[:, 0:N], func=AF.Sigmoid, bias=bfm[:, dr, cc:cc + 1], scale=1.0), r=[ps, bfm], w=[sg[b]])
                    k.op("dve", lambda e: e.tensor_scalar(out=fo[b][:, 0:N], in0=sg[b][:, 0:N], scalar1=oml[:, cc:cc + 1], scalar2=lb[:, cc:cc + 1],
                                                          op0=ALU.mult, op1=ALU.add), r=[sg[b], oml, lb], w=[fo[b]])
                    k.dma(self.Fs.ap()[dr, cc * 128:(cc + 1) * 128, t0:t0 + N], fo[b][:, 0:N], w=[("Fs", dr, cc, t0)])
            for sub in range(N // 128):
                for vb in range(2):
                    ps = self.psb[5 + vb]
                    for c in range(NCH):
                        k.op("pe", lambda e: e.matmul(out=ps[:, :], lhsT=ht[:, c, sub * 128:(sub + 1) * 128], rhs=w[:, c, D + vb * 512:D + (vb + 1) * 512],
                                                      start=(c == 0), stop=(c == NCH - 1)), r=[w, ht], w=[ps])
                    vv = vo[nv % 2]
                    nv += 1
                    k.op("act", lambda e: e.activation(out=vv[:], in_=ps[:, :], func=AF.Identity), r=[ps], w=[vv])
                    k.dma(self.Vs.ap()[t0 + sub * 128:t0 + (sub + 1) * 128, vb * 512:(vb + 1) * 512], vv[:], w=[("Vs", t0, sub, vb)])
        k.pop()

    def hg_core(self, i):
        k = self.k
        CH = 128
        NKT = T // CH
        k.push()
        idf = k.sb("idf", [128, 128], F32)
        ident = k.sb("ident", [128, 128], BF16)
        k.dma(idf[:], self.c_ident.ap())
        k.op("dve", lambda e: e.tensor_copy(out=ident[:], in_=idf[:]), r=[idf], w=[ident])
        masks = []
        for nm, src in (("triu", self.c_triu), ("tril", self.c_tril)):
            mk = k.sb(nm, [128, 128], F32)
            k.dma(mk[:], src.ap())
            masks.append(mk)
        gn = k.sb("gn", [128, 1], F32)
        k.dma(gn[:], self.hg_gn.ap())
        zeros = k.sb("zeros", [128, CH], F32)
        k.op("dve", lambda e: e.memset(zeros[:], 0.0), w=[zeros])
        qT = k.sb("hqT", [128, T], BF16)
        gT = k.sb("hgT", [128, T], BF16)
        vh = k.sb("hvh", [128, NKT, 128], BF16)
        f = k.sb("hf", [128, T], F32)
        P = k.sb("hP", [128, T], F32)
        kinv = k.sb("hkinv", [128, T], F32)
        qdec = k.sb("hqdec", [128, T], BF16)
        kinvb = k.sb("hkinvb", [128, T], BF16)
        Oacc = k.sb("hO", [128, S], F32)
        Sst = k.sb("hS", [128, 128], F32)
        Sbf = [k.sb("hSbf%d" % b, [128, 128], BF16) for b in range(2)]
        kdec = [k.sb("hkdec%d" % b, [128, CH], BF16) for b in range(2)]
        kdt = [k.sb("hkdt%d" % b, [128, CH], BF16) for b in range(2)]
        attm = [k.sb("hattm%d" % b, [128, CH], BF16) for b in range(2)]
        sq = k.sb("hsq", [128, 1, 512], BF16)
        rstd = k.sb("hrstd", [128, 512], F32)
        ons = [k.sb("hon%d" % b, [128, 512], BF16) for b in range(2)]
        ps_att, ps_o, ps_ds = self.psb[1], self.psb[2], self.psb[3]
        ps_tr = self.ps_bf[:, 0:128]
        nb = 0
        for hd in range(8):
            k.dma(qT[:], self.QKs.ap()[hd * 128:(hd + 1) * 128, :])
            k.dma(gT[:], self.QKs.ap()[D + hd * 128:D + (hd + 1) * 128, :])
            k.dma(vh[:], self.Vs.ap()[:, hd * 128:(hd + 1) * 128].rearrange("(kt p) d -> p kt d", p=128))
            for dr in range(2):
                k.dma(f[:], self.Fs.ap()[dr, hd * 128:(hd + 1) * 128, :])
                order = list(range(NKT)) if dr == 0 else [1, 0] + list(range(NKT - 1, 1, -1))
                for kt in range(NKT):
                    c0 = kt * CH
                    if dr == 0:
                        fa, pa = f[:, c0:c0 + CH], P[:, c0:c0 + CH]
                    else:
                        fa = bass.AP(f, c0 + CH - 1, [[T, 128], [-1, CH]])
                        pa = bass.AP(P, c0 + CH - 1, [[T, 128], [-1, CH]])
                    k.op("dve", lambda e: e.tensor_tensor_scan(out=pa, data0=fa, data1=zeros[:], initial=1.0, op0=ALU.mult, op1=ALU.add),
                         r=[f, zeros], w=[P])
                k.op("dve", lambda e: e.reciprocal(out=kinv[:], in_=P[:]), r=[P], w=[kinv])
                k.op("pool", lambda e: e.tensor_scalar(out=f[:], in0=f[:], scalar1=-1.0, scalar2=1.0, op0=ALU.mult, op1=ALU.add), r=[f], w=[f])
                k.op("dve", lambda e: e.tensor_tensor(out=kinv[:], in0=kinv[:], in1=f[:], op=ALU.mult), r=[kinv, f], w=[kinv])
                k.op("pool", lambda e: e.tensor_tensor(out=qdec[:], in0=qT[:], in1=P[:], op=ALU.mult), r=[qT, P], w=[qdec])
                k.op("act", lambda e: e.activation(out=kinvb[:], in_=kinv[:], func=AF.Identity), r=[kinv], w=[kinvb])
                k.op("dve", lambda e: e.memset(Sst[:], 0.0), w=[Sst])
                k.op("pool", lambda e: e.memset(Sbf[0][:], 0.0), w=[Sbf[0]])
                si = 0
                mask = masks[dr]
                for kt in order:
                    c0 = kt * CH
                    plast = P[:, c0 + CH - 1:c0 + CH] if dr == 0 else P[:, c0:c0 + 1]
                    Scur = Sbf[si % 2]
                    if kt >= 2:
                        l0 = c0 - LC
                        am = attm[nb % 2]
                        k.op("pe", lambda e: e.matmul(out=ps_att[:, 0:CH], lhsT=kinvb[:, c0:c0 + CH], rhs=qdec[:, c0:c0 + CH], start=True, stop=True),
                             r=[kinvb, qdec], w=[ps_att])
                        k.op("dve", lambda e: e.tensor_tensor(out=am[:], in0=ps_att[:, 0:CH], in1=mask[:], op=ALU.mult), r=[ps_att, mask], w=[am])
                        k.op("pe", lambda e: e.matmul(out=ps_o[:, 0:CH], lhsT=vh[:, kt, :], rhs=am[:], start=True, stop=False), r=[vh, am], w=[ps_o])
                        k.op("pe", lambda e: e.matmul(out=ps_o[:, 0:CH], lhsT=Scur[:], rhs=qdec[:, c0:c0 + CH], start=False, stop=True), r=[Scur, qdec], w=[ps_o])
                        if dr == 0:
                            k.op("dve", lambda e: e.tensor_copy(out=Oacc[:, l0:l0 + CH], in_=ps_o[:, 0:CH]), r=[ps_o], w=[Oacc])
                        else:
                            k.op("dve", lambda e: e.tensor_tensor(out=Oacc[:, l0:l0 + CH], in0=ps_o[:, 0:CH], in1=Oacc[:, l0:l0 + CH], op=ALU.add), r=[ps_o, Oacc], w=[Oacc])
                    kd, kt_ = kdec[nb % 2], kdt[nb % 2]
                    nb += 1
                    k.op("dve", lambda e: e.tensor_scalar(out=kd[:], in0=kinv[:, c0:c0 + CH], scalar1=plast, scalar2=None, op0=ALU.mult), r=[kinv, P], w=[kd])
                    k.op("pe", lambda e: e.transpose(out=ps_tr, in_=kd[:], identity=ident[:]), r=[kd, ident], w=["pstr"])
                    k.op("act", lambda e: e.activation(out=kt_[:], in_=ps_tr, func=AF.Identity), r=["pstr"], w=[kt_])
                    k.op("pe", lambda e: e.matmul(out=ps_ds[:, 0:128], lhsT=kt_[:], rhs=vh[:, kt, :], start=True, stop=True), r=[kt_, vh], w=[ps_ds])
                    k.op("dve", lambda e: e.tensor_scalar(out=Sst[:], in0=Sst[:], scalar1=plast, scalar2=None, op0=ALU.mult), r=[Sst, P], w=[Sst])
                    k.op("dve", lambda e: e.tensor_tensor(out=Sst[:], in0=ps_ds[:, 0:128], in1=Sst[:], op=ALU.add), r=[ps_ds, Sst], w=[Sst])
                    si += 1
                    k.op("act", lambda e: e.activation(out=Sbf[si % 2][:], in_=Sst[:], func=AF.Identity), r=[Sst], w=[Sbf[si % 2]])
            for qb in range(8):
                l0 = qb * 512
                k.op("act", lambda e: e.activation(out=sq[:, 0, :], in_=Oacc[:, l0:l0 + 512], func=AF.Square), r=[Oacc], w=[sq])
                self.rstd_from(sq, 1, 512, rstd, self.psb[6], 128)
                on = ons[qb % 2]
                k.op("dve", lambda e: e.scalar_tensor_tensor(out=rstd[:], in0=Oacc[:, l0:l0 + 512], scalar=gn[:, 0:1], in1=rstd[:], op0=ALU.mult, op1=ALU.mult), r=[Oacc, gn, rstd], w=[rstd])
                k.op("dve", lambda e: e.tensor_tensor(out=on[:], in0=rstd[:], in1=gT[:, LC + l0:LC + l0 + 512], op=ALU.mult), r=[rstd, gT], w=[on])
                k.dma(self.AO.ap()[hd * 128:(hd + 1) * 128, LC + l0:LC + l0 + 512], on[:], w=[("AO", hd, l0)])
        k.pop()

    def ffn(self, i, which, j, src, dst, with_ctx, final=False):
        k = self.k
        NT = 256
        k.push()
        w1 = k.sb("w1", [128, NCH, DFF], BF16)
        w3 = k.sb("w3", [128, NCH, DFF], BF16)
        w2 = k.sb("w2", [128, NFF, D], BF16)
        stage = [k.sb("wstg%d" % b, [128, 8, 256], F32) for b in range(2)]
        self._stg = 0
        self.wload(w1, self.w_ff1.ap()[i, which], NCH, DFF, stage=stage)
        self.wload(w3, self.w_ff3.ap()[i, which], NCH, DFF, stage=stage)
        self.wload(w2, self.w_ff2.ap()[i, which], NFF, D, stage=stage)
        xts = [k.sb("xt%d" % b, [128, NCH, NT], F32) for b in range(2)]
        sq = k.sb("sq", [128, NCH, NT], BF16)
        h = k.sb("h", [128, NCH, NT], BF16)
        g = k.sb("g", [128, NFF, NT], BF16)
        y = k.sb("y", [128, NCH, NT], F32)
        sl = [k.sb("sl%d" % b, [128, NT], BF16) for b in range(2)]
        rstd = k.sb("rstd", [128, NT], F32)
        psn = self.psb[0]
        tiles = self.token_tiles(with_ctx, NT)
        import os
        dbg = int(os.environ.get("FF_DBG", "0"))
        if dbg:
            tiles = tiles[:dbg]
        for ti, (t0, N, v) in enumerate(tiles):
            xt = xts[ti % 2]
            k.dma(xt[:, :, 0:N], src.ap()[:, t0:t0 + N].rearrange("(c p) n -> p c n", p=128), q="sp")
            step = int(os.environ.get("FF_STEP", "9"))
            if step >= 1:
                self.prenorm(xt, N, i, j, v, sq, y, h, rstd, psn)
            for f in range(NFF if step >= 2 else 0):
                pa = self.psb[1 + (f % 2)]
                pb = self.psb[3 + (f % 2)]
                for c in range(NCH):
                    k.op("pe", lambda e: e.matmul(out=pa[:, 0:N], lhsT=w1[:, c, f * 128:(f + 1) * 128], rhs=h[:, c, 0:N],
                                                  start=(c == 0), stop=(c == NCH - 1)), r=[w1, h], w=[pa])
                for c in range(NCH):
                    k.op("pe", lambda e: e.matmul(out=pb[:, 0:N], lhsT=w3[:, c, f * 128:(f + 1) * 128], rhs=h[:, c, 0:N],
                                                  start=(c == 0), stop=(c == NCH - 1)), r=[w3, h], w=[pb])
                s = sl[f % 2]
                sub = int(os.environ.get("FF_SUB", "9"))
                if sub >= 2:
                    k.op("act", lambda e: e.activation(out=s[:, 0:N], in_=pa[:, 0:N], func=AF.Silu), r=[pa], w=[s])
                if sub >= 3:
                    k.op("dve", lambda e: e.tensor_tensor(out=g[:, f, 0:N], in0=pb[:, 0:N], in1=s[:, 0:N], op=ALU.mult), r=[s, pb], w=[(g.name, f)])
            for c in range(NCH if step >= 3 else 0):
                py = self.psb[5 + (c % 2)]
                for f in range(NFF):
                    k.op("pe", lambda e: e.matmul(out=py[:, 0:N], lhsT=w2[:, f, c * 128:(c + 1) * 128], rhs=g[:, f, 0:N],
                                                  start=(f == 0), stop=(f == NFF - 1)), r=[w2, (g.name, f)], w=[py])
                k.op("dve", lambda e: e.tensor_copy(out=y[:, c, 0:N], in_=py[:, 0:N]), r=[py], w=[y])
                k.op("act", lambda e: e.activation(out=sq[:, c, 0:N], in_=y[:, c, 0:N], func=AF.Square), r=[y], w=[sq])
            if step >= 4:
                self.postres(y, xt, N, i, j, v, sq, rstd, psn)
            if final:
                k.dma(self.out.ap()[:, t0 - LC:t0 - LC + N].rearrange("(c p) n -> p c n", p=128), xt[:, :, 0:N], q="sp")
            else:
                k.dma(dst.ap()[:, t0:t0 + N].rearrange("(c p) n -> p c n", p=128), xt[:, :, 0:N], q="sp",
                      w=[(dst.name, t0)])
        k.pop()


def fm(v):
    v = np.asarray(v, np.float32)
    lead = v.shape[:-1]
    a = v.reshape(lead + (NCH, 128))
    a = np.moveaxis(a, -1, 0)
    return np.ascontiguousarray(a)


_CONST = {}


def consts():
    if _CONST:
        return _CONST
    nfreq = 16
    inv = 10000.0 ** (-np.arange(nfreq, dtype=np.float32) / nfreq)
    t = np.arange(S)
    row = (t // 64).astype(np.float32)
    col = (t % 64).astype(np.float32)
    ang = np.concatenate([row[:, None] * inv, col[:, None] * inv], axis=-1).astype(np.float32)
    p = np.arange(128)
    pair = (p % 64) // 2
    C = np.cos(ang)[:, pair].T
    Sn = np.sin(ang)[:, pair].T
    sign = np.where(p % 2 == 0, -1.0, 1.0)[:, None]
    pm = np.zeros((128, 128), np.float32)
    pm[p ^ 1, p] = 1.0
    a = (p // 64)[:, None, None]
    kc = (p % 64)[:, None, None]
    jj = np.arange(22)[None, :, None]
    qc = np.arange(64)[None, None, :]
    dr = (17 - jj) + a
    c0 = np.clip(qc - 8, 0, 48)
    colok = (kc >= c0) & (kc < c0 + 16)
    NEG = -30000.0
    mI = np.where(colok & (dr >= 3) & (dr <= 10), 0.0, NEG).astype(np.float32).reshape(128, 1408)
    mF = np.where(colok & (dr >= 0) & (dr <= 14), 0.0, NEG).astype(np.float32).reshape(128, 1408)
    dri = np.broadcast_to(np.clip(dr, 0, 14), (128, 22, 64))
    dci = np.broadcast_to(np.clip(kc - qc + 15, 0, 30), (128, 22, 64))
    drv = np.broadcast_to((dr >= 0) & (dr <= 14), (128, 22, 64))
    _CONST.update(triu=np.triu(np.ones((128, 128), np.float32)), tril=np.tril(np.ones((128, 128), np.float32)))
    _CONST.update(C=np.ascontiguousarray(C, np.float32), S=np.ascontiguousarray(Sn * sign, np.float32), pm=pm,
                  ident=np.eye(128, dtype=np.float32), mI=mI, mF=mF, dri=dri, dci=dci, drv=drv)
    return _CONST


def to_sm(a):
    a = np.asarray(a, np.float32)
    rest = a.shape[3:]
    a = a.reshape((2, 32, 2, 64) + rest)
    a = np.moveaxis(a, (2, 3), (0, 1))
    return np.ascontiguousarray(a.reshape((128, 2, 32) + rest))


def make_inputs(inp, b, xin=None):
    cs = consts()
    if xin is None:
        xin = np.ascontiguousarray(np.concatenate([inp["ctx"][b], inp["x"][b]], axis=0).T)
    cT = np.stack([fm(inp["c"][b]), fm(inp["c_ctx"])], axis=-1)
    b_ada = fm(inp["b_ada"].reshape(DEPTH, 9, D)).reshape(128, DEPTH, 72)
    rpb = inp["na_rpb"][0]
    rpbg = rpb[:, cs["dri"], cs["dci"]]
    rpbg = np.where(cs["drv"][None], rpbg, np.float32(0.0)).reshape(16, 128, 1408)
    lam = np.stack([inp["da_lam_q1"][0], inp["da_lam_k1"][0], inp["da_lam_q2"][0], inp["da_lam_k2"][0]], axis=-1)
    return {
        "xin": xin, "cT": np.ascontiguousarray(cT),
        "w_ada": inp["w_ada"], "b_ada": np.ascontiguousarray(b_ada),
        "g_pre": fm(inp["g_pre"]), "g_post": fm(inp["g_post"]),
        "w_ff1": inp["w_ff1"], "w_ff3": inp["w_ff3"], "w_ff2": inp["w_ff2"],
        "da_w_qkv": inp["da_w_qkv"][0], "da_w_o": inp["da_w_o"][0],
        "da_lam": np.ascontiguousarray(lam, np.float32), "da_subln": np.ascontiguousarray(inp["da_subln"][0].reshape(128, 1)),
        "na_w_qkv": inp["na_w_qkv"][0], "na_w_o": inp["na_w_o"][0],
        "na_rpbg": np.ascontiguousarray(rpbg, np.float32),
        "s5_are": to_sm(inp["s5_a_re"][0]), "s5_aim": to_sm(inp["s5_a_im"][0]),
        "s5_ldt": to_sm(np.broadcast_to(inp["s5_log_dt"][0][:, :, None], (2, 64, 64))),
        "s5_bre": to_sm(inp["s5_b_re"][0]), "s5_bim": to_sm(inp["s5_b_im"][0]),
        "s5_cre": to_sm(np.swapaxes(inp["s5_c_re"][0], 2, 3)), "s5_cim": to_sm(np.swapaxes(inp["s5_c_im"][0], 2, 3)),
        "s5_d": fm(inp["s5_d"][0]), "s5_bglu": fm(inp["s5_b_glu"][0]), "s5_w_glu": inp["s5_w_glu"][0],
        "hg_w_qig": inp["hg_w_qig"][0], "hg_w_f": inp["hg_w_f"][0], "hg_b_f": fm(inp["hg_b_f"][0]),
        "hg_lb": fm(inp["hg_lb_logits"]), "hg_gn": np.ascontiguousarray(inp["hg_gnorm"][0].reshape(128, 1)),
        "hg_w_o": inp["hg_w_o"][0], "c_triu": cs["triu"], "c_tril": cs["tril"],
        "c_ropeC": cs["C"], "c_ropeS": cs["S"], "c_pm": cs["pm"], "c_ident": cs["ident"],
        "c_maskI": cs["mI"], "c_maskF": cs["mF"],
    }


def kernel(**inputs):
    inp = {k_: np.asarray(v) for k_, v in inputs.items()}
    prog = Prog()
    in_maps = [make_inputs(inp, b) for b in range(8)]
    res = run_bass_kernel_spmd(prog.nc, in_maps, core_ids=list(range(8)))
    out = np.stack([np.ascontiguousarray(res.results[b]["out"].T) for b in range(8)], axis=0)
    return out.astype(np.float32)
```

```python
import math
import numpy as np
from contextlib import ExitStack
import concourse.bass as bass
import concourse.mybir as mybir
from concourse.bass_utils import run_bass_kernel_spmd

F32 = mybir.dt.float32
BF16 = mybir.dt.bfloat16
AF = mybir.ActivationFunctionType
ALU = mybir.AluOpType

D = 1024
S = 4096
LC = 256
T = S + LC
DFF = 2816
NCH = 8
NFF = 22
DEPTH = 4
EPS = 1e-6
EPOCH = 30000
NRING = 32
NPR = 40


class KB:
    def __init__(self):
        self.nc = bass.Bass("TRN2", target_bir_lowering=False)
        self.es = ExitStack()
        nc = self.nc
        self.eng = {"pe": nc.tensor, "dve": nc.vector, "act": nc.scalar, "pool": nc.gpsimd, "sp": nc.sync}
        self.sems = {e: [] for e in self.eng}
        self.cnt = {e: 0 for e in self.eng}
        self.known = {}
        self.last_w = {}
        self.readers = {}
        self.ring = [self.es.enter_context(nc.semaphore("dr%d" % i)) for i in range(NRING)]
        self.ring_val = [0] * NRING
        self.ndma = 0
        self.nins = {e: 0 for e in self.eng}
        self.scopes = []
        self.pring = [self.es.enter_context(nc.semaphore("pr%d" % i)) for i in range(NPR)]
        self.pr_used = 0
        self.uid = 0
        self.dummy = self.sb("dummy", [128, 1], F32)

    def sb(self, name, shape, dtype):
        self.uid += 1
        t = self.nc.sbuf_tensor("%s_%d" % (name, self.uid), list(shape), dtype)
        st = self.scopes[-1] if self.scopes else self.es
        return st.enter_context(t)

    def ps(self, name, shape, dtype=F32):
        st = self.scopes[-1] if self.scopes else self.es
        return st.enter_context(self.nc.psum_tensor(name, list(shape), dtype))

    def push(self):
        self.scopes.append(ExitStack())

    def pop(self):
        self.barrier()
        self.scopes.pop().close()

    def _cursem(self, e):
        if not self.sems[e] or self.cnt[e] >= EPOCH:
            s = self.es.enter_context(self.nc.semaphore("s_%s_%d" % (e, len(self.sems[e]))))
            self.sems[e].append(s)
            self.cnt[e] = 0
        return len(self.sems[e]) - 1

    @staticmethod
    def _key(x):
        if isinstance(x, (tuple, str)):
            return x
        if hasattr(x, "tensor"):
            return x.tensor.name
        return x.name

    def _wait(self, e, dep):
        if dep[0] == "c":
            _, te, ep, c = dep
            if te == e and e == "pe":
                return
            kk = (e, te, ep)
            if self.known.get(kk, 0) >= c:
                return
            self.eng[e].wait_ge(self.sems[te][ep], c)
            self.known[kk] = c
        elif dep[0] == "p":
            _, slot, val = dep
            kk = (e, "pring", slot)
            if self.known.get(kk, 0) >= val:
                return
            self.eng[e].wait_ge(self.pring[slot], val)
            self.known[kk] = val
        else:
            _, slot, val = dep
            kk = (e, "ring", slot)
            if self.known.get(kk, 0) >= val:
                return
            self.eng[e].wait_ge(self.ring[slot], val)
            self.known[kk] = val

    def _deps(self, e, r, w):
        deps = []
        for k in r:
            k = self._key(k)
            if k in self.last_w:
                deps.append(self.last_w[k])
        for k in w:
            k = self._key(k)
            if k in self.last_w:
                deps.append(self.last_w[k])
            deps.extend(self.readers.get(k, ()))
        for d in deps:
            self._wait(e, d)

    def _record(self, me, r, w):
        for k in w:
            k = self._key(k)
            self.last_w[k] = me
            self.readers[k] = []
        for k in r:
            k = self._key(k)
            lst = self.readers.setdefault(k, [])
            lst[:] = [d for d in lst if d[:-1] != me[:-1]]
            lst.append(me)

    def op(self, e, fn, r=(), w=()):
        ep = self._cursem(e)
        self._deps(e, r, w)
        ins = fn(self.eng[e])
        self.cnt[e] += 1
        self.nins[e] += 1
        ins.then_inc(self.sems[e][ep], 1)
        self._record(("c", e, ep, self.cnt[e]), r, w)
        return ins

    def dma(self, out, in_, r=None, w=None, q="sp", **kw):
        r = [in_] if r is None else r
        w = [out] if w is None else w
        if q == "pool":
            assert self.pr_used < NPR, "too many gpsimd DMAs in one phase"
            slot = self.pr_used
            self.pr_used += 1
            self._deps(q, r, w)
            ins = self.eng[q].dma_start(out=out, in_=in_, **kw)
            ins.then_inc(self.pring[slot], 16)
            self._record(("p", slot, 16), r, w)
            return ins
        slot = self.ndma % NRING
        self.ndma += 1
        if self.ring_val[slot] > 0:
            self._wait(q, ("d", slot, self.ring_val[slot]))
        self._deps(q, r, w)
        ins = self.eng[q].dma_start(out=out, in_=in_, **kw)
        self.ring_val[slot] += 16
        ins.then_inc(self.ring[slot], 16)
        self._record(("d", slot, self.ring_val[slot]), r, w)
        return ins

    def barrier(self):
        self._wait_all("pool")
        if self.pr_used:
            for slot in range(self.pr_used):
                self._wait("pool", ("p", slot, 16))
            for slot in range(self.pr_used):
                self.eng["pool"].sem_clear(self.pring[slot])
            self.known = {kk: v for kk, v in self.known.items() if kk[1] != "pring"}
            self.pr_used = 0
            self.op("pool", lambda e: e.memset(self.dummy[:], 0.0), w=[self.dummy])
        for e in self.eng:
            if e != "pool":
                self._wait_all(e)
        self.last_w = {}
        self.readers = {}

    def _wait_all(self, e):
        for slot in range(NRING):
            if self.ring_val[slot]:
                self._wait(e, ("d", slot, self.ring_val[slot]))
        for te in self.eng:
            if te != e and self.sems[te] and self.cnt[te]:
                self._wait(e, ("c", te, len(self.sems[te]) - 1, self.cnt[te]))

    def close(self):
        self.barrier()
        while self.scopes:
            self.scopes.pop().close()
        self.es.close()


def bcast_mid(ap2d, n):
    a = ap2d.ap
    return bass.AP(ap2d.tensor, ap2d.offset, [list(a[0]), [0, n], list(a[1])])


class Prog:
    def __init__(self, stop_after=None, layers=None):
        self.k = KB()
        self.nc = self.k.nc
        self.stop_after = stop_after
        self.layers = list(range(DEPTH)) if layers is None else layers
        self.build()

    def din(self, name, shape, dt=F32):
        return self.nc.dram_tensor(name, list(shape), dt, kind="ExternalInput")

    def wload(self, dst, src2d, kc_n, ncols, q="pool", col0=0, stage=None):
        k = self.k
        if stage is None:
            c = 0
            while c < ncols:
                w = min(2048, ncols - c)
                k0 = 0
                while k0 < kc_n:
                    kn = min(16, kc_n - k0)
                    k.dma(dst[:, k0:k0 + kn, c:c + w],
                          src2d[k0 * 128:(k0 + kn) * 128, col0 + c:col0 + c + w].rearrange("(k p) n -> p k n", p=128), q="sp")
                    k0 += kn
                c += w
            return
        CW = 256
        for k0 in range(0, kc_n, 8):
            kn = min(8, kc_n - k0)
            for c in range(0, ncols, CW):
                w = min(CW, ncols - c)
                st = stage[self._stg % len(stage)]
                self._stg += 1
                k.dma(st[:, 0:kn, 0:w],
                      src2d[k0 * 128:(k0 + kn) * 128, col0 + c:col0 + c + w].rearrange("(k p) n -> p k n", p=128), q="sp")
                k.op("pool", lambda e: e.tensor_copy(out=dst[:, k0:k0 + kn, c:c + w], in_=st[:, 0:kn, 0:w]), r=[st], w=[dst])

    def build(self):
        k, nc = self.k, self.nc
        self.xin = self.din("xin", [D, T])
        self.cT = self.din("cT", [128, NCH, 2])
        self.w_ada = self.din("w_ada", [DEPTH, D, 9 * D])
        self.b_ada = self.din("b_ada", [128, DEPTH, 72])
        self.g_pre = self.din("g_pre", [128, DEPTH, 3, NCH])
        self.g_post = self.din("g_post", [128, DEPTH, 3, NCH])
        self.w_ff1 = self.din("w_ff1", [DEPTH, 2, D, DFF])
        self.w_ff3 = self.din("w_ff3", [DEPTH, 2, D, DFF])
        self.w_ff2 = self.din("w_ff2", [DEPTH, 2, DFF, D])
        self.da_wqkv = self.din("da_w_qkv", [D, 3 * D])
        self.da_wo = self.din("da_w_o", [D, D])
        self.da_lam = self.din("da_lam", [64, 4])
        self.da_subln = self.din("da_subln", [128, 1])
        self.na_wqkv = self.din("na_w_qkv", [D, 3 * D])
        self.na_wo = self.din("na_w_o", [D, D])
        self.na_rpbg = self.din("na_rpbg", [16, 128, 1408])
        self.s5_are = self.din("s5_are", [128, 2, 32])
        self.s5_aim = self.din("s5_aim", [128, 2, 32])
        self.s5_ldt = self.din("s5_ldt", [128, 2, 32])
        self.s5_bre = self.din("s5_bre", [128, 2, 32, 16])
        self.s5_bim = self.din("s5_bim", [128, 2, 32, 16])
        self.s5_cre = self.din("s5_cre", [128, 2, 32, 16])
        self.s5_cim = self.din("s5_cim", [128, 2, 32, 16])
        self.s5_d = self.din("s5_d", [128, NCH])
        self.s5_bglu = self.din("s5_bglu", [128, NCH])
        self.s5_wglu = self.din("s5_w_glu", [D, D])
        self.hg_wqig = self.din("hg_w_qig", [D, 3 * D])
        self.hg_wf = self.din("hg_w_f", [2, D, D])
        self.hg_bf = self.din("hg_b_f", [128, 2, NCH])
        self.hg_lb = self.din("hg_lb", [128, DEPTH, NCH])
        self.hg_gn = self.din("hg_gn", [128, 1])
        self.hg_wo = self.din("hg_w_o", [D, D])
        self.c_triu = self.din("c_triu", [128, 128])
        self.c_tril = self.din("c_tril", [128, 128])
        self.Fs = nc.dram_tensor("Fs", [2, D, T], F32, kind="Internal")
        self.c_ropeC = self.din("c_ropeC", [128, S])
        self.c_ropeS = self.din("c_ropeS", [128, S])
        self.c_pm = self.din("c_pm", [128, 128])
        self.c_ident = self.din("c_ident", [128, 128])
        self.c_maskI = self.din("c_maskI", [128, 1408])
        self.c_maskF = self.din("c_maskF", [128, 1408])
        self.Hs = nc.dram_tensor("Hs", [D, T], BF16, kind="Internal")
        self.Ys = nc.dram_tensor("Ys", [D, T], F32, kind="Internal")
        self.QKs = nc.dram_tensor("QKs", [2 * D, T], BF16, kind="Internal")
        self.Vs = nc.dram_tensor("Vs", [T, D], BF16, kind="Internal")
        self.AO = nc.dram_tensor("AOs", [D, T], BF16, kind="Internal")
        self.out = nc.dram_tensor("out", [D, S], F32, kind="ExternalOutput")
        self.X = nc.dram_tensor("Xres", [D, T], F32, kind="Internal")

        self.ones_bf = k.sb("ones_bf", [128, 128], BF16)
        k.op("dve", lambda e: e.memset(self.ones_bf[:], 1.0), w=[self.ones_bf])
        self.A = k.sb("modA", [128, DEPTH, 3, NCH, 2], F32)
        self.SH = k.sb("modSH", [128, DEPTH, 3, NCH, 2], F32)
        self.GT = k.sb("modGT", [128, DEPTH, 3, NCH, 2], F32)
        self.psb = [k.ps("psb%d" % i, [128, 512], F32) for i in range(7)]
        self.ps_bf = k.ps("psbf", [128, 1024], BF16)
        self._eps = k.sb("eps_c", [128, 1], F32)
        k.op("dve", lambda e: e.memset(self._eps[:], EPS), w=[self._eps])

        self.setup_mod()
        if self.stop_after == "mod":
            k.dma(self.out.ap(), self.xin.ap()[:, LC:T], q="sp")
            k.close()
            return
        first = True
        for i in self.layers:
            last = i == DEPTH - 1
            self.ffn(i, 0, 0, src=(self.xin if first else self.X), dst=self.X, with_ctx=True)
            first = False
            if self.stop_after == "a%d" % i:
                return self.finish_dbg()
            kind = i % 4
            if kind in (0, 1, 2, 3):
                self.mixer_prologue(i)
                if kind == 0:
                    self.s5_core(i)
                elif kind == 3:
                    self.hg_proj(i)
                    self.hg_core(i)
                    self.oproj_phase(self.hg_wo, with_ctx=not last)
                elif kind == 1:
                    self.qkv_phase(self.da_wqkv, rope=True, qscale=None)
                    self.da_attn(i)
                    self.oproj_phase(self.da_wo)
                else:
                    self.qkv_phase(self.na_wqkv, rope=False, qscale=0.125)
                    self.na_attn(i)
                    self.oproj_phase(self.na_wo)
                self.mixer_epilogue(i, with_ctx=not last)
            if self.stop_after == "m%d" % i:
                return self.finish_dbg()
            self.ffn(i, 1, 2, src=self.X, dst=self.X, with_ctx=not last, final=last)
            if self.stop_after == "b%d" % i:
                return self.finish_dbg()
        k.close()

    def finish_dbg(self):
        k = self.k
        k.dma(self.out.ap(), self.X.ap()[:, LC:T], q="sp")
        k.close()

    def setup_mod(self):
        k = self.k
        k.push()
        sc = k.sb("sc", [128, NCH, 2], F32)
        k.dma(sc[:], self.cT.ap())
        sg = k.sb("sg", [128, NCH, 2], F32)
        k.op("act", lambda e: e.activation(out=sg[:], in_=sc[:], func=AF.Sigmoid), r=[sc], w=[sg])
        k.op("dve", lambda e: e.tensor_tensor(out=sc[:], in0=sc[:], in1=sg[:], op=ALU.mult), r=[sc, sg], w=[sc])
        bada = k.sb("bada", [128, DEPTH, 72], F32)
        k.dma(bada[:], self.b_ada.ap())
        gpre = k.sb("gpre", [128, DEPTH, 3, NCH], F32)
        gpost = k.sb("gpost", [128, DEPTH, 3, NCH], F32)
        k.dma(gpre[:], self.g_pre.ap())
        k.dma(gpost[:], self.g_post.ap())
        m = k.sb("m_sb", [128, DEPTH, 72, 2], F32)
        PIECE = 1152
        wbuf = [k.sb("wada%d" % i, [128, NCH, PIECE], F32) for i in range(2)]
        pi = 0
        for i in range(DEPTH):
            ps = self.psb[i % 2]
            for pc in range(8):
                wb = wbuf[pi % 2]
                pi += 1
                self.wload(wb, self.w_ada.ap()[i], NCH, PIECE, col0=pc * PIECE)
                for jj in range(9):
                    jc = pc * 9 + jj
                    for kc in range(NCH):
                        k.op("pe", lambda e: e.matmul(out=ps[:, 2 * jc:2 * jc + 2], lhsT=wb[:, kc, jj * 128:(jj + 1) * 128],
                                                      rhs=sc[:, kc, :], start=(kc == 0), stop=(kc == NCH - 1)),
                             r=[wb, sc], w=[ps])
            k.op("dve", lambda e: e.tensor_tensor(out=m[:, i], in0=ps[:, 0:144].rearrange("p (j v) -> p j v", v=2),
                                                  in1=bada[:, i].unsqueeze(2).to_broadcast([128, 72, 2]), op=ALU.add),
                 r=[ps, bada], w=[m])
        for i in range(DEPTH):
            for j in range(3):
                base = j * 24
                sh = m[:, i, base + 0:base + 8, :]
                scl = m[:, i, base + 8:base + 16, :]
                gt = m[:, i, base + 16:base + 24, :]
                wgt = 0.5 if j != 1 else 1.0
                gp = gpre[:, i, j, :].unsqueeze(2).to_broadcast([128, NCH, 2])
                gq = gpost[:, i, j, :].unsqueeze(2).to_broadcast([128, NCH, 2])
                k.op("dve", lambda e: e.scalar_tensor_tensor(out=self.A[:, i, j], in0=scl, scalar=1.0, in1=gp, op0=ALU.add, op1=ALU.mult),
                     r=[m, gpre], w=[self.A])
                k.op("dve", lambda e: e.scalar_tensor_tensor(out=self.GT[:, i, j], in0=gt, scalar=wgt, in1=gq, op0=ALU.mult, op1=ALU.mult),
                     r=[m, gpost], w=[self.GT])
                k.op("dve", lambda e: e.tensor_copy(out=self.SH[:, i, j], in_=sh), r=[m], w=[self.SH])
        k.pop()

    def rstd_from(self, sq, nchunks, N, rstd, ps, dim):
        k = self.k
        for c in range(nchunks):
            k.op("pe", lambda e: e.matmul(out=ps[:, 0:N], lhsT=self.ones_bf[:], rhs=sq[:, c, 0:N], start=(c == 0), stop=(c == nchunks - 1)),
                 r=[sq, self.ones_bf], w=[ps])
        k.op("act", lambda e: e.activation(out=rstd[:, 0:N], in_=ps[:, 0:N], func=AF.Sqrt, bias=self.eps_ap(), scale=1.0 / dim), r=[ps], w=[rstd])
        k.op("dve", lambda e: e.reciprocal(out=rstd[:, 0:N], in_=rstd[:, 0:N]), r=[rstd], w=[rstd])

    def eps_ap(self):
        return self._eps[:]

    def token_tiles(self, with_ctx, n):
        tiles = []
        if with_ctx:
            for t0 in range(0, LC, n):
                tiles.append((t0, min(n, LC - t0), 1))
        for t0 in range(LC, T, n):
            tiles.append((t0, min(n, T - t0), 0))
        return tiles

    def prenorm(self, xt, N, i, j, v, sq, tmp, h, rstd, ps):
        k = self.k
        k.op("act", lambda e: e.activation(out=sq[:, :, 0:N], in_=xt[:, :, 0:N], func=AF.Square), r=[xt], w=[sq])
        self.rstd_from(sq, NCH, N, rstd, ps, D)
        for c in range(NCH):
            k.op("dve", lambda e: e.scalar_tensor_tensor(out=tmp[:, c, 0:N], in0=xt[:, c, 0:N], scalar=self.A[:, i, j, c, v:v + 1],
                                                         in1=rstd[:, 0:N], op0=ALU.mult, op1=ALU.mult), r=[xt, rstd, self.A], w=[tmp])
            k.op("act", lambda e: e.activation(out=h[:, c, 0:N], in_=tmp[:, c, 0:N], func=AF.Identity, bias=self.SH[:, i, j, c, v:v + 1], scale=1.0),
                 r=[tmp, self.SH], w=[h])

    def postres(self, y, xt, N, i, j, v, sq, rstd, ps):
        k = self.k
        self.rstd_from(sq, NCH, N, rstd, ps, D)
        for c in range(NCH):
            k.op("dve", lambda e: e.scalar_tensor_tensor(out=y[:, c, 0:N], in0=y[:, c, 0:N], scalar=self.GT[:, i, j, c, v:v + 1],
                                                         in1=rstd[:, 0:N], op0=ALU.mult, op1=ALU.mult), r=[y, rstd, self.GT], w=[y])
        k.op("pool", lambda e: e.tensor_tensor(out=xt[:, :, 0:N], in0=xt[:, :, 0:N], in1=y[:, :, 0:N], op=ALU.add), r=[xt, y], w=[xt])


    def mixer_prologue(self, i):
        k = self.k
        NT = 256
        k.push()
        xts = [k.sb("pxt%d" % b, [128, NCH, NT], F32) for b in range(2)]
        sq = k.sb("psq", [128, NCH, NT], BF16)
        tmp = k.sb("ptmp", [128, NCH, NT], F32)
        hs = [k.sb("ph%d" % b, [128, NCH, NT], BF16) for b in range(2)]
        rstd = k.sb("prstd", [128, NT], F32)
        for ti, (t0, N, v) in enumerate(self.token_tiles(True, NT)):
            xt, h = xts[ti % 2], hs[ti % 2]
            k.dma(xt[:, :, 0:N], self.X.ap()[:, t0:t0 + N].rearrange("(c p) n -> p c n", p=128))
            self.prenorm(xt, N, i, 1, v, sq, tmp, h, rstd, self.psb[0])
            k.dma(self.Hs.ap()[:, t0:t0 + N].rearrange("(c p) n -> p c n", p=128), h[:, :, 0:N], w=[("Hs", t0)])
        k.pop()

    def mixer_epilogue(self, i, with_ctx):
        k = self.k
        NT = 256
        k.push()
        xts = [k.sb("ext%d" % b, [128, NCH, NT], F32) for b in range(2)]
        ys = [k.sb("eyt%d" % b, [128, NCH, NT], F32) for b in range(2)]
        sq = k.sb("esq", [128, NCH, NT], BF16)
        rstd = k.sb("erstd", [128, NT], F32)
        for ti, (t0, N, v) in enumerate(self.token_tiles(with_ctx, NT)):
            xt, y = xts[ti % 2], ys[ti % 2]
            k.dma(xt[:, :, 0:N], self.X.ap()[:, t0:t0 + N].rearrange("(c p) n -> p c n", p=128))
            k.dma(y[:, :, 0:N], self.Ys.ap()[:, t0:t0 + N].rearrange("(c p) n -> p c n", p=128))
            k.op("act", lambda e: e.activation(out=sq[:, :, 0:N], in_=y[:, :, 0:N], func=AF.Square), r=[y], w=[sq])
            self.postres(y, xt, N, i, 1, v, sq, rstd, self.psb[0])
            k.dma(self.X.ap()[:, t0:t0 + N].rearrange("(c p) n -> p c n", p=128), xt[:, :, 0:N], w=[("X", t0)])
        k.pop()

    def qkv_phase(self, w_dram, rope, qscale):
        k = self.k
        NT = 512
        k.push()
        w = k.sb("wqkv", [128, NCH, 3 * D], BF16)
        stage = [k.sb("wstg%d" % b, [128, 8, 256], F32) for b in range(2)]
        self._stg = 0
        self.wload(w, w_dram.ap(), NCH, 3 * D, stage=stage)
        if rope:
            Ct = k.sb("ropeC", [128, S], F32)
            St = k.sb("ropeS", [128, S], F32)
            k.dma(Ct[:], self.c_ropeC.ap())
            k.dma(St[:], self.c_ropeS.ap())
            pmf = k.sb("pmf", [128, 128], F32)
            pm = k.sb("pm", [128, 128], BF16)
            k.dma(pmf[:], self.c_pm.ap())
            k.op("dve", lambda e: e.tensor_copy(out=pm[:], in_=pmf[:]), r=[pmf], w=[pm])
        hts = [k.sb("ht%d" % b, [128, NCH, NT], BF16) for b in range(2)]
        qs = [k.sb("qs%d" % b, [128, NT], BF16) for b in range(2)]
        t1 = [k.sb("t1%d" % b, [128, NT], F32) for b in range(2)]
        t2 = [k.sb("t2%d" % b, [128, NT], F32) for b in range(2)]
        qo = [k.sb("qo%d" % b, [128, NT], BF16) for b in range(2)]
        vo = [k.sb("vo%d" % b, [128, 512], BF16) for b in range(2)]
        nv = 0
        for ti, (t0, N, v) in enumerate(self.token_tiles(True, NT)):
            ht = hts[ti % 2]
            k.dma(ht[:, :, 0:N], self.Hs.ap()[:, t0:t0 + N].rearrange("(c p) n -> p c n", p=128))
            for cc in range(16):
                b = cc % 2
                ps = self.psb[1 + b]
                for c in range(NCH):
                    k.op("pe", lambda e: e.matmul(out=ps[:, 0:N], lhsT=w[:, c, cc * 128:(cc + 1) * 128], rhs=ht[:, c, 0:N],
                                                  start=(c == 0), stop=(c == NCH - 1)), r=[w, ht], w=[ps])
                if rope and v == 0:
                    l0 = t0 - LC
                    sw = self.psb[3 + b]
                    k.op("dve", lambda e: e.tensor_copy(out=qs[b][:, 0:N], in_=ps[:, 0:N]), r=[ps], w=[qs[b]])
                    k.op("pe", lambda e: e.matmul(out=sw[:, 0:N], lhsT=pm[:], rhs=qs[b][:, 0:N], start=True, stop=True), r=[pm, qs[b]], w=[sw])
                    k.op("dve", lambda e: e.tensor_tensor(out=t1[b][:, 0:N], in0=ps[:, 0:N], in1=Ct[:, l0:l0 + N], op=ALU.mult), r=[ps, Ct], w=[t1[b]])
                    k.op("dve", lambda e: e.tensor_tensor(out=t2[b][:, 0:N], in0=sw[:, 0:N], in1=St[:, l0:l0 + N], op=ALU.mult), r=[sw, St], w=[t2[b]])
                    k.op("pool", lambda e: e.tensor_tensor(out=qo[b][:, 0:N], in0=t1[b][:, 0:N], in1=t2[b][:, 0:N], op=ALU.add), r=[t1[b], t2[b]], w=[qo[b]])
                elif qscale is not None and cc < 8:
                    k.op("dve", lambda e: e.tensor_scalar(out=qo[b][:, 0:N], in0=ps[:, 0:N], scalar1=float(qscale), scalar2=None, op0=ALU.mult), r=[ps], w=[qo[b]])
                else:
                    k.op("dve", lambda e: e.tensor_copy(out=qo[b][:, 0:N], in_=ps[:, 0:N]), r=[ps], w=[qo[b]])
                k.dma(self.QKs.ap()[cc * 128:(cc + 1) * 128, t0:t0 + N], qo[b][:, 0:N], w=[("QKs", cc, t0)])
            for sub in range(N // 128):
                for vb in range(2):
                    ps = self.psb[5 + vb]
                    for c in range(NCH):
                        k.op("pe", lambda e: e.matmul(out=ps[:, :], lhsT=ht[:, c, sub * 128:(sub + 1) * 128], rhs=w[:, c, 2 * D + vb * 512:2 * D + (vb + 1) * 512],
                                                      start=(c == 0), stop=(c == NCH - 1)), r=[w, ht], w=[ps])
                    vv = vo[nv % 2]
                    nv += 1
                    k.op("act", lambda e: e.activation(out=vv[:], in_=ps[:, :], func=AF.Identity), r=[ps], w=[vv])
                    k.dma(self.Vs.ap()[t0 + sub * 128:t0 + (sub + 1) * 128, vb * 512:(vb + 1) * 512], vv[:], w=[("Vs", t0, sub, vb)])
        k.pop()

    def oproj_phase(self, wo_dram, with_ctx=True):
        k = self.k
        NT = 512
        k.push()
        wo = k.sb("wo", [128, NCH, D], BF16)
        stage = [k.sb("wstg%d" % b, [128, 8, 256], F32) for b in range(2)]
        self._stg = 0
        self.wload(wo, wo_dram.ap(), NCH, D, stage=stage)
        aos = [k.sb("ao%d" % b, [128, NCH, NT], BF16) for b in range(2)]
        yo = [k.sb("yo%d" % b, [128, NT], F32) for b in range(2)]
        for ti, (t0, N, v) in enumerate(self.token_tiles(with_ctx, NT)):
            ao = aos[ti % 2]
            k.dma(ao[:, :, 0:N], self.AO.ap()[:, t0:t0 + N].rearrange("(c p) n -> p c n", p=128))
            for c2 in range(NCH):
                ps = self.psb[1 + c2 % 2]
                for c in range(NCH):
                    k.op("pe", lambda e: e.matmul(out=ps[:, 0:N], lhsT=wo[:, c, c2 * 128:(c2 + 1) * 128], rhs=ao[:, c, 0:N],
                                                  start=(c == 0), stop=(c == NCH - 1)), r=[wo, ao], w=[ps])
                y = yo[c2 % 2]
                k.op("dve", lambda e: e.tensor_copy(out=y[:, 0:N], in_=ps[:, 0:N]), r=[ps], w=[y])
                k.dma(self.Ys.ap()[c2 * 128:(c2 + 1) * 128, t0:t0 + N], y[:, 0:N], w=[("Ys", c2, t0)])
        k.pop()

    def da_attn(self, i):
        k = self.k
        lam_init = 0.8 - 0.6 * math.exp(-0.3 * i)
        k.push()
        lv = k.sb("lamv", [64, 4], F32)
        k.dma(lv[:], self.da_lam.ap())
        pr = k.sb("lampr", [64, 2], F32)
        k.op("dve", lambda e: e.tensor_tensor(out=pr[:, 0:1], in0=lv[:, 0:1], in1=lv[:, 1:2], op=ALU.mult), r=[lv], w=[pr])
        k.op("dve", lambda e: e.tensor_tensor(out=pr[:, 1:2], in0=lv[:, 2:3], in1=lv[:, 3:4], op=ALU.mult), r=[lv, pr], w=[pr])
        onesf = k.sb("onesf", [64, 128], F32)
        k.op("dve", lambda e: e.memset(onesf[:], 1.0), w=[onesf])
        psl = self.psb[6]
        k.op("pe", lambda e: e.matmul(out=psl[:, 0:2], lhsT=onesf[:], rhs=pr[:], start=True, stop=True), r=[onesf, pr], w=[psl])
        ex = k.sb("lamex", [128, 2], F32)
        k.op("act", lambda e: e.activation(out=ex[:], in_=psl[:, 0:2], func=AF.Exp), r=[psl], w=[ex])
        neglam = k.sb("neglam", [128, 1], F32)
        k.op("dve", lambda e: e.tensor_tensor(out=neglam[:], in0=ex[:, 1:2], in1=ex[:, 0:1], op=ALU.subtract), r=[ex], w=[neglam])
        k.op("dve", lambda e: e.tensor_scalar(out=neglam[:], in0=neglam[:], scalar1=-lam_init, scalar2=None, op0=ALU.add), r=[neglam], w=[neglam])
        gsub = k.sb("gsub", [128, 1], F32)
        k.dma(gsub[:], self.da_subln.ap())
        k.op("dve", lambda e: e.tensor_scalar(out=gsub[:], in0=gsub[:], scalar1=1.0 - lam_init, scalar2=None, op0=ALU.mult), r=[gsub], w=[gsub])

        qTs = [k.sb("qT%d" % b, [128, T], BF16) for b in range(2)]
        kTs = [k.sb("kT%d" % b, [128, T], BF16) for b in range(2)]
        vhs = [k.sb("vh%d" % b, [128, 34, 128], BF16) for b in range(2)]
        pb = [k.sb("pexp%d" % b, [128, 512], BF16) for b in range(4)]
        accA = [k.sb("accA%d" % b, [128, 512], F32) for b in range(2)]
        accB = [k.sb("accB%d" % b, [128, 512], F32) for b in range(2)]
        ones128f = k.sb("ones128f", [128, 128], F32)
        k.op("dve", lambda e: e.memset(ones128f[:], 1.0), w=[ones128f])
        rb = k.sb("rb", [128, 512], F32)
        o1 = k.sb("o1", [128, 512], F32)
        o2 = k.sb("o2", [128, 512], F32)
        sq = k.sb("dsq", [128, 1, 512], BF16)
        rstd = k.sb("drstd", [128, 512], F32)
        ons = [k.sb("on%d" % b, [128, 512], BF16) for b in range(2)]
        blocks = [(0, LC, [0, 1])] + [(LC + 512 * b, 512, list(range(34))) for b in range(8)]
        nb = 0
        for hd in range(8):
            qT, kT, vh = qTs[hd % 2], kTs[hd % 2], vhs[hd % 2]
            k.dma(qT[:], self.QKs.ap()[hd * 128:(hd + 1) * 128, :])
            k.dma(kT[:], self.QKs.ap()[D + hd * 128:D + (hd + 1) * 128, :])
            k.dma(vh[:], self.Vs.ap()[:, hd * 128:(hd + 1) * 128].rearrange("(kt p) d -> p kt d", p=128))
            for (q0, N, kts) in blocks:
                units = [(ki, kt, m) for ki, kt in enumerate(kts) for m in range(2)]

                def emit_qk(u):
                    ki, kt, m = units[u]
                    pss = self.psb[4 + u % 3]
                    k.op("pe", lambda e: e.matmul(out=pss[:, 0:N], lhsT=kT[m * 64:(m + 1) * 64, kt * 128:(kt + 1) * 128],
                                                  rhs=qT[m * 64:(m + 1) * 64, q0:q0 + N], start=True, stop=True), r=[kT, qT], w=[pss])
                emit_qk(0)
                emit_qk(1)
                for u, (ki, kt, m) in enumerate(units):
                    num = self.psb[2 * m]
                    pss = self.psb[4 + u % 3]
                    P = pb[u % 4]
                    k.op("act", lambda e: e.activation(out=P[:, 0:N], in_=pss[:, 0:N], func=AF.Exp, scale=0.125), r=[pss], w=[P])
                    if u + 2 < len(units):
                        emit_qk(u + 2)
                    k.op("pe", lambda e: e.matmul(out=num[:, 0:N], lhsT=vh[:, kt, :], rhs=P[:, 0:N], start=(ki == 0), stop=(ki == len(kts) - 1)), r=[vh, P], w=[num])
                    eng_, acc_ = ("dve", accA[m]) if ki % 2 == 0 else ("pool", accB[m])
                    if ki < 2:
                        k.op(eng_, lambda e: e.tensor_copy(out=acc_[:, 0:N], in_=P[:, 0:N]), r=[P], w=[acc_])
                    else:
                        k.op(eng_, lambda e: e.tensor_tensor(out=acc_[:, 0:N], in0=acc_[:, 0:N], in1=P[:, 0:N], op=ALU.add), r=[P, acc_], w=[acc_])
                for m in range(2):
                    den = self.psb[2 * m + 1]
                    k.op("dve", lambda e: e.tensor_tensor(out=accA[m][:, 0:N], in0=accA[m][:, 0:N], in1=accB[m][:, 0:N], op=ALU.add), r=[accA[m], accB[m]], w=[accA[m]])
                    k.op("pe", lambda e: e.matmul(out=den[:, 0:N], lhsT=ones128f[:], rhs=accA[m][:, 0:N], start=True, stop=True), r=[ones128f, accA[m]], w=[den])
                n0, d0, n1, d1 = self.psb[0], self.psb[1], self.psb[2], self.psb[3]
                k.op("dve", lambda e: e.reciprocal(out=rb[:, 0:N], in_=d0[:, 0:N]), r=[d0], w=[rb])
                k.op("dve", lambda e: e.tensor_tensor(out=o1[:, 0:N], in0=n0[:, 0:N], in1=rb[:, 0:N], op=ALU.mult), r=[n0, rb], w=[o1])
                k.op("dve", lambda e: e.reciprocal(out=rb[:, 0:N], in_=d1[:, 0:N]), r=[d1], w=[rb])
                k.op("dve", lambda e: e.tensor_tensor(out=o2[:, 0:N], in0=n1[:, 0:N], in1=rb[:, 0:N], op=ALU.mult), r=[n1, rb], w=[o2])
                k.op("dve", lambda e: e.scalar_tensor_tensor(out=o1[:, 0:N], in0=o2[:, 0:N], scalar=neglam[:, 0:1], in1=o1[:, 0:N], op0=ALU.mult, op1=ALU.add), r=[o1, o2, neglam], w=[o1])
                k.op("act", lambda e: e.activation(out=sq[:, 0, 0:N], in_=o1[:, 0:N], func=AF.Square), r=[o1], w=[sq])
                self.rstd_from(sq, 1, N, rstd, self.psb[1], 128)
                on = ons[nb % 2]
                nb += 1
                k.op("dve", lambda e: e.scalar_tensor_tensor(out=on[:, 0:N], in0=o1[:, 0:N], scalar=gsub[:, 0:1], in1=rstd[:, 0:N], op0=ALU.mult, op1=ALU.mult), r=[o1, gsub, rstd], w=[on])
                k.dma(self.AO.ap()[hd * 128:(hd + 1) * 128, q0:q0 + N], on[:, 0:N], w=[("AO", hd, q0)])
        k.pop()

    def na_attn(self, i):
        k = self.k
        k.push()
        idf = k.sb("idf", [128, 128], F32)
        ident = k.sb("ident", [128, 128], BF16)
        k.dma(idf[:], self.c_ident.ap())
        k.op("dve", lambda e: e.tensor_copy(out=ident[:], in_=idf[:]), r=[idf], w=[ident])
        mI = k.sb("mI", [128, 1408], F32)
        mF = k.sb("mF", [128, 1408], F32)
        k.dma(mI[:], self.c_maskI.ap())
        k.dma(mF[:], self.c_maskF.ap())
        qTs = [k.sb("nqT%d" % b, [64, T], BF16) for b in range(2)]
        kTs = [k.sb("nkT%d" % b, [64, T], BF16) for b in range(2)]
        vhs = [k.sb("nvh%d" % b, [128, 34, 64], BF16) for b in range(2)]
        Gs = [k.sb("nG%d" % b, [128, 1408], F32) for b in range(2)]
        TIs = [k.sb("nTI%d" % b, [128, 1408], BF16) for b in range(2)]
        TFs = [k.sb("nTF%d" % b, [128, 1408], BF16) for b in range(2)]
        pb = [k.sb("npexp%d" % b, [128, 512], BF16) for b in range(4)]
        rb = k.sb("nrb", [64, 512], F32)
        sbias = [k.sb("nsb%d" % b, [128, 512], F32) for b in range(3)]
        nacc = k.sb("nacc", [128, 512], F32)
        ones128f = k.sb("nones128f", [128, 128], F32)
        k.op("dve", lambda e: e.memset(ones128f[:], 1.0), w=[ones128f])
        ons = [k.sb("non%d" % b, [64, 512], BF16) for b in range(2)]
        def lat_keys(qr0, nr, krs, tab):
            return [(0, None, 0, 0), (1, None, 0, 0)] + [(2 + kr // 2, tab, 10 - (kr - qr0), nr) for kr in krs]
        blocks = [(0, LC, [(0, None, 0, 0), (1, None, 0, 0)])]
        blocks.append((LC, 4 * 64, lat_keys(0, 4, [0, 2, 4, 6], "F")))
        for qr0 in range(4, 60, 8):
            blocks.append((LC + qr0 * 64, 8 * 64, lat_keys(qr0, 8, list(range(qr0 - 4, qr0 + 12, 2)), "I")))
        blocks.append((LC + 60 * 64, 64, lat_keys(60, 1, [56, 58, 60, 62], "I")))
        blocks.append((LC + 61 * 64, 3 * 64, lat_keys(61, 3, [56, 58, 60, 62], "F")))
        nb = 0
        for hd in range(16):
            b2 = hd % 2
            qT, kT, vh, G, TI, TF = qTs[b2], kTs[b2], vhs[b2], Gs[b2], TIs[b2], TFs[b2]
            k.dma(qT[:], self.QKs.ap()[hd * 64:(hd + 1) * 64, :])
            k.dma(kT[:], self.QKs.ap()[D + hd * 64:D + (hd + 1) * 64, :])
            k.dma(vh[:], self.Vs.ap()[:, hd * 64:(hd + 1) * 64].rearrange("(kt p) d -> p kt d", p=128))
            k.dma(G[:], self.na_rpbg.ap()[hd])
            k.op("dve", lambda e: e.tensor_tensor(out=TI[:], in0=G[:], in1=mI[:], op=ALU.add), r=[G, mI], w=[TI])
            k.op("pool", lambda e: e.tensor_tensor(out=TF[:], in0=G[:], in1=mF[:], op=ALU.add), r=[G, mF], w=[TF])
            for (q0, N, keys) in blocks:
                num, den = self.psb[0], self.psb[1]
                def emit_qk(u):
                    kt_ = keys[u][0]
                    pq = self.psb[2 + u % 4]
                    k.op("pe", lambda e: e.matmul(out=pq[:, 0:N], lhsT=kT[:, kt_ * 128:(kt_ + 1) * 128], rhs=qT[:, q0:q0 + N],
                                                  start=True, stop=True), r=[kT, qT], w=[pq])
                emit_qk(0)
                emit_qk(1)
                for ki, (kt, tab, jj0, nr) in enumerate(keys):
                    pss = self.psb[2 + ki % 4]
                    P = pb[ki % 4]
                    if ki + 2 < len(keys):
                        emit_qk(ki + 2)
                    if tab is not None:
                        tb = TI if tab == "I" else TF
                        sbb = sbias[ki % 3]
                        k.op("dve", lambda e: e.tensor_tensor(out=sbb[:, 0:N], in0=pss[:, 0:N], in1=tb[:, jj0 * 64:(jj0 + nr) * 64], op=ALU.add), r=[pss, tb], w=[sbb])
                        k.op("act", lambda e: e.activation(out=P[:, 0:N], in_=sbb[:, 0:N], func=AF.Exp), r=[sbb], w=[P])
                    else:
                        k.op("act", lambda e: e.activation(out=P[:, 0:N], in_=pss[:, 0:N], func=AF.Exp), r=[pss], w=[P])
                    k.op("pe", lambda e: e.matmul(out=num[0:64, 0:N], lhsT=vh[:, kt, :], rhs=P[:, 0:N], start=(ki == 0), stop=(ki == len(keys) - 1)), r=[vh, P], w=[num])
                    if ki < 1:
                        k.op("pool", lambda e: e.tensor_copy(out=nacc[:, 0:N], in_=P[:, 0:N]), r=[P], w=[nacc])
                    else:
                        k.op("pool", lambda e: e.tensor_tensor(out=nacc[:, 0:N], in0=nacc[:, 0:N], in1=P[:, 0:N], op=ALU.add), r=[P, nacc], w=[nacc])
                k.op("pe", lambda e: e.matmul(out=den[0:64, 0:N], lhsT=ones128f[:, 0:64], rhs=nacc[:, 0:N], start=True, stop=True), r=[ones128f, nacc], w=[den])
                on = ons[nb % 2]
                nb += 1
                k.op("dve", lambda e: e.reciprocal(out=rb[:, 0:N], in_=den[0:64, 0:N]), r=[den], w=[rb])
                k.op("dve", lambda e: e.tensor_tensor(out=on[:, 0:N], in0=num[0:64, 0:N], in1=rb[:, 0:N], op=ALU.mult), r=[num, rb], w=[on])
                k.dma(self.AO.ap()[hd * 64:(hd + 1) * 64, q0:q0 + N], on[:, 0:N], w=[("AO", hd, q0)])
        k.pop()


    def s5_core(self, i):
        k = self.k
        L = 256
        NTL = T // L
        k.push()
        TT = lambda e, o, a, b, op: e.tensor_tensor(out=o, in0=a, in1=b, op=op)
        wg = k.sb("wglu", [128, NCH, D], BF16)
        idf = k.sb("idf", [128, 128], F32)
        k.dma(idf[:], self.c_ident.ap())
        prm = {}
        for nm, src, shp in (("are", self.s5_are, [128, 2, 32]), ("aim", self.s5_aim, [128, 2, 32]), ("ldt", self.s5_ldt, [128, 2, 32]),
                                                          ("d", self.s5_d, [128, NCH]), ("bglu", self.s5_bglu, [128, NCH])):
            t_ = k.sb("s5" + nm, shp, F32)
            k.dma(t_[:], src.ap())
            prm[nm] = t_
        for nm in ("bre", "bim", "cre", "cim"):
            prm[nm] = k.sb("s5" + nm, [128, 32, 16], F32)
        hpi = k.sb("hpi", [128, 1], F32)
        k.op("dve", lambda e: e.memset(hpi[:], math.pi / 2), w=[hpi])
        sm = {nm: k.sb("s5" + nm, [128, 32], F32) for nm in ("dt", "rho", "th", "c", "s", "cc", "ss", "nr", "ni", "inv", "gr", "gi", "t0", "t1")}
        bbr = k.sb("bbr", [128, 32, 16], F32)
        bbi = k.sb("bbi", [128, 32, 16], F32)
        bt = k.sb("bt", [128, 32, 16], F32)
        ZA = k.sb("ZA", [128, 32, 128], F32)
        ZB = k.sb("ZB", [128, 32, 128], F32)
        stage = [Z_[:, 0:16, :].rearrange("p (a b) n -> p a (b n)", b=2) for Z_ in (ZA, ZB)]
        self._stg = 0
        self.wload(wg, self.s5_wglu.ap(), NCH, D, stage=stage)
        WBr = k.sb("WBr", [128, 32, 128], BF16)
        WBi = k.sb("WBi", [128, 32, 128], BF16)
        ZCr = k.sb("ZCr", [128, 32, 128], BF16)
        ZCi = k.sb("ZCi", [128, 32, 128], BF16)
        k.op("pool", lambda e: e.memset(ZCr[:], 0.0), w=[ZCr])
        k.op("pool", lambda e: e.memset(ZCi[:], 0.0), w=[ZCi])
        Tc = k.sb("Tc", [128, 32, L], F32)
        Ts = k.sb("Ts", [128, 32, L], F32)
        car = k.sb("car", [128, 32], F32)
        cai = k.sb("cai", [128, 32], F32)
        hts = [k.sb("s5ht%d" % b, [128, NCH, L], BF16) for b in range(1)] * 2
        wk = {nm: [k.sb("s5w%s%d" % (nm, b), [128, L], F32) for b in range(2)] for nm in ("a", "b", "gr", "gi", "rr", "ri", "hr", "hi")}
        hb = {nm: [k.sb("s5h%s%d" % (nm, b), [128, L], BF16) for b in range(2)] for nm in ("r", "i")}
        wcs = [k.sb("s5wc%d" % b, [128, L], F32) for b in range(2)]
        wds = [k.sb("s5wd%d" % b, [128, L], F32) for b in range(2)]
        ytot = k.sb("ytot", [128, NCH, L], F32)
        zb = k.sb("zb", [128, NCH, L], BF16)
        sg = [k.sb("s5sg%d" % b, [128, L], F32) for b in range(1)] * 2
        yo = [k.sb("s5yo%d" % b, [128, L], F32) for b in range(1)] * 2
        ps_x = [(self.psb[1], self.psb[2]), (self.psb[3], self.psb[4])]
        ps_tr = self.psb[0]

        def bc16(ap2):
            return ap2.unsqueeze(2).to_broadcast([128, 32, 16])

        def rev(t_, n):
            return bass.AP(t_, n - 1, [[t_[:].ap[0][0], 128], [-1, n]])

        nw = 0
        for dr in range(2):
            A = sm
            for nm, src in (("bre", self.s5_bre), ("bim", self.s5_bim), ("cre", self.s5_cre), ("cim", self.s5_cim)):
                k.dma(prm[nm][:], src.ap()[:, dr])
            k.op("act", lambda e: e.activation(out=A["dt"][:], in_=prm["ldt"][:, dr], func=AF.Exp), r=[prm["ldt"]], w=[A["dt"]])
            k.op("dve", lambda e: TT(e, A["t0"][:], prm["are"][:, dr], A["dt"][:], ALU.mult), r=[prm["are"], A["dt"]], w=[A["t0"]])
            k.op("act", lambda e: e.activation(out=A["rho"][:], in_=A["t0"][:], func=AF.Exp), r=[A["t0"]], w=[A["rho"]])
            k.op("dve", lambda e: TT(e, A["th"][:], prm["aim"][:, dr], A["dt"][:], ALU.mult), r=[prm["aim"], A["dt"]], w=[A["th"]])
            k.op("act", lambda e: e.activation(out=A["s"][:], in_=A["th"][:], func=AF.Sin, scale=1.0 / 16), r=[A["th"]], w=[A["s"]])
            k.op("act", lambda e: e.activation(out=A["c"][:], in_=A["th"][:], func=AF.Sin, scale=1.0 / 16, bias=hpi[:]), r=[A["th"], hpi], w=[A["c"]])
            for _ in range(4):
                k.op("dve", lambda e: TT(e, A["cc"][:], A["c"][:], A["c"][:], ALU.mult), r=[A["c"]], w=[A["cc"]])
                k.op("dve", lambda e: TT(e, A["ss"][:], A["s"][:], A["s"][:], ALU.mult), r=[A["s"]], w=[A["ss"]])
                k.op("dve", lambda e: e.scalar_tensor_tensor(out=A["s"][:], in0=A["c"][:], scalar=2.0, in1=A["s"][:], op0=ALU.mult, op1=ALU.mult), r=[A["c"], A["s"]], w=[A["s"]])
                k.op("dve", lambda e: TT(e, A["c"][:], A["cc"][:], A["ss"][:], ALU.subtract), r=[A["cc"], A["ss"]], w=[A["c"]])
            k.op("dve", lambda e: TT(e, A["nr"][:], A["rho"][:], A["c"][:], ALU.mult), r=[A["rho"], A["c"]], w=[A["nr"]])
            k.op("dve", lambda e: e.tensor_scalar(out=A["nr"][:], in0=A["nr"][:], scalar1=-1.0, scalar2=None, op0=ALU.add), r=[A["nr"]], w=[A["nr"]])
            k.op("dve", lambda e: TT(e, A["ni"][:], A["rho"][:], A["s"][:], ALU.mult), r=[A["rho"], A["s"]], w=[A["ni"]])
            k.op("dve", lambda e: TT(e, A["t0"][:], prm["are"][:, dr], prm["are"][:, dr], ALU.mult), r=[prm["are"]], w=[A["t0"]])
            k.op("dve", lambda e: TT(e, A["t1"][:], prm["aim"][:, dr], prm["aim"][:, dr], ALU.mult), r=[prm["aim"]], w=[A["t1"]])
            k.op("dve", lambda e: TT(e, A["inv"][:], A["t0"][:], A["t1"][:], ALU.add), r=[A["t0"], A["t1"]], w=[A["inv"]])
            k.op("dve", lambda e: e.reciprocal(out=A["inv"][:], in_=A["inv"][:]), r=[A["inv"]], w=[A["inv"]])
            k.op("dve", lambda e: TT(e, A["t0"][:], A["nr"][:], prm["are"][:, dr], ALU.mult), r=[A["nr"], prm["are"]], w=[A["t0"]])
            k.op("dve", lambda e: TT(e, A["t1"][:], A["ni"][:], prm["aim"][:, dr], ALU.mult), r=[A["ni"], prm["aim"]], w=[A["t1"]])
            k.op("dve", lambda e: TT(e, A["gr"][:], A["t0"][:], A["t1"][:], ALU.add), r=[A["t0"], A["t1"]], w=[A["gr"]])
            k.op("dve", lambda e: TT(e, A["gr"][:], A["gr"][:], A["inv"][:], ALU.mult), r=[A["gr"], A["inv"]], w=[A["gr"]])
            k.op("dve", lambda e: TT(e, A["t0"][:], A["ni"][:], prm["are"][:, dr], ALU.mult), r=[A["ni"], prm["are"]], w=[A["t0"]])
            k.op("dve", lambda e: TT(e, A["t1"][:], A["nr"][:], prm["aim"][:, dr], ALU.mult), r=[A["nr"], prm["aim"]], w=[A["t1"]])
            k.op("dve", lambda e: TT(e, A["gi"][:], A["t0"][:], A["t1"][:], ALU.subtract), r=[A["t0"], A["t1"]], w=[A["gi"]])
            k.op("dve", lambda e: TT(e, A["gi"][:], A["gi"][:], A["inv"][:], ALU.mult), r=[A["gi"], A["inv"]], w=[A["gi"]])
            k.op("dve", lambda e: TT(e, bbr[:], prm["bre"][:], bc16(A["gr"][:]), ALU.mult), r=[prm["bre"], A["gr"]], w=[bbr])
            k.op("dve", lambda e: TT(e, bt[:], prm["bim"][:], bc16(A["gi"][:]), ALU.mult), r=[prm["bim"], A["gi"]], w=[bt])
            k.op("dve", lambda e: TT(e, bbr[:], bbr[:], bt[:], ALU.subtract), r=[bbr, bt], w=[bbr])
            k.op("dve", lambda e: TT(e, bbi[:], prm["bim"][:], bc16(A["gr"][:]), ALU.mult), r=[prm["bim"], A["gr"]], w=[bbi])
            k.op("dve", lambda e: TT(e, bt[:], prm["bre"][:], bc16(A["gi"][:]), ALU.mult), r=[prm["bre"], A["gi"]], w=[bt])
            k.op("dve", lambda e: TT(e, bbi[:], bbi[:], bt[:], ALU.add), r=[bbi, bt], w=[bbi])
            k.op("pool", lambda e: e.memset(ZA[:], 0.0), w=[ZA])
            k.op("pool", lambda e: e.memset(ZB[:], 0.0), w=[ZB])
            for gl in range(2):
                for r4 in range(4):
                    c0 = (2 * r4 + gl) * 16
                    pr_ = slice(gl * 64, (gl + 1) * 64)
                    k.op("dve", lambda e: e.tensor_copy(out=ZA[pr_, r4::4, c0:c0 + 16], in_=bbr[pr_, r4::4, :]), r=[bbr], w=[ZA])
                    k.op("dve", lambda e: e.tensor_copy(out=ZB[pr_, r4::4, c0:c0 + 16], in_=bbi[pr_, r4::4, :]), r=[bbi], w=[ZB])
                    k.op("dve", lambda e: e.tensor_copy(out=ZCr[pr_, r4::4, c0:c0 + 16], in_=prm["cre"][pr_, r4::4, :]), r=[prm["cre"]], w=[ZCr])
                    k.op("dve", lambda e: e.tensor_scalar(out=ZCi[pr_, r4::4, c0:c0 + 16], in0=prm["cim"][pr_, r4::4, :], scalar1=-1.0, scalar2=None, op0=ALU.mult),
                         r=[prm["cim"]], w=[ZCi])
            for st in range(32):
                for Z, W in ((ZA, WBr), (ZB, WBi)):
                    k.op("pe", lambda e: e.transpose(out=ps_tr[:, 0:128], in_=Z[:, st, :], identity=idf[:]), r=[Z, idf], w=[ps_tr])
                    k.op("act", lambda e: e.activation(out=W[:, st, :], in_=ps_tr[:, 0:128], func=AF.Identity), r=[ps_tr], w=[W])
            k.op("dve", lambda e: e.tensor_copy(out=Tc[:, :, 0], in_=A["c"][:]), r=[A["c"]], w=[Tc])
            k.op("dve", lambda e: e.tensor_copy(out=Ts[:, :, 0], in_=A["s"][:]), r=[A["s"]], w=[Ts])
            m = 1
            while m < L:
                pc = Tc[:, :, m - 1:m].to_broadcast([128, 32, m])
                pS = Ts[:, :, m - 1:m].to_broadcast([128, 32, m])
                k.op("dve", lambda e: TT(e, ZA[:, :, 0:m], Tc[:, :, 0:m], pc, ALU.mult), r=[Tc, WBr, WBi], w=[ZA])
                k.op("dve", lambda e: TT(e, ZB[:, :, 0:m], Ts[:, :, 0:m], pS, ALU.mult), r=[Ts, Tc], w=[ZB])
                k.op("dve", lambda e: TT(e, Tc[:, :, m:2 * m], ZA[:, :, 0:m], ZB[:, :, 0:m], ALU.subtract), r=[ZA, ZB], w=[Tc])
                k.op("dve", lambda e: TT(e, ZA[:, :, 0:m], Tc[:, :, 0:m], pS, ALU.mult), r=[Tc, Ts], w=[ZA])
                k.op("dve", lambda e: TT(e, ZB[:, :, 0:m], Ts[:, :, 0:m], pc, ALU.mult), r=[Ts, Tc], w=[ZB])
                k.op("dve", lambda e: TT(e, Ts[:, :, m:2 * m], ZA[:, :, 0:m], ZB[:, :, 0:m], ALU.add), r=[ZA, ZB], w=[Ts])
                m *= 2
            if dr == 1:
                H2 = L // 2
                for Tt in (Tc, Ts):
                    up = bass.AP(Tt, L - 1, [[32 * L, 128], [L, 32], [-1, H2]])
                    lo = bass.AP(Tt, H2 - 1, [[32 * L, 128], [L, 32], [-1, H2]])
                    k.op("dve", lambda e: e.tensor_copy(out=ZA[:, :, 0:H2], in_=up), r=[Tt], w=[ZA])
                    k.op("dve", lambda e: e.tensor_copy(out=Tt[:, :, H2:L], in_=lo), r=[Tt, ZA], w=[Tt])
                    k.op("dve", lambda e: e.tensor_copy(out=Tt[:, :, 0:H2], in_=ZA[:, :, 0:H2]), r=[ZA], w=[Tt])
            k.op("dve", lambda e: e.memset(car[:], 0.0), w=[car])
            k.op("dve", lambda e: e.memset(cai[:], 0.0), w=[cai])
            order = list(range(NTL)) if dr == 0 else [0] + list(range(NTL - 1, 0, -1))
            for oi, tl in enumerate(order):
                t0 = tl * L
                ht = hts[oi % 2]
                k.dma(ht[:], self.Hs.ap()[:, t0:t0 + L].rearrange("(c p) n -> p c n", p=128))
                if dr == 1:
                    k.dma(ytot[:], self.Ys.ap()[:, t0:t0 + L].rearrange("(c p) n -> p c n", p=128), w=[(ytot.name, c_) for c_ in range(NCH)])
                def emit_x(st_, b_):
                    xr_, xi_ = ps_x[b_]
                    c_ = st_ // 4
                    k.op("pe", lambda e: e.matmul(out=xr_[:, 0:L], lhsT=WBr[:, st_, :], rhs=ht[:, c_, :], start=True, stop=True), r=[WBr, ht], w=[xr_])
                    k.op("pe", lambda e: e.matmul(out=xi_[:, 0:L], lhsT=WBi[:, st_, :], rhs=ht[:, c_, :], start=True, stop=True), r=[WBi, ht], w=[xi_])
                emit_x(0, nw % 2)
                for c in range(NCH):
                    py = self.psb[5 + c % 2]
                    for s4 in range(4):
                        st = 4 * c + s4
                        b = nw % 2
                        nw += 1
                        pxr, pxi = ps_x[b]
                        wa, wb_, gr, gi, rr, ri, hr, hi = (wk[n_][b] for n_ in ("a", "b", "gr", "gi", "rr", "ri", "hr", "hi"))
                        wc, wd = wcs[b], wds[b]
                        tc, ts = Tc[:, st, :], Ts[:, st, :]
                        if dr == 0:
                            sc_out = lambda t_: t_[:]
                            last = L - 1
                        else:
                            sc_out = lambda t_: bass.AP(t_, L - 1, [[L, 128], [-1, L]])
                            last = 0
                        k.op("dve", lambda e: TT(e, wa[:], pxr[:, 0:L], tc, ALU.mult), r=[pxr, Tc], w=[wa])
                        k.op("dve", lambda e: TT(e, wb_[:], pxi[:, 0:L], ts, ALU.mult), r=[pxi, Ts], w=[wb_])
                        k.op("dve", lambda e: TT(e, gr[:], wa[:], wb_[:], ALU.add), r=[wa, wb_], w=[gr])
                        k.op("dve", lambda e: TT(e, wa[:], pxi[:, 0:L], tc, ALU.mult), r=[pxi, Tc], w=[wa])
                        k.op("dve", lambda e: TT(e, wb_[:], pxr[:, 0:L], ts, ALU.mult), r=[pxr, Ts], w=[wb_])
                        k.op("dve", lambda e: TT(e, gi[:], wa[:], wb_[:], ALU.subtract), r=[wa, wb_], w=[gi])
                        rho_b = A["rho"][:, st:st + 1].to_broadcast([128, L])
                        k.op("dve", lambda e: e.tensor_tensor_scan(out=sc_out(rr), data0=rho_b, data1=sc_out(gr), initial=car[:, st:st + 1], op0=ALU.mult, op1=ALU.add),
                             r=[gr, A["rho"], car], w=[rr])
                        k.op("dve", lambda e: e.tensor_tensor_scan(out=sc_out(ri), data0=rho_b, data1=sc_out(gi), initial=cai[:, st:st + 1], op0=ALU.mult, op1=ALU.add),
                             r=[gi, A["rho"], cai], w=[ri])
                        k.op("pool", lambda e: TT(e, wc[:], rr[:], tc, ALU.mult), r=[rr, Tc], w=[wc])
                        k.op("pool", lambda e: TT(e, wd[:], ri[:], ts, ALU.mult), r=[ri, Ts], w=[wd])
                        k.op("pool", lambda e: TT(e, hr[:], wc[:], wd[:], ALU.subtract), r=[wc, wd], w=[hr])
                        k.op("pool", lambda e: TT(e, wc[:], rr[:], ts, ALU.mult), r=[rr, Ts], w=[wc])
                        k.op("pool", lambda e: TT(e, wd[:], ri[:], tc, ALU.mult), r=[ri, Tc], w=[wd])
                        k.op("pool", lambda e: TT(e, hi[:], wc[:], wd[:], ALU.add), r=[wc, wd], w=[hi])
                        k.op("pool", lambda e: e.tensor_copy(out=car[:, st:st + 1], in_=hr[:, last:last + 1]), r=[hr], w=[car])
                        k.op("pool", lambda e: e.tensor_copy(out=cai[:, st:st + 1], in_=hi[:, last:last + 1]), r=[hi], w=[cai])
                        hbr, hbi = hb["r"][b], hb["i"][b]
                        k.op("act", lambda e: e.activation(out=hbr[:], in_=hr[:], func=AF.Identity), r=[hr], w=[hbr])
                        k.op("act", lambda e: e.activation(out=hbi[:], in_=hi[:], func=AF.Identity), r=[hi], w=[hbi])
                        if st + 1 < 32:
                            emit_x(st + 1, nw % 2)
                        k.op("pe", lambda e: e.matmul(out=py[:, 0:L], lhsT=ZCr[:, st, :], rhs=hbr[:], start=(s4 == 0), stop=False), r=[ZCr, hbr], w=[py])
                        k.op("pe", lambda e: e.matmul(out=py[:, 0:L], lhsT=ZCi[:, st, :], rhs=hbi[:], start=False, stop=(s4 == 3)), r=[ZCi, hbi], w=[py])
                    if dr == 0:
                        y = yo[c % 2]
                        k.op("dve", lambda e: e.tensor_copy(out=y[:], in_=py[:, 0:L]), r=[py], w=[y])
                        k.dma(self.Ys.ap()[c * 128:(c + 1) * 128, t0:t0 + L], y[:], w=[("Ysp", c, t0)])
                    else:
                        k.op("dve", lambda e: TT(e, ytot[:, c, :], py[:, 0:L], ytot[:, c, :], ALU.add), r=[py, (ytot.name, c)], w=[(ytot.name, c)])
                        k.op("dve", lambda e: e.scalar_tensor_tensor(out=ytot[:, c, :], in0=ht[:, c, :], scalar=prm["d"][:, c:c + 1], in1=ytot[:, c, :], op0=ALU.mult, op1=ALU.add),
                             r=[ht, prm["d"], (ytot.name, c)], w=[(ytot.name, c)])
                        k.op("act", lambda e: e.activation(out=ytot[:, c, :], in_=ytot[:, c, :], func=AF.Gelu), r=[(ytot.name, c)], w=[(ytot.name, c)])
                        k.op("act", lambda e: e.activation(out=zb[:, c, :], in_=ytot[:, c, :], func=AF.Identity), r=[(ytot.name, c)], w=[(zb.name, c)])
                if dr == 1:
                    for c2 in range(NCH):
                        pu = self.psb[5 + c2 % 2]
                        for c in range(NCH):
                            k.op("pe", lambda e: e.matmul(out=pu[:, 0:L], lhsT=wg[:, c, c2 * 128:(c2 + 1) * 128], rhs=zb[:, c, :], start=(c == 0), stop=(c == NCH - 1)),
                                 r=[wg, (zb.name, c)], w=[pu])
                        sg_ = sg[c2 % 2]
                        y = yo[c2 % 2]
                        k.op("act", lambda e: e.activation(out=sg_[:], in_=pu[:, 0:L], func=AF.Sigmoid, bias=prm["bglu"][:, c2:c2 + 1], scale=1.0), r=[pu, prm["bglu"]], w=[sg_])
                        k.op("dve", lambda e: TT(e, y[:], ytot[:, c2, :], sg_[:], ALU.mult), r=[(ytot.name, c2), sg_], w=[y])
                        k.dma(self.Ys.ap()[c2 * 128:(c2 + 1) * 128, t0:t0 + L], y[:], w=[("Ysf", c2, t0)])
        k.pop()

    def hg_proj(self, i):
        k = self.k
        NT = 512
        k.push()
        w = k.sb("wqig", [128, NCH, 3 * D], BF16)
        wf = k.sb("wf", [128, NCH, 2 * D], BF16)
        stage = [k.sb("wstg%d" % b, [128, 8, 256], F32) for b in range(2)]
        self._stg = 0
        self.wload(w, self.hg_wqig.ap(), NCH, 3 * D, stage=stage)
        for dr in range(2):
            for c in range(0, D, 256):
                st = stage[self._stg % 2]
                self._stg += 1
                k.dma(st[:, :, :], self.hg_wf.ap()[dr][:, c:c + 256].rearrange("(k p) n -> p k n", p=128))
                k.op("pool", lambda e: e.tensor_copy(out=wf[:, :, dr * D + c:dr * D + c + 256], in_=st[:, :, :]), r=[st], w=[wf])
        bfm = k.sb("bfm", [128, 2, NCH], F32)
        k.dma(bfm[:], self.hg_bf.ap())
        lg = k.sb("lblg", [128, DEPTH, NCH], F32)
        k.dma(lg[:], self.hg_lb.ap())
        k.op("act", lambda e: e.activation(out=lg[:], in_=lg[:], func=AF.Exp), r=[lg], w=[lg])
        ssum = k.sb("lbsum", [128, NCH], F32)
        lb = k.sb("lb", [128, NCH], F32)
        oml = k.sb("oml", [128, NCH], F32)
        k.op("dve", lambda e: e.tensor_tensor(out=ssum[:], in0=lg[:, 0], in1=lg[:, 1], op=ALU.add), r=[lg], w=[ssum])
        for l in (2, 3):
            k.op("dve", lambda e: e.tensor_tensor(out=ssum[:], in0=ssum[:], in1=lg[:, l], op=ALU.add), r=[lg, ssum], w=[ssum])
        k.op("dve", lambda e: e.reciprocal(out=ssum[:], in_=ssum[:]), r=[ssum], w=[ssum])
        k.op("dve", lambda e: e.memset(lb[:], 0.0), w=[lb])
        for l in range(1, i + 1):
            k.op("dve", lambda e: e.tensor_tensor(out=lb[:], in0=lb[:], in1=lg[:, l], op=ALU.add), r=[lg, lb], w=[lb])
        k.op("dve", lambda e: e.tensor_tensor(out=lb[:], in0=lb[:], in1=ssum[:], op=ALU.mult), r=[lb, ssum], w=[lb])
        k.op("dve", lambda e: e.tensor_scalar(out=oml[:], in0=lb[:], scalar1=-1.0, scalar2=1.0, op0=ALU.mult, op1=ALU.add), r=[lb], w=[oml])
        self.hg_lbt = None
        hts = [k.sb("ht%d" % b, [128, NCH, NT], BF16) for b in range(2)]
        qo = [k.sb("qo%d" % b, [128, NT], BF16) for b in range(2)]
        sg = [k.sb("sg%d" % b, [128, NT], F32) for b in range(2)]
        fo = [k.sb("fo%d" % b, [128, NT], F32) for b in range(2)]
        vo = [k.sb("vo%d" % b, [128, 512], BF16) for b in range(2)]
        nv = 0
        for ti, (t0, N, v) in enumerate(self.token_tiles(True, NT)):
            ht = hts[ti % 2]
            k.dma(ht[:, :, 0:N], self.Hs.ap()[:, t0:t0 + N].rearrange("(c p) n -> p c n", p=128))
            for cc in range(16):
                b = cc % 2
                ps = self.psb[1 + b]
                col0 = cc * 128 if cc < 8 else 2 * D + (cc - 8) * 128
                for c in range(NCH):
                    k.op("pe", lambda e: e.matmul(out=ps[:, 0:N], lhsT=w[:, c, col0:col0 + 128], rhs=ht[:, c, 0:N],
                                                  start=(c == 0), stop=(c == NCH - 1)), r=[w, ht], w=[ps])
                fn = AF.Identity if cc < 8 else AF.Silu
                k.op("act", lambda e: e.activation(out=qo[b][:, 0:N], in_=ps[:, 0:N], func=fn), r=[ps], w=[qo[b]])
                k.dma(self.QKs.ap()[cc * 128:(cc + 1) * 128, t0:t0 + N], qo[b][:, 0:N], w=[("QKs", cc, t0)])
            for dr in range(2):
                for cc in range(8):
                    b = cc % 2
                    ps = self.psb[3 + b]
                    for c in range(NCH):
                        k.op("pe", lambda e: e.matmul(out=ps[:, 0:N], lhsT=wf[:, c, dr * D + cc * 128:dr * D + (cc + 1) * 128], rhs=ht[:, c, 0:N],
                                                      start=(c == 0), stop=(c == NCH - 1)), r=[wf, ht], w=[ps])
                    k.op("act", lambda e: e.activation(out=sg[b][:, 0:N], in_=ps[:, 0:N], func=AF.Sigmoid, bias=bfm[:, dr, cc:cc + 1], scale=1.0), r=[ps, bfm], w=[sg[b]])
                    k.op("dve", lambda e: e.tensor_scalar(out=fo[b][:, 0:N], in0=sg[b][:, 0:N], scalar1=oml[:, cc:cc + 1], scalar2=lb[:, cc:cc + 1],
                                                          op0=ALU.mult, op1=ALU.add), r=[sg[b], oml, lb], w=[fo[b]])
                    k.dma(self.Fs.ap()[dr, cc * 128:(cc + 1) * 128, t0:t0 + N], fo[b][:, 0:N], w=[("Fs", dr, cc, t0)])
            for sub in range(N // 128):
                for vb in range(2):
                    ps = self.psb[5 + vb]
                    for c in range(NCH):
                        k.op("pe", lambda e: e.matmul(out=ps[:, :], lhsT=ht[:, c, sub * 128:(sub + 1) * 128], rhs=w[:, c, D + vb * 512:D + (vb + 1) * 512],
                                                      start=(c == 0), stop=(c == NCH - 1)), r=[w, ht], w=[ps])
                    vv = vo[nv % 2]
                    nv += 1
                    k.op("act", lambda e: e.activation(out=vv[:], in_=ps[:, :], func=AF.Identity), r=[ps], w=[vv])
                    k.dma(self.Vs.ap()[t0 + sub * 128:t0 + (sub + 1) * 128, vb * 512:(vb + 1) * 512], vv[:], w=[("Vs", t0, sub, vb)])
        k.pop()

    def hg_core(self, i):
        k = self.k
        CH = 128
        NKT = T // CH
        k.push()
        idf = k.sb("idf", [128, 128], F32)
        ident = k.sb("ident", [128, 128], BF16)
        k.dma(idf[:], self.c_ident.ap())
        k.op("dve", lambda e: e.tensor_copy(out=ident[:], in_=idf[:]), r=[idf], w=[ident])
        masks = []
        for nm, src in (("triu", self.c_triu), ("tril", self.c_tril)):
            mk = k.sb(nm, [128, 128], F32)
            k.dma(mk[:], src.ap())
            masks.append(mk)
        gn = k.sb("gn", [128, 1], F32)
        k.dma(gn[:], self.hg_gn.ap())
        zeros = k.sb("zeros", [128, CH], F32)
        k.op("dve", lambda e: e.memset(zeros[:], 0.0), w=[zeros])
        qT = k.sb("hqT", [128, T], BF16)
        gT = k.sb("hgT", [128, T], BF16)
        vh = k.sb("hvh", [128, NKT, 128], BF16)
        f = k.sb("hf", [128, T], F32)
        P = k.sb("hP", [128, T], F32)
        kinv = k.sb("hkinv", [128, T], F32)
        qdec = k.sb("hqdec", [128, T], BF16)
        kinvb = k.sb("hkinvb", [128, T], BF16)
        Oacc = k.sb("hO", [128, S], F32)
        Sst = k.sb("hS", [128, 128], F32)
        Sbf = [k.sb("hSbf%d" % b, [128, 128], BF16) for b in range(2)]
        kdec = [k.sb("hkdec%d" % b, [128, CH], BF16) for b in range(2)]
        kdt = [k.sb("hkdt%d" % b, [128, CH], BF16) for b in range(2)]
        attm = [k.sb("hattm%d" % b, [128, CH], BF16) for b in range(2)]
        sq = k.sb("hsq", [128, 1, 512], BF16)
        rstd = k.sb("hrstd", [128, 512], F32)
        ons = [k.sb("hon%d" % b, [128, 512], BF16) for b in range(2)]
        ps_att, ps_o, ps_ds = self.psb[1], self.psb[2], self.psb[3]
        ps_tr = self.ps_bf[:, 0:128]
        nb = 0
        for hd in range(8):
            k.dma(qT[:], self.QKs.ap()[hd * 128:(hd + 1) * 128, :])
            k.dma(gT[:], self.QKs.ap()[D + hd * 128:D + (hd + 1) * 128, :])
            k.dma(vh[:], self.Vs.ap()[:, hd * 128:(hd + 1) * 128].rearrange("(kt p) d -> p kt d", p=128))
            for dr in range(2):
                k.dma(f[:], self.Fs.ap()[dr, hd * 128:(hd + 1) * 128, :])
                order = list(range(NKT)) if dr == 0 else [1, 0] + list(range(NKT - 1, 1, -1))
                for kt in range(NKT):
                    c0 = kt * CH
                    if dr == 0:
                        fa, pa = f[:, c0:c0 + CH], P[:, c0:c0 + CH]
                    else:
                        fa = bass.AP(f, c0 + CH - 1, [[T, 128], [-1, CH]])
                        pa = bass.AP(P, c0 + CH - 1, [[T, 128], [-1, CH]])
                    k.op("dve", lambda e: e.tensor_tensor_scan(out=pa, data0=fa, data1=zeros[:], initial=1.0, op0=ALU.mult, op1=ALU.add),
                         r=[f, zeros], w=[P])
                k.op("dve", lambda e: e.reciprocal(out=kinv[:], in_=P[:]), r=[P], w=[kinv])
                k.op("pool", lambda e: e.tensor_scalar(out=f[:], in0=f[:], scalar1=-1.0, scalar2=1.0, op0=ALU.mult, op1=ALU.add), r=[f], w=[f])
                k.op("dve", lambda e: e.tensor_tensor(out=kinv[:], in0=kinv[:], in1=f[:], op=ALU.mult), r=[kinv, f], w=[kinv])
                k.op("pool", lambda e: e.tensor_tensor(out=qdec[:], in0=qT[:], in1=P[:], op=ALU.mult), r=[qT, P], w=[qdec])
                k.op("act", lambda e: e.activation(out=kinvb[:], in_=kinv[:], func=AF.Identity), r=[kinv], w=[kinvb])
                k.op("dve", lambda e: e.memset(Sst[:], 0.0), w=[Sst])
                k.op("pool", lambda e: e.memset(Sbf[0][:], 0.0), w=[Sbf[0]])
                si = 0
                mask = masks[dr]
                for kt in order:
                    c0 = kt * CH
                    plast = P[:, c0 + CH - 1:c0 + CH] if dr == 0 else P[:, c0:c0 + 1]
                    Scur = Sbf[si % 2]
                    if kt >= 2:
                        l0 = c0 - LC
                        am = attm[nb % 2]
                        k.op("pe", lambda e: e.matmul(out=ps_att[:, 0:CH], lhsT=kinvb[:, c0:c0 + CH], rhs=qdec[:, c0:c0 + CH], start=True, stop=True),
                             r=[kinvb, qdec], w=[ps_att])
                        k.op("dve", lambda e: e.tensor_tensor(out=am[:], in0=ps_att[:, 0:CH], in1=mask[:], op=ALU.mult), r=[ps_att, mask], w=[am])
                        k.op("pe", lambda e: e.matmul(out=ps_o[:, 0:CH], lhsT=vh[:, kt, :], rhs=am[:], start=True, stop=False), r=[vh, am], w=[ps_o])
                        k.op("pe", lambda e: e.matmul(out=ps_o[:, 0:CH], lhsT=Scur[:], rhs=qdec[:, c0:c0 + CH], start=False, stop=True), r=[Scur, qdec], w=[ps_o])
                        if dr == 0:
                            k.op("dve", lambda e: e.tensor_copy(out=Oacc[:, l0:l0 + CH], in_=ps_o[:, 0:CH]), r=[ps_o], w=[Oacc])
                        else:
                            k.op("dve", lambda e: e.tensor_tensor(out=Oacc[:, l0:l0 + CH], in0=ps_o[:, 0:CH], in1=Oacc[:, l0:l0 + CH], op=ALU.add), r=[ps_o, Oacc], w=[Oacc])
                    kd, kt_ = kdec[nb % 2], kdt[nb % 2]
                    nb += 1
                    k.op("dve", lambda e: e.tensor_scalar(out=kd[:], in0=kinv[:, c0:c0 + CH], scalar1=plast, scalar2=None, op0=ALU.mult), r=[kinv, P], w=[kd])
                    k.op("pe", lambda e: e.transpose(out=ps_tr, in_=kd[:], identity=ident[:]), r=[kd, ident], w=["pstr"])
                    k.op("act", lambda e: e.activation(out=kt_[:], in_=ps_tr, func=AF.Identity), r=["pstr"], w=[kt_])
                    k.op("pe", lambda e: e.matmul(out=ps_ds[:, 0:128], lhsT=kt_[:], rhs=vh[:, kt, :], start=True, stop=True), r=[kt_, vh], w=[ps_ds])
                    k.op("dve", lambda e: e.tensor_scalar(out=Sst[:], in0=Sst[:], scalar1=plast, scalar2=None, op0=ALU.mult), r=[Sst, P], w=[Sst])
                    k.op("dve", lambda e: e.tensor_tensor(out=Sst[:], in0=ps_ds[:, 0:128], in1=Sst[:], op=ALU.add), r=[ps_ds, Sst], w=[Sst])
                    si += 1
                    k.op("act", lambda e: e.activation(out=Sbf[si % 2][:], in_=Sst[:], func=AF.Identity), r=[Sst], w=[Sbf[si % 2]])
            for qb in range(8):
                l0 = qb * 512
                k.op("act", lambda e: e.activation(out=sq[:, 0, :], in_=Oacc[:, l0:l0 + 512], func=AF.Square), r=[Oacc], w=[sq])
                self.rstd_from(sq, 1, 512, rstd, self.psb[6], 128)
                on = ons[qb % 2]
                k.op("dve", lambda e: e.scalar_tensor_tensor(out=rstd[:], in0=Oacc[:, l0:l0 + 512], scalar=gn[:, 0:1], in1=rstd[:], op0=ALU.mult, op1=ALU.mult), r=[Oacc, gn, rstd], w=[rstd])
                k.op("dve", lambda e: e.tensor_tensor(out=on[:], in0=rstd[:], in1=gT[:, LC + l0:LC + l0 + 512], op=ALU.mult), r=[rstd, gT], w=[on])
                k.dma(self.AO.ap()[hd * 128:(hd + 1) * 128, LC + l0:LC + l0 + 512], on[:], w=[("AO", hd, l0)])
        k.pop()

    def ffn(self, i, which, j, src, dst, with_ctx, final=False):
        k = self.k
        NT = 256
        k.push()
        w1 = k.sb("w1", [128, NCH, DFF], BF16)
        w3 = k.sb("w3", [128, NCH, DFF], BF16)
        w2 = k.sb("w2", [128, NFF, D], BF16)
        stage = [k.sb("wstg%d" % b, [128, 8, 256], F32) for b in range(2)]
        self._stg = 0

        def piece(dst, src2d, k0, kn, c, w):
            st = stage[self._stg % 2]
            self._stg += 1
            k.dma(st[:, 0:kn, 0:w], src2d[k0 * 128:(k0 + kn) * 128, c:c + w].rearrange("(k p) n -> p k n", p=128))
            k.op("pool", lambda e: e.tensor_copy(out=dst[:, k0:k0 + kn, c:c + w], in_=st[:, 0:kn, 0:w]), r=[st], w=[(dst.name, k0, c)])
        for c in range(0, DFF, 256):
            piece(w1, self.w_ff1.ap()[i, which], 0, NCH, c, 256)
            piece(w3, self.w_ff3.ap()[i, which], 0, NCH, c, 256)
        for k0 in range(0, NFF, 8):
            for c in range(0, D, 256):
                piece(w2, self.w_ff2.ap()[i, which], k0, min(8, NFF - k0), c, 256)
        xts = [k.sb("xt%d" % b, [128, NCH, NT], F32) for b in range(2)]
        sq = k.sb("sq", [128, NCH, NT], BF16)
        h = k.sb("h", [128, NCH, NT], BF16)
        g = k.sb("g", [128, NFF, NT], BF16)
        y = k.sb("y", [128, NCH, NT], F32)
        sl = [k.sb("sl%d" % b, [128, NT], BF16) for b in range(2)]
        rstd = k.sb("rstd", [128, NT], F32)
        psn = self.psb[0]
        tiles = self.token_tiles(with_ctx, NT)
        import os
        dbg = int(os.environ.get("FF_DBG", "0"))
        if dbg:
            tiles = tiles[:dbg]
        for ti, (t0, N, v) in enumerate(tiles):
            xt = xts[ti % 2]
            k.dma(xt[:, :, 0:N], src.ap()[:, t0:t0 + N].rearrange("(c p) n -> p c n", p=128), q="sp")
            step = int(os.environ.get("FF_STEP", "9"))
            if step >= 1:
                self.prenorm(xt, N, i, j, v, sq, y, h, rstd, psn)
            for f in range(NFF if step >= 2 else 0):
                pa = self.psb[1 + (f % 2)]
                pb = self.psb[3 + (f % 2)]
                for c in range(NCH):
                    k.op("pe", lambda e: e.matmul(out=pa[:, 0:N], lhsT=w1[:, c, f * 128:(f + 1) * 128], rhs=h[:, c, 0:N],
                                                  start=(c == 0), stop=(c == NCH - 1)), r=[(w1.name, 0, (f // 2) * 256), h], w=[pa])
                for c in range(NCH):
                    k.op("pe", lambda e: e.matmul(out=pb[:, 0:N], lhsT=w3[:, c, f * 128:(f + 1) * 128], rhs=h[:, c, 0:N],
                                                  start=(c == 0), stop=(c == NCH - 1)), r=[(w3.name, 0, (f // 2) * 256), h], w=[pb])
                s = sl[f % 2]
                sub = int(os.environ.get("FF_SUB", "9"))
                if sub >= 2:
                    k.op("act", lambda e: e.activation(out=s[:, 0:N], in_=pa[:, 0:N], func=AF.Silu), r=[pa], w=[s])
                if sub >= 3:
                    k.op("dve", lambda e: e.tensor_tensor(out=g[:, f, 0:N], in0=pb[:, 0:N], in1=s[:, 0:N], op=ALU.mult), r=[s, pb], w=[(g.name, f)])
            for c in range(NCH if step >= 3 else 0):
                py = self.psb[5 + (c % 2)]
                for f in range(NFF):
                    k.op("pe", lambda e: e.matmul(out=py[:, 0:N], lhsT=w2[:, f, c * 128:(c + 1) * 128], rhs=g[:, f, 0:N],
                                                  start=(f == 0), stop=(f == NFF - 1)), r=[(w2.name, (f // 8) * 8, (c // 2) * 256), (g.name, f)], w=[py])
                k.op("dve", lambda e: e.tensor_copy(out=y[:, c, 0:N], in_=py[:, 0:N]), r=[py], w=[y])
                k.op("act", lambda e: e.activation(out=sq[:, c, 0:N], in_=y[:, c, 0:N], func=AF.Square), r=[y], w=[sq])
            if step >= 4:
                self.postres(y, xt, N, i, j, v, sq, rstd, psn)
            if final:
                k.dma(self.out.ap()[:, t0 - LC:t0 - LC + N].rearrange("(c p) n -> p c n", p=128), xt[:, :, 0:N], q="sp")
            else:
                k.dma(dst.ap()[:, t0:t0 + N].rearrange("(c p) n -> p c n", p=128), xt[:, :, 0:N], q="sp",
                      w=[(dst.name, t0)])
        k.pop()


def fm(v):
    v = np.asarray(v, np.float32)
    lead = v.shape[:-1]
    a = v.reshape(lead + (NCH, 128))
    a = np.moveaxis(a, -1, 0)
    return np.ascontiguousarray(a)


_CONST = {}


def consts():
    if _CONST:
        return _CONST
    nfreq = 16
    inv = 10000.0 ** (-np.arange(nfreq, dtype=np.float32) / nfreq)
    t = np.arange(S)
    row = (t // 64).astype(np.float32)
    col = (t % 64).astype(np.float32)
    ang = np.concatenate([row[:, None] * inv, col[:, None] * inv], axis=-1).astype(np.float32)
    p = np.arange(128)
    pair = (p % 64) // 2
    C = np.cos(ang)[:, pair].T
    Sn = np.sin(ang)[:, pair].T
    sign = np.where(p % 2 == 0, -1.0, 1.0)[:, None]
    pm = np.zeros((128, 128), np.float32)
    pm[p ^ 1, p] = 1.0
    a = (p // 64)[:, None, None]
    kc = (p % 64)[:, None, None]
    jj = np.arange(22)[None, :, None]
    qc = np.arange(64)[None, None, :]
    dr = (17 - jj) + a
    c0 = np.clip(qc - 8, 0, 48)
    colok = (kc >= c0) & (kc < c0 + 16)
    NEG = -30000.0
    mI = np.where(colok & (dr >= 3) & (dr <= 10), 0.0, NEG).astype(np.float32).reshape(128, 1408)
    mF = np.where(colok & (dr >= 0) & (dr <= 14), 0.0, NEG).astype(np.float32).reshape(128, 1408)
    dri = np.broadcast_to(np.clip(dr, 0, 14), (128, 22, 64))
    dci = np.broadcast_to(np.clip(kc - qc + 15, 0, 30), (128, 22, 64))
    drv = np.broadcast_to((dr >= 0) & (dr <= 14), (128, 22, 64))
    _CONST.update(triu=np.triu(np.ones((128, 128), np.float32)), tril=np.tril(np.ones((128, 128), np.float32)))
    _CONST.update(C=np.ascontiguousarray(C, np.float32), S=np.ascontiguousarray(Sn * sign, np.float32), pm=pm,
                  ident=np.eye(128, dtype=np.float32), mI=mI, mF=mF, dri=dri, dci=dci, drv=drv)
    return _CONST


def to_sm(a):
    a = np.asarray(a, np.float32)
    rest = a.shape[3:]
    a = a.reshape((2, 32, 2, 64) + rest)
    a = np.moveaxis(a, (2, 3), (0, 1))
    return np.ascontiguousarray(a.reshape((128, 2, 32) + rest))


def make_inputs(inp, b, xin=None):
    cs = consts()
    if xin is None:
        xin = np.ascontiguousarray(np.concatenate([inp["ctx"][b], inp["x"][b]], axis=0).T)
    cT = np.stack([fm(inp["c"][b]), fm(inp["c_ctx"])], axis=-1)
    b_ada = fm(inp["b_ada"].reshape(DEPTH, 9, D)).reshape(128, DEPTH, 72)
    rpb = inp["na_rpb"][0]
    rpbg = rpb[:, cs["dri"], cs["dci"]]
    rpbg = np.where(cs["drv"][None], rpbg, np.float32(0.0)).reshape(16, 128, 1408)
    lam = np.stack([inp["da_lam_q1"][0], inp["da_lam_k1"][0], inp["da_lam_q2"][0], inp["da_lam_k2"][0]], axis=-1)
    return {
        "xin": xin, "cT": np.ascontiguousarray(cT),
        "w_ada": inp["w_ada"], "b_ada": np.ascontiguousarray(b_ada),
        "g_pre": fm(inp["g_pre"]), "g_post": fm(inp["g_post"]),
        "w_ff1": inp["w_ff1"], "w_ff3": inp["w_ff3"], "w_ff2": inp["w_ff2"],
        "da_w_qkv": inp["da_w_qkv"][0], "da_w_o": inp["da_w_o"][0],
        "da_lam": np.ascontiguousarray(lam, np.float32), "da_subln": np.ascontiguousarray(inp["da_subln"][0].reshape(128, 1)),
        "na_w_qkv": inp["na_w_qkv"][0], "na_w_o": inp["na_w_o"][0],
        "na_rpbg": np.ascontiguousarray(rpbg, np.float32),
        "s5_are": to_sm(inp["s5_a_re"][0]), "s5_aim": to_sm(inp["s5_a_im"][0]),
        "s5_ldt": to_sm(np.broadcast_to(inp["s5_log_dt"][0][:, :, None], (2, 64, 64))),
        "s5_bre": to_sm(inp["s5_b_re"][0]), "s5_bim": to_sm(inp["s5_b_im"][0]),
        "s5_cre": to_sm(np.swapaxes(inp["s5_c_re"][0], 2, 3)), "s5_cim": to_sm(np.swapaxes(inp["s5_c_im"][0], 2, 3)),
        "s5_d": fm(inp["s5_d"][0]), "s5_bglu": fm(inp["s5_b_glu"][0]), "s5_w_glu": inp["s5_w_glu"][0],
        "hg_w_qig": inp["hg_w_qig"][0], "hg_w_f": inp["hg_w_f"][0], "hg_b_f": fm(inp["hg_b_f"][0]),
        "hg_lb": fm(inp["hg_lb_logits"]), "hg_gn": np.ascontiguousarray(inp["hg_gnorm"][0].reshape(128, 1)),
        "hg_w_o": inp["hg_w_o"][0], "c_triu": cs["triu"], "c_tril": cs["tril"],
        "c_ropeC": cs["C"], "c_ropeS": cs["S"], "c_pm": cs["pm"], "c_ident": cs["ident"],
        "c_maskI": cs["mI"], "c_maskF": cs["mF"],
    }


def kernel(**inputs):
    inp = {k_: np.asarray(v) for k_, v in inputs.items()}
    prog = Prog()
    in_maps = [make_inputs(inp, b) for b in range(8)]
    res = run_bass_kernel_spmd(prog.nc, in_maps, core_ids=list(range(8)))
    out = np.stack([np.ascontiguousarray(res.results[b]["out"].T) for b in range(8)], axis=0)
    return out.astype(np.float32)
```
